# Optimizing a Trainium2 kernel written in Bass

```python
import math
import jax, jax.numpy as jnp
from jax import lax
import numpy as np

D_MODEL = 1024
BATCH = 8
SEQ = 2048
DEPTH = 1

MEM_LEN = 256
RET_HEADS = 4
RET_DK = 256
RET_DV = 512
RET_CHUNK = 128
ROPE_BASE = 10000.0
DIFF_HEADS = 8
DIFF_D = 64
Q_BLOCK = 128
MEM_HEADS = 4
MEM_D = 256
D_FF = 2816
FFN_RES = 0.5
EPS = 1e-6
NEG = -1e30
N_BRANCH = 3

RET_QK_W = RET_HEADS * RET_DK
RET_V_W = RET_HEADS * RET_DV
DIFF_QK_W = DIFF_HEADS * 2 * DIFF_D
DIFF_V_W = DIFF_HEADS * 2 * DIFF_D
MEM_Q_W = MEM_HEADS * MEM_D
IN_SPLITS = (RET_QK_W, RET_QK_W, RET_V_W, RET_V_W, DIFF_QK_W, DIFF_QK_W, DIFF_V_W, MEM_Q_W, N_BRANCH * D_MODEL)
IN_WIDTH = sum(IN_SPLITS)

kernel_name = "hybrid_retention_diffattn_memory_macaron"


def rms_norm(x, g=None):
    xf = x.astype(jnp.float32)
    y = xf * lax.rsqrt(jnp.mean(xf * xf, axis=-1, keepdims=True) + EPS)
    if g is not None:
        y = y * g.astype(jnp.float32)
    return y.astype(x.dtype)


def swiglu_ffn(h, w_in, w_out):
    gate, up = jnp.split(h @ w_in, 2, axis=-1)
    return (jax.nn.silu(gate) * up) @ w_out


def rotary(x, positions):
    half = x.shape[-1] // 2
    inv = ROPE_BASE ** (-jnp.arange(half, dtype=jnp.float32) / half)
    ang = positions.astype(jnp.float32)[..., None] * inv
    cos = jnp.cos(ang)[:, :, None, :]
    sin = jnp.sin(ang)[:, :, None, :]
    x1, x2 = x[..., :half], x[..., half:]
    return jnp.concatenate([x1 * cos - x2 * sin, x1 * sin + x2 * cos], axis=-1)


def retention(q, k, v, positions):
    B, S, H, dk = q.shape
    dv = v.shape[-1]
    C = RET_CHUNK
    n = S // C
    q = rotary(q.astype(jnp.float32), positions)
    k = rotary(k.astype(jnp.float32), positions) * (dk ** -0.5)
    v = v.astype(jnp.float32)

    def chunks(t):
        return t.reshape(B, n, C, H, t.shape[-1]).transpose(1, 0, 3, 2, 4)

    log_g = jnp.log1p(-(2.0 ** (-5.0 - jnp.arange(H, dtype=jnp.float32))))
    idx = jnp.arange(C, dtype=jnp.float32)
    dist = idx[:, None] - idx[None, :]
    intra_decay = jnp.where(dist >= 0, jnp.exp(log_g[:, None, None] * jnp.maximum(dist, 0.0)), 0.0)
    q_decay = jnp.exp(log_g[:, None] * (idx + 1.0))[:, :, None]
    k_decay = jnp.exp(log_g[:, None] * (C - 1.0 - idx))[:, :, None]
    chunk_decay = jnp.exp(log_g * C)[:, None, None]

    def step(state, qkv):
        qc, kc, vc = qkv
        scores = jnp.einsum('bhid,bhjd->bhij', qc, kc) * intra_decay
        out = (jnp.einsum('bhij,bhje->bhie', scores, vc)
               + jnp.einsum('bhid,bhde->bhie', qc * q_decay, state))
        state = chunk_decay * state + jnp.einsum('bhjd,bhje->bhde', kc * k_decay, vc)
        return state, out

    state0 = jnp.zeros((B, H, dk, dv), jnp.float32)
    _, out = lax.scan(step, state0, (chunks(q), chunks(k), chunks(v)))
    return out.transpose(1, 0, 3, 2, 4).reshape(B, S, H, dv)


def diff_attention(q, k, v, lam):
    B, S, H, _, d = q.shape
    nb = S // Q_BLOCK
    kh = k.transpose(0, 2, 3, 1, 4)
    vh = v.transpose(0, 2, 1, 3)
    qb = q.transpose(0, 2, 3, 1, 4).reshape(B, H, 2, nb, Q_BLOCK, d).transpose(3, 0, 1, 2, 4, 5)
    starts = jnp.arange(nb) * Q_BLOCK
    kpos = jnp.arange(S)
    scale = d ** -0.5

    def block(args):
        qblk, start = args
        s = jnp.einsum('bhcqd,bhckd->bhcqk', qblk, kh).astype(jnp.float32) * scale
        qpos = start + jnp.arange(Q_BLOCK)
        s = jnp.where(kpos[None, :] <= qpos[:, None], s, NEG)
        p = jax.nn.softmax(s, axis=-1)
        a = p[:, :, 0] - lam * p[:, :, 1]
        return jnp.einsum('bhqk,bhke->bhqe', a.astype(vh.dtype), vh)

    out = lax.map(block, (qb, starts))
    return out.transpose(1, 0, 3, 2, 4).reshape(B, S, H, 2 * d)


def memory_attention(q, k, v):
    s = jnp.einsum('bshd,bmhd->bhsm', q, k).astype(jnp.float32) * (q.shape[-1] ** -0.5)
    p = jax.nn.softmax(s, axis=-1)
    return jnp.einsum('bhsm,bmhd->bshd', p.astype(v.dtype), v)


def setup_inputs(seed: int = 0) -> dict:
    key = jax.random.key(seed)
    ks = jax.random.split(key, 32)
    f32 = jnp.float32

    def normal(k, shape, scale):
        return jax.random.normal(k, shape, f32) * scale

    def gain(k, shape):
        return 1.0 + 0.02 * jax.random.normal(k, shape, f32)

    L = DEPTH
    start = jax.random.randint(ks[2], (BATCH, 1), 0, 1024)
    positions = (start + jnp.arange(SEQ)[None, :]).astype(jnp.int32)
    return {
        "x": normal(ks[0], (BATCH, SEQ, D_MODEL), 1.0),
        "mem": normal(ks[1], (BATCH, MEM_LEN, D_MODEL), 1.0),
        "positions": positions,
        "g_ffn1": gain(ks[3], (L, D_MODEL)),
        "w_ffn1_in": normal(ks[4], (L, D_MODEL, 2 * D_FF), D_MODEL ** -0.5),
        "w_ffn1_out": normal(ks[5], (L, D_FF, D_MODEL), D_FF ** -0.5),
        "g_mix": gain(ks[6], (L, D_MODEL)),
        "w_in": normal(ks[7], (L, D_MODEL, IN_WIDTH), D_MODEL ** -0.5),
        "g_diff_q": gain(ks[8], (L, DIFF_D)),
        "g_diff_k": gain(ks[9], (L, DIFF_D)),
        "lam_q1": normal(ks[10], (L, DIFF_D), 0.1),
        "lam_k1": normal(ks[11], (L, DIFF_D), 0.1),
        "lam_q2": normal(ks[12], (L, DIFF_D), 0.1),
        "lam_k2": normal(ks[13], (L, DIFF_D), 0.1),
        "g_diff_out": gain(ks[14], (L, 2 * DIFF_D)),
        "g_mem_q": gain(ks[15], (L, MEM_D)),
        "g_mem_k": gain(ks[16], (L, MEM_D)),
        "g_mem": gain(ks[17], (L, D_MODEL)),
        "w_mem_kv": normal(ks[18], (L, D_MODEL, 2 * MEM_Q_W), D_MODEL ** -0.5),
        "w_br_ret": normal(ks[19], (L, RET_V_W, D_MODEL), RET_V_W ** -0.5),
        "w_br_diff": normal(ks[20], (L, DIFF_V_W, D_MODEL), DIFF_V_W ** -0.5),
        "w_br_mem": normal(ks[21], (L, MEM_Q_W, D_MODEL), MEM_Q_W ** -0.5),
        "w_o": normal(ks[22], (L, D_MODEL, D_MODEL), D_MODEL ** -0.5),
        "g_ffn2": gain(ks[23], (L, D_MODEL)),
        "w_ffn2_in": normal(ks[24], (L, D_MODEL, 2 * D_FF), D_MODEL ** -0.5),
        "w_ffn2_out": normal(ks[25], (L, D_FF, D_MODEL), D_FF ** -0.5),
    }


def reference(x, mem, positions, g_ffn1, w_ffn1_in, w_ffn1_out, g_mix, w_in,
              g_diff_q, g_diff_k, lam_q1, lam_k1, lam_q2, lam_k2, g_diff_out,
              g_mem_q, g_mem_k, g_mem, w_mem_kv, w_br_ret, w_br_diff, w_br_mem,
              w_o, g_ffn2, w_ffn2_in, w_ffn2_out):
    B, S, _ = x.shape
    M = mem.shape[1]
    offsets = np.cumsum(IN_SPLITS)[:-1].tolist()
    for l in range(DEPTH):
        lam_init = 0.8 - 0.6 * math.exp(-0.3 * l)

        x = x + FFN_RES * swiglu_ffn(rms_norm(x, g_ffn1[l]), w_ffn1_in[l], w_ffn1_out[l])

        h = rms_norm(x, g_mix[l])
        rq, rk, rv, rg, dq, dk, dv, mq, gates = jnp.split(h @ w_in[l], offsets, axis=-1)

        ret = retention(rq.reshape(B, S, RET_HEADS, RET_DK), rk.reshape(B, S, RET_HEADS, RET_DK),
                        rv.reshape(B, S, RET_HEADS, RET_DV), positions)
        ret = rms_norm(ret) * jax.nn.silu(rg.reshape(B, S, RET_HEADS, RET_DV).astype(jnp.float32))
        ret = ret.reshape(B, S, RET_V_W).astype(x.dtype)

        lam = (jnp.exp(jnp.sum(lam_q1[l].astype(jnp.float32) * lam_k1[l].astype(jnp.float32)))
               - jnp.exp(jnp.sum(lam_q2[l].astype(jnp.float32) * lam_k2[l].astype(jnp.float32)))
               + lam_init)
        dqn = rms_norm(dq.reshape(B, S, DIFF_HEADS, 2, DIFF_D), g_diff_q[l])
        dkn = rms_norm(dk.reshape(B, S, DIFF_HEADS, 2, DIFF_D), g_diff_k[l])
        dif = diff_attention(dqn, dkn, dv.reshape(B, S, DIFF_HEADS, 2 * DIFF_D), lam)
        dif = (rms_norm(dif, g_diff_out[l]) * (1.0 - lam_init)).reshape(B, S, DIFF_V_W)

        mk, mv = jnp.split(rms_norm(mem, g_mem[l]) @ w_mem_kv[l], 2, axis=-1)
        mo = memory_attention(rms_norm(mq.reshape(B, S, MEM_HEADS, MEM_D), g_mem_q[l]),
                              rms_norm(mk.reshape(B, M, MEM_HEADS, MEM_D), g_mem_k[l]),
                              mv.reshape(B, M, MEM_HEADS, MEM_D)).reshape(B, S, MEM_Q_W)

        gt = jax.nn.sigmoid(gates.reshape(B, S, N_BRANCH, D_MODEL))
        y = (gt[:, :, 0] * (ret @ w_br_ret[l])
             + gt[:, :, 1] * (dif @ w_br_diff[l])
             + gt[:, :, 2] * (mo @ w_br_mem[l]))
        x = x + y @ w_o[l]

        x = x + FFN_RES * swiglu_ffn(rms_norm(x, g_ffn2[l]), w_ffn2_in[l], w_ffn2_out[l])
    return x
```

```python
import math
from contextlib import ExitStack

import numpy as np
import ml_dtypes

import concourse.bass as bass
import concourse.mybir as mybir
from concourse.bass_utils import run_bass_kernel_spmd

F32 = mybir.dt.float32
BF16 = mybir.dt.bfloat16
I32 = mybir.dt.int32
AF = mybir.ActivationFunctionType
ALU = mybir.AluOpType

S = 2048
D = 1024
KC = D // 128
DFF = 2816
NFF = DFF // 128
TT = 512
NT = S // TT
EPS = 1e-6
NCORES = 8
STAGE = 99


class Buf:
    __slots__ = ("name", "last_w", "readers")

    def __init__(self, name):
        self.name = name
        self.last_w = None
        self.readers = []


ENG_NAMES = ("pe", "act", "dve", "pool", "sp")
NDMA_SEM = 20


class Prog:
    def __init__(self, nc, es, same_engine_sync=True):
        self.nc = nc
        self.same_engine_sync = same_engine_sync
        self.fuse_waits = True
        self.engs = {"pe": nc.tensor, "act": nc.scalar, "dve": nc.vector,
                     "pool": nc.gpsimd, "sp": nc.sync}
        self.eid = {n: i for i, n in enumerate(ENG_NAMES)}
        self.sems = []
        for n in ENG_NAMES:
            self.sems.append(es.enter_context(nc.semaphore("s_" + n)))
        self.dma_sem_ids = {}
        for q in ("sp", "pool"):
            ids = []
            for i in range(NDMA_SEM):
                ids.append(len(self.sems))
                self.sems.append(es.enter_context(nc.semaphore(f"d_{q}{i}")))
            self.dma_sem_ids[q] = ids
        self.nclk = len(self.sems)
        self.count = [0] * self.nclk
        self.clk = {n: [0] * self.nclk for n in ENG_NAMES}
        self.snap = {}
        self.dma_rr = {"sp": 0, "pool": 0}
        self.n_wait = 0
        self.n_inst = 0

    def _need(self, ename, tok):
        sid, val = tok
        c = self.clk[ename]
        if c[sid] >= val:
            return False
        if sid == self.eid.get(ename, -1):
            if ename == "pe" or not self.same_engine_sync:
                return False
        assert val <= self.count[sid], f"wait for unsignalled token {tok} (count {self.count[sid]})"
        sn = self.snap.get(tok)
        if sn is not None:
            for i in range(self.nclk):
                if sn[i] > c[i]:
                    c[i] = sn[i]
        if c[sid] < val:
            c[sid] = val
        return True

    def _wait(self, ename, tok):
        if self._need(ename, tok):
            self.engs[ename].wait_ge(self.sems[tok[0]], tok[1])
            self.n_wait += 1

    def _deps(self, reads, writes):
        deps = []
        for b in reads:
            if b.last_w is not None:
                deps.append(b.last_w)
        for b in writes:
            if b.last_w is not None:
                deps.append(b.last_w)
            deps.extend(b.readers)
        return deps

    def _record(self, tok, reads, writes):
        for b in reads:
            b.readers.append(tok)
        for b in writes:
            b.last_w = tok
            b.readers = []

    def op(self, ename, fn, reads=(), writes=(), signal=True, fuse=True):
        deps = self._deps(reads, writes)
        deps.sort(key=lambda t: -t[1])
        need = [tok for tok in deps if self._need(ename, tok)]
        fused = None
        if need and fuse and self.fuse_waits and ename in ("act", "dve"):
            fused = need.pop()
        for tok in need:
            self.engs[ename].wait_ge(self.sems[tok[0]], tok[1])
            self.n_wait += 1
        ins = fn(self.engs[ename])
        if fused is not None:
            ins._wait_ge(self.sems[fused[0]], fused[1])
        sid = self.eid[ename]
        self.n_inst += 1
        if signal:
            ins.then_inc(self.sems[sid], 1)
            self.count[sid] += 1
            tok = (sid, self.count[sid])
            self.snap[tok] = list(self.clk[ename])
        else:
            tok = (sid, self.count[sid] + 1)
        self._record(tok, reads, writes)
        return tok

    def dma(self, q, out, in_, reads=(), writes=()):
        for tok in self._deps(reads, writes):
            self._wait(q, tok)
        ids = self.dma_sem_ids[q]
        sid = ids[self.dma_rr[q] % NDMA_SEM]
        self.dma_rr[q] += 1
        if self.count[sid] > 0:
            self._wait(q, (sid, self.count[sid]))
        self.engs[q].dma_start(out=out, in_=in_).then_inc(self.sems[sid], 16)
        self.n_inst += 1
        self.count[sid] += 16
        tok = (sid, self.count[sid])
        self.snap[tok] = list(self.clk[q])
        self._record(tok, reads, writes)
        return tok

    def barrier(self):
        for ename in ENG_NAMES:
            for sid in range(self.nclk):
                if self.count[sid] > 0:
                    own = sid == self.eid[ename]
                    if own and ename in ("pe", "sp"):
                        continue
                    c = self.clk[ename]
                    if c[sid] < self.count[sid]:
                        self.engs[ename].wait_ge(self.sems[sid], self.count[sid])
                        c[sid] = self.count[sid]
                        self.n_wait += 1

    def finish(self, toks):
        for tok in toks:
            self._wait("sp", tok)


class WStream:
    HOLD = 2

    def __init__(self, P, bufs, bufobjs):
        self.P = P
        self.bufs = bufs
        self.bobj = bufobjs
        self.plan = []
        self.issued = 0
        self.taken = 0

    def add(self, tag, parts):
        self.plan.append((tag, parts))

    def _issue(self, i):
        tag, parts = self.plan[i]
        k = i % len(self.bufs)
        for dst_fn, src in parts:
            self.P.dma("pool", dst_fn(self.bufs[k]), src, writes=[self.bobj[k]])

    def prefetch(self):
        ahead = len(self.bufs) - self.HOLD
        while self.issued < min(len(self.plan), self.taken + 1 + ahead):
            self._issue(self.issued)
            self.issued += 1

    def take(self, tag):
        i = self.taken
        assert self.plan[i][0] == tag, (self.plan[i][0], tag)
        ahead = len(self.bufs) - self.HOLD
        while self.issued < min(len(self.plan), i + 1 + ahead):
            self._issue(self.issued)
            self.issued += 1
        self.taken += 1
        k = i % len(self.bufs)
        return self.bufs[k], self.bobj[k]


def v3(t, a, b):
    return t[:, 0:a * b].rearrange("p (a b) -> p a b", a=a)


INW = 13312
OFF_RQ, OFF_RK, OFF_RV, OFF_RG = 0, 1024, 2048, 4096
OFF_DQ, OFF_DK, OFF_DV, OFF_MQ, OFF_GT = 6144, 7168, 8192, 9216, 10240
LAM_INIT = 0.8 - 0.6 * math.exp(0.0)
GAMMA = [1.0 - 2.0 ** (-5.0 - h) for h in range(4)]
TWO_PI = 2.0 * math.pi
CW1 = 6.28125
CW2 = TWO_PI - CW1
PI_SAFE = 3.1415925

PV_GFFN1, PV_GMIX, PV_GFFN2, PV_GMEM = 0, 8, 16, 24
PV_GDQ, PV_GDK, PV_GDO, PV_GMQ, PV_GMK, PV_INV, PV_KDEC = 32, 33, 34, 35, 37, 39, 40
CF_ID, CF_N = 0, 128
CR_DEC, CR_QDEC, CR_N = 0, 512, 1024
CB_ID, CB_O1024, CB_BD64, CB_O128, CB_O256, CB_ONE, CB_TRI, CB_N = 0, 128, 256, 384, 512, 640, 768, 896


def build_program(stage=STAGE, same_engine_sync=True, debug=False):
    nc = bass.Bass("TRN2", target_bir_lowering=False)
    es = ExitStack()
    with es:
        def din(name, shape, dt=F32):
            return nc.dram_tensor(name, list(shape), dt, kind="ExternalInput").ap()

        def dscratch(name, shape, dt):
            kind = "ExternalOutput" if debug else "Internal"
            return nc.dram_tensor(name, list(shape), dt, kind=kind).ap()

        x_d = din("x", [S, D])
        mem_d = din("mem", [256, D])
        pos_d = din("positions", [1, S], I32)
        w1a_d = din("w_ffn1_in", [D, 2 * DFF])
        w1b_d = din("w_ffn1_out", [DFF, D])
        w2a_d = din("w_ffn2_in", [D, 2 * DFF])
        w2b_d = din("w_ffn2_out", [DFF, D])
        win_d = din("w_in", [D, INW])
        wkv_d = din("w_mem_kv", [D, 2048])
        wbr_d = din("w_br_ret", [2048, D])
        wbd_d = din("w_br_diff", [D, D])
        wbm_d = din("w_br_mem", [D, D])
        wo_d = din("w_o", [D, D])
        pvec_d = din("pvec", [128, 64])
        lamv_d = din("lamv", [1, 256])
        cf32_d = din("cf32", [128, CF_N])
        cret_d = din("cret", [128, CR_N])
        cbf_d = din("cbf", [128, CB_N], BF16)
        out_d = nc.dram_tensor("out", [S, D], F32, kind="ExternalOutput").ap()
        yp_d = dscratch("yp", [3, KC, 128, S], BF16)
        ypdb = [[Buf(f"ypd{b}_{t}") for t in range(4)] for b in range(3)]

        win3 = win_d.rearrange("(k p) n -> p k n", p=128)

        P = Prog(nc, es, same_engine_sync=same_engine_sync)

        def sb(name, shape, dt, st=es):
            return st.enter_context(nc.sbuf_tensor(name, list(shape), dt))

        xT = sb("xT", [128, KC * S], F32)
        hT = sb("hT", [128, KC * S], BF16)
        pvec = sb("pvec_sb", [128, 64], F32)
        cf32 = sb("cf32_sb", [128, CF_N], F32)
        cbf = sb("cbf_sb", [128, CB_N], BF16)
        epsc = sb("epsc", [128, 1], F32)
        onec = sb("onec", [128, 1], F32)
        NWB = 4
        wbufs = [sb(f"wbuf{i}", [128, 4096], BF16) for i in range(NWB)]
        wbobj = [Buf(f"wbuf{i}") for i in range(NWB)]
        W = WStream(P, wbufs, wbobj)

        xT3 = xT[:, :].rearrange("p (k s) -> p k s", k=KC)
        hT3 = hT[:, :].rearrange("p (k s) -> p k s", k=KC)
        xTb = [[Buf(f"xT{k}_{t}") for t in range(NT)] for k in range(KC)]
        hTb = [[Buf(f"hT{k}_{t}") for t in range(NT)] for k in range(KC)]
        b_const = Buf("const")

        identf = cf32[:, CF_ID:CF_ID + 128]
        identb = cbf[:, CB_ID:CB_ID + 128]
        ones_d = cbf[:, CB_O1024:CB_O1024 + 128]
        bd64 = cbf[:, CB_BD64:CB_BD64 + 128]
        ones128 = cbf[:, CB_O128:CB_O128 + 128]
        ones256 = cbf[:, CB_O256:CB_O256 + 128]
        ones1 = cbf[:, CB_ONE:CB_ONE + 128]
        tri = cbf[:, CB_TRI:CB_TRI + 128]

        ps, psb, pb, pbb, pw, pwb = [], [], [], [], [], []
        psum_ctr = [0]

        def set_psum(ph, n_single=6, n_bf=2, n_wide=0):
            k = psum_ctr[0]
            psum_ctr[0] += 1
            ps[:] = [ph.enter_context(nc.psum_tensor(f"ps{k}_{i}", [128, 512], F32)) for i in range(n_single)]
            psb[:] = [Buf(f"ps{i}") for i in range(n_single)]
            pb[:] = [ph.enter_context(nc.psum_tensor(f"pb{k}_{i}", [128, 1024], BF16)) for i in range(n_bf)]
            pbb[:] = [Buf(f"pb{i}") for i in range(n_bf)]
            pw[:] = [ph.enter_context(nc.psum_tensor(f"pw{k}_{i}", [128, 1024], F32)) for i in range(n_wide)]
            pwb[:] = [Buf(f"pwh{i}") for i in range(2 * n_wide)]

        def mm(psi, lhsT, rhs, start, stop, reads, sig=None, cols=None):
            out = ps[psi][:, :] if cols is None else ps[psi][:, cols]
            P.op("pe", lambda e: e.matmul(out, lhsT=lhsT, rhs=rhs, start=start, stop=stop),
                 reads=reads, writes=[psb[psi]], signal=(stop if sig is None else sig))

        def mm2(out, outbuf, lhsT, rhs, start, stop, reads, sig=None):
            P.op("pe", lambda e: e.matmul(out, lhsT=lhsT, rhs=rhs, start=start, stop=stop),
                 reads=reads, writes=[outbuf], signal=(stop if sig is None else sig))

        def act(out, in_, func, reads, writes, **kw):
            P.op("act", lambda e: e.activation(out=out, in_=in_, func=func, **kw), reads=reads, writes=writes,
                 fuse=("accum_out" not in kw))

        def rstd_from_ms(psi, n, rstd_ap, rstd_buf, prange=slice(0, 128)):
            act(rstd_ap, ps[psi][prange, 0:n], AF.Ln, [psb[psi], b_const], [rstd_buf], bias=epsc[prange, 0:1])
            act(rstd_ap, rstd_ap, AF.Exp, [rstd_buf], [rstd_buf], scale=-0.5)

        def plan_ffn(tag, wa, wb):
            wa3 = wa.rearrange("(k p) n -> p k n", p=128)
            wb3 = wb.rearrange("(j p) n -> p j n", p=128)
            for b in range(NFF // 2):
                W.add((tag, "a", b), [
                    (lambda t: v3(t, KC, 512)[:, :, 0:256], wa3[:, :, b * 256:(b + 1) * 256]),
                    (lambda t: v3(t, KC, 512)[:, :, 256:512],
                     wa3[:, :, DFF + b * 256:DFF + (b + 1) * 256]),
                ])
                W.add((tag, "b", b), [
                    (lambda t: v3(t, 2, 1024), wb3[:, 2 * b:2 * b + 2, :]),
                ])

        def blk_in(c0):
            return [(lambda t: v3(t, KC, 512), win3[:, :, c0:c0 + 512])]

        def plan_merge(tag, wsrc, nk, gate_off):
            w3 = wsrc.rearrange("(k p) n -> p k n", p=128)
            for cb in range(4):
                W.add((tag, "w", cb), [(lambda t, nk=nk: v3(t, nk, 256), w3[:, :, cb * 256:(cb + 1) * 256])])
                if cb % 2 == 0:
                    W.add((tag, "g", cb // 2), blk_in(gate_off + (cb // 2) * 512))

        plan_ffn("ffn1", w1a_d, w1b_d)
        if stage >= 2:
            for g4 in range(2):
                W.add(("dq", g4), blk_in(OFF_DQ + g4 * 512))
                W.add(("dk", g4), blk_in(OFF_DK + g4 * 512))
                W.add(("dv", g4), blk_in(OFF_DV + g4 * 512))
            for t in range(NT):
                plan_merge(("mdiff", t), wbd_d, 8, OFF_GT + 1024)
        if stage >= 3:
            wkv3 = wkv_d.rearrange("(k p) n -> p k n", p=128)
            for i in range(4):
                W.add(("mkv", i), [(lambda t: v3(t, KC, 512), wkv3[:, :, i * 512:(i + 1) * 512])])
            for t in range(NT):
                for i in range(2):
                    W.add(("mq", t, i), blk_in(OFF_MQ + i * 512))
                plan_merge(("mmem", t), wbm_d, 8, OFF_GT + 2048)
        if stage >= 4:
            for t in range(NT):
                for h in range(4):
                    W.add(("rqk", t, h), [
                        (lambda tt: v3(tt, KC, 512)[:, :, 0:256], win3[:, :, OFF_RQ + h * 256:OFF_RQ + (h + 1) * 256]),
                        (lambda tt: v3(tt, KC, 512)[:, :, 256:512], win3[:, :, OFF_RK + h * 256:OFF_RK + (h + 1) * 256]),
                    ])
                    W.add(("rg", t, h), blk_in(OFF_RG + h * 512))
                    W.add(("rv", t, h), blk_in(OFF_RV + h * 512))
                plan_merge(("mret", t), wbr_d, 16, OFF_GT)
        if stage >= 6:
            plan_ffn("ffn2", w2a_d, w2b_d)

        P.dma("sp", pvec[:, :], pvec_d[:, :], writes=[b_const])
        P.dma("sp", cf32[:, :], cf32_d[:, :], writes=[b_const])
        P.dma("sp", cbf[:, :], cbf_d[:, :], writes=[b_const])
        P.op("dve", lambda e: e.memset(epsc[:, :], EPS), writes=[b_const])
        P.op("dve", lambda e: e.memset(onec[:, :], 1.0), writes=[b_const])

        W.prefetch()
        with ExitStack() as ph:
            set_psum(ph)
            xin = [sb(f"xin{i}", [128, D], F32, ph) for i in range(2)]
            xinb = [Buf(f"xin{i}") for i in range(2)]
            for r in range(S // 128):
                t = r // 4
                xi, xib = xin[r % 2], xinb[r % 2]
                P.dma("sp", xi[:, :], x_d[r * 128:(r + 1) * 128, :], writes=[xib])
                for half in range(2):
                    pk = 4 + half
                    for q in range(4):
                        kc = half * 4 + q
                        P.op("pe", lambda e, kc=kc, q=q, pk=pk, xi=xi: e.transpose(
                            ps[pk][:, q * 128:(q + 1) * 128], xi[:, kc * 128:(kc + 1) * 128], identf),
                            reads=[xib, b_const], writes=[psb[pk]], signal=(q == 3))
                    dst = xT3[:, half * 4:half * 4 + 4, r * 128:(r + 1) * 128]
                    src = ps[pk][:, :].rearrange("p (a b) -> p a b", a=4)
                    wr = [xTb[half * 4 + q][t] for q in range(4)]
                    if half == 0:
                        P.op("dve", lambda e, dst=dst, src=src: e.tensor_copy(out=dst, in_=src),
                             reads=[psb[pk]], writes=wr)
                    else:
                        act(dst, src, AF.Copy, [psb[pk]], wr)
            P.barrier()

        ph_names = []
        def rmsnorm_to_hT(gcol0, ph, tiles=None, dst=None, nbufs=None):
            key = f"{gcol0}_{len(ph_names)}"
            ph_names.append(key)
            if nbufs is None:
                sq = [sb(f"nsq{i}_{key}", [128, TT], BF16, ph) for i in range(2)]
                sqb = [Buf(f"nsq{i}") for i in range(2)]
                rstd = sb(f"nrstd_{key}", [128, TT], F32, ph)
                rstdb = Buf("nrstd")
            else:
                sq, sqb, rstd, rstdb = nbufs
            for t in (range(NT) if tiles is None else tiles):
                ts = slice(t * TT, (t + 1) * TT)
                for kc in range(KC):
                    s_, sb_ = sq[kc % 2], sqb[kc % 2]
                    act(s_[:, :], xT3[:, kc, ts], AF.Square, [xTb[kc][t]], [sb_])
                    mm(5, ones_d, s_[:, :], kc == 0, kc == KC - 1, [sb_, b_const], sig=True)
                rstd_from_ms(5, TT, rstd[:, :], rstdb)
                for kc in range(KC):
                    o_ap = hT3[:, kc, ts] if dst is None else dst[0][:, kc, :]
                    o_b = hTb[kc][t] if dst is None else dst[1][kc]
                    P.op("dve", lambda e, kc=kc, o_ap=o_ap: e.scalar_tensor_tensor(
                        out=o_ap, in0=xT3[:, kc, ts], scalar=pvec[:, gcol0 + kc:gcol0 + kc + 1],
                        in1=rstd[:, :], op0=ALU.mult, op1=ALU.mult),
                        reads=[xTb[kc][t], rstdb, b_const], writes=[o_b])

        def ffn(tag, gcol0):
            with ExitStack() as ph:
                set_psum(ph)
                rmsnorm_to_hT(gcol0, ph)
                sg = [sb(f"sg{i}_{tag}", [128, TT], F32, ph) for i in range(2)]
                sgb = [Buf(f"sg{i}") for i in range(2)]
                uu = [sb(f"uu{i}_{tag}", [128, TT], BF16, ph) for i in range(4)]
                uub = [Buf(f"uu{i}") for i in range(4)]
                it = 0
                W.HOLD = 3
                pend = []

                def out_proj(wb3, wbb, ub, t):
                    ts = slice(t * TT, (t + 1) * TT)
                    for c in range(KC):
                        po = 4 + (c % 2)
                        for j in range(2):
                            mm(po, wb3[:, j, c * 128:(c + 1) * 128], ub[j][0][:, :], j == 0, j == 1,
                               [wbb, ub[j][1]])
                        P.op("dve", lambda e, c=c, po=po: e.scalar_tensor_tensor(
                            out=xT3[:, c, ts], in0=ps[po][:, :], scalar=0.5, in1=xT3[:, c, ts],
                            op0=ALU.mult, op1=ALU.add),
                            reads=[psb[po], xTb[c][t]], writes=[xTb[c][t]])

                for b in range(NFF // 2):
                    wa, wab = W.take((tag, "a", b))
                    wbt, wbb = W.take((tag, "b", b))
                    wa3 = v3(wa, KC, 512)
                    wb3 = v3(wbt, 2, 1024)
                    for t in range(NT):
                        ts = slice(t * TT, (t + 1) * TT)
                        ub = []
                        for j in range(2):
                            pg, pu = (it % 2) * 2, (it % 2) * 2 + 1
                            for kc in range(KC):
                                mm(pg, wa3[:, kc, j * 128:(j + 1) * 128], hT3[:, kc, ts], kc == 0, kc == KC - 1,
                                   [wab, hTb[kc][t]])
                            for kc in range(KC):
                                mm(pu, wa3[:, kc, 256 + j * 128:256 + (j + 1) * 128], hT3[:, kc, ts],
                                   kc == 0, kc == KC - 1, [wab, hTb[kc][t]])
                            s_, sb_ = sg[it % 2], sgb[it % 2]
                            u_, ub_ = uu[it % 4], uub[it % 4]
                            act(s_[:, :], ps[pg][:, :], AF.Silu, [psb[pg]], [sb_])
                            P.op("dve", lambda e, s_=s_, u_=u_, pu=pu: e.tensor_tensor(
                                out=u_[:, :], in0=ps[pu][:, :], in1=s_[:, :], op=ALU.mult),
                                reads=[psb[pu], sb_], writes=[ub_])
                            ub.append((u_, ub_))
                            it += 1
                        if pend:
                            out_proj(*pend.pop(0))
                        pend.append((wb3, wbb, ub, t))
                out_proj(*pend.pop(0))
                W.HOLD = 2
                P.barrier()

        def branch_merge(tag, bi, t, src3, src_bufs, nk, ph_bufs, h3t=None, hbt=None):
            gsb, gsbb, ypt, yptb = ph_bufs
            ts = slice(t * TT, (t + 1) * TT)
            if h3t is None:
                h3t = hT3[:, :, ts]
                hbt = [hTb[kc][t] for kc in range(KC)]
            wg3 = None
            for cb in range(4):
                wt, wtb = W.take((tag, "w", cb))
                w3 = v3(wt, nk, 256)
                if cb % 2 == 0:
                    wg, wgb = W.take((tag, "g", cb // 2))
                    wg3 = v3(wg, KC, 512)
                for cc in range(2):
                    c = cb * 2 + cc
                    pz, pg = (c % 2) * 2, (c % 2) * 2 + 1
                    for k in range(nk):
                        mm(pz, w3[:, k, cc * 128:(cc + 1) * 128], src3[:, k, :], k == 0, k == nk - 1,
                           [wtb, src_bufs[k]])
                    gc = (c % 4) * 128
                    for kc in range(KC):
                        mm(pg, wg3[:, kc, gc:gc + 128], h3t[:, kc, :], kc == 0, kc == KC - 1,
                           [wgb, hbt[kc]])
                    g_, gb_ = gsb[c % 2], gsbb[c % 2]
                    act(g_[:, :], ps[pg][:, :], AF.Sigmoid, [psb[pg]], [gb_])
                    P.op("dve", lambda e, c=c, pz=pz, g_=g_: e.tensor_tensor(
                        out=ypt[:, c * TT:(c + 1) * TT], in0=ps[pz][:, :], in1=g_[:, :], op=ALU.mult),
                        reads=[psb[pz], gb_], writes=[yptb])
            P.dma("sp", yp_d[bi, :, :, ts].rearrange("k p s -> p k s"),
                  ypt[:, 0:KC * TT].rearrange("p (k s) -> p k s", k=KC), reads=[yptb], writes=[ypdb[bi][t]])

        def merge_bufs(ph, nm):
            gsb = [sb(f"gsb{i}_{nm}", [128, TT], F32, ph) for i in range(2)]
            gsbb = [Buf(f"gsb{i}") for i in range(2)]
            ypt = sb(f"ypt_{nm}", [128, KC * TT], BF16, ph)
            return gsb, gsbb, ypt, Buf("ypt")

        if stage >= 1:
            ffn("ffn1", PV_GFFN1)

        if stage >= 2:
            with ExitStack() as ph:
                set_psum(ph)
                rmsnorm_to_hT(PV_GMIX, ph)
                P.barrier()

            with ExitStack() as ph:
                DIF = sb("DIF", [128, 8 * S], BF16, ph)
                DIF3 = DIF[:, :].rearrange("p (h s) -> p h s", h=8)
                difb = [[Buf(f"dif{h}_{t}") for t in range(NT)] for h in range(8)]
                ph2 = ExitStack()
                dq0 = sb("dq0", [128, S], BF16, ph2)
                dq1 = sb("dq1", [128, S], BF16, ph2)
                dkn = sb("dkn", [128, S], BF16, ph2)
                dvT = sb("dvT", [128, 16 * 128], BF16, ph2)
                dqb = [Buf(f"dq_{t}") for t in range(NT)]
                dkb = [Buf(f"dk_{t}") for t in range(NT)]
                dvb = [Buf(f"dv_{t}") for t in range(NT)]
                set_psum(ph2, 4, 0, 2)
                sqd = [sb(f"sqd{i}", [128, TT], BF16, ph2) for i in range(2)]
                sqdb = [Buf(f"sqd{i}") for i in range(2)]
                rsd = [sb(f"rsd{i}", [128, TT], F32, ph2) for i in range(2)]
                rsdb = [Buf(f"rsd{i}") for i in range(2)]
                ET = [sb(f"ET{i}", [128, 2 * TT], BF16, ph2) for i in range(2)]
                ETb = [Buf(f"ET{i}") for i in range(2)]
                Esum = sb("Esum", [128, TT], F32, ph2)
                Esumb = Buf("Esum")
                rl1 = sb("rl1", [128, TT], F32, ph2)
                rl1b = Buf("rl1")
                rl0 = sb("rl0", [128, TT], F32, ph2)
                rl0b = Buf("rl0")
                rse = sb("rse", [128, TT], F32, ph2)
                rseb = Buf("rse")
                sqe = sb("sqe", [128, TT], BF16, ph2)
                sqeb = Buf("sqe")
                Ehi = sb("Ehi", [128, TT], BF16, ph2)
                Elo = sb("Elo", [128, TT], BF16, ph2)
                Ehib, Elob = Buf("Ehi"), Buf("Elo")
                Ocp = sb("Ocp", [128, 2 * TT], F32, ph2)
                Ocpb = Buf("Ocp")
                lamv = sb("lamv_sb", [128, 256], F32, ph2)
                lamt = sb("lamt", [128, 64], F32, ph2)
                lcol = sb("lcol", [128, 4], F32, ph2)
                lamb = Buf("lam")

                P.dma("sp", lamv[:, :], lamv_d[0:1, :].broadcast_to([128, 256]), writes=[lamb])
                for i in range(2):
                    P.op("dve", lambda e, i=i: e.tensor_tensor(
                        out=lamt[:, :], in0=lamv[:, i * 128:i * 128 + 64], in1=lamv[:, i * 128 + 64:i * 128 + 128],
                        op=ALU.mult), reads=[lamb], writes=[lamb])
                    P.op("dve", lambda e, i=i: e.reduce_sum(out=lcol[:, i:i + 1], in_=lamt[:, :],
                                                            axis=mybir.AxisListType.X),
                         reads=[lamb], writes=[lamb])
                act(lcol[:, 0:2], lcol[:, 0:2], AF.Exp, [lamb], [lamb])
                P.op("dve", lambda e: e.tensor_tensor(out=lcol[:, 2:3], in0=lcol[:, 1:2], in1=lcol[:, 0:1],
                                                      op=ALU.subtract), reads=[lamb], writes=[lamb])
                P.op("dve", lambda e: e.tensor_scalar(out=lcol[:, 2:3], in0=lcol[:, 2:3], scalar1=-LAM_INIT,
                                                      scalar2=None, op0=ALU.add), reads=[lamb], writes=[lamb])
                P.op("dve", lambda e: e.tensor_scalar(out=lcol[:, 3:4], in0=pvec[:, PV_GDO:PV_GDO + 1],
                                                      scalar1=1.0 - LAM_INIT, scalar2=None, op0=ALU.mult),
                     reads=[lamb, b_const], writes=[lamb])
                nlam = lcol[:, 2:3]
                gdo = lcol[:, 3:4]
                P.op("dve", lambda e: e.memset(dq0[64:128, :], 0.0), writes=dqb)
                P.op("dve", lambda e: e.memset(dq1[0:64, :], 0.0), writes=dqb)

                W.HOLD = 4
                wq = wk = wv = None
                pending = []

                def flush_pending():
                    while pending:
                        pending.pop(0)()

                def pop_pending(n):
                    for _ in range(n):
                        if pending:
                            pending.pop(0)()

                def pwh(i):
                    return pw[i // 2][:, (i % 2) * 512:(i % 2 + 1) * 512]

                for h in range(8):
                    hl = h % 4
                    if hl == 0:
                        wq, wqb = W.take(("dq", h // 4))
                        wk, wkb = W.take(("dk", h // 4))
                        wv, wvb = W.take(("dv", h // 4))
                    wq3, wk3, wv3 = v3(wq, KC, 512), v3(wk, KC, 512), v3(wv, KC, 512)
                    groups = [(t, which) for t in range(NT) for which in (0, 1)]

                    def stageA(g):
                        t, which = groups[g]
                        ts = slice(t * TT, (t + 1) * TT)
                        w3, wb_ = (wq3, wqb) if which == 0 else (wk3, wkb)
                        i = g % 4
                        for kc in range(KC):
                            mm2(pwh(i), pwb[i], w3[:, kc, hl * 128:(hl + 1) * 128], hT3[:, kc, ts], kc == 0, kc == KC - 1,
                                [wb_, hTb[kc][t]])

                    def stageB(g):
                        t, which = groups[g]
                        ts = slice(t * TT, (t + 1) * TT)
                        i, j = g % 4, g % 2
                        act(sqd[j][:, :], pwh(i), AF.Square, [pwb[i]], [sqdb[j]])
                        mm(3, bd64, sqd[j][:, :], True, True, [sqdb[j], b_const])
                        rstd_from_ms(3, TT, rsd[j][:, :], rsdb[j])
                        if which == 0:
                            for c, dq in enumerate((dq0, dq1)):
                                pr = slice(c * 64, (c + 1) * 64)
                                P.op("dve", lambda e, dq=dq, pr=pr: e.scalar_tensor_tensor(
                                    out=dq[pr, ts], in0=pwh(i)[pr, :], scalar=pvec[pr, PV_GDQ:PV_GDQ + 1],
                                    in1=rsd[j][pr, :], op0=ALU.mult, op1=ALU.mult),
                                    reads=[pwb[i], rsdb[j], b_const], writes=[dqb[t]])
                        else:
                            P.op("dve", lambda e: e.scalar_tensor_tensor(
                                out=dkn[:, ts], in0=pwh(i), scalar=pvec[:, PV_GDK:PV_GDK + 1],
                                in1=rsd[j][:, :], op0=ALU.mult, op1=ALU.mult),
                                reads=[pwb[i], rsdb[j], b_const], writes=[dkb[t]])

                    def stageV(t):
                        for c4 in range(4):
                            tok = slice(t * TT + c4 * 128, t * TT + (c4 + 1) * 128)
                            for kc in range(KC):
                                mm(t % 2, hT3[:, kc, tok], wv3[:, kc, hl * 128:(hl + 1) * 128], kc == 0, kc == KC - 1,
                                   [wvb, hTb[kc][t]], cols=slice(c4 * 128, (c4 + 1) * 128), sig=(kc == KC - 1))
                        act(dvT[:, t * 512:(t + 1) * 512], ps[t % 2][:, :], AF.Copy, [psb[t % 2]], [dvb[t]])

                    for g in range(8 + 2):
                        if g < 8:
                            stageA(g)
                        if g >= 1:
                            pop_pending(2)
                        if g >= 2:
                            stageB(g - 2)
                        if g % 2 == 1 and g < 8:
                            stageV(g // 2)

                    for qt in range(NT):
                        nkt = 4 * qt + 4
                        ts = slice(qt * TT, (qt + 1) * TT)

                        def emitS(kt):
                            w = kt % 2
                            j0 = max(0, kt - 4 * qt) * 128
                            qs = slice(qt * TT + j0, (qt + 1) * TT)
                            for c, dq in enumerate((dq0, dq1)):
                                mm2(pw[w][:, c * 512 + j0:(c + 1) * 512], pwb[2 * w + c], dkn[:, kt * 128:(kt + 1) * 128],
                                    dq[:, qs], True, True, [dkb[kt // 4], dqb[qt]], sig=(c == 1))

                        emitS(0)
                        for kt in range(nkt):
                            if kt + 1 < nkt:
                                emitS(kt + 1)
                            if kt >= 1:
                                pop_pending(2 if len(pending) > 6 else 1)
                            w = kt % 2
                            j0 = max(0, kt - 4 * qt) * 128
                            diag = kt >= 4 * qt
                            E, Eb = ET[w], ETb[w]
                            s3 = pw[w][:, :].rearrange("p (c s) -> p c s", c=2)[:, :, j0:TT]
                            e3 = E[:, :].rearrange("p (c s) -> p c s", c=2)[:, :, j0:TT]
                            act(e3, s3, AF.Exp, [pwb[2 * w], pwb[2 * w + 1]], [Eb], scale=0.125)
                            if diag:
                                for c in range(2):
                                    blk = E[:, c * 512 + j0:c * 512 + j0 + 128]
                                    P.op("dve", lambda e, blk=blk: e.tensor_tensor(out=blk, in0=blk, in1=tri, op=ALU.mult),
                                         reads=[Eb, b_const], writes=[Eb])
                            e0, es0 = E[:, j0:TT], Esum[:, j0:TT]
                            if kt == 0:
                                P.op("dve", lambda e, e0=e0, es0=es0: e.tensor_copy(out=es0, in_=e0), reads=[Eb], writes=[Esumb])
                            else:
                                P.op("dve", lambda e, e0=e0, es0=es0: e.tensor_tensor(out=es0, in0=e0, in1=es0, op=ALU.add),
                                     reads=[Eb, Esumb], writes=[Esumb])
                            for c in range(2):
                                mm(c, dvT[:, kt * 128:(kt + 1) * 128], E[:, c * 512 + j0:(c + 1) * 512], kt == 0, kt == nkt - 1,
                                   [dvb[kt // 4], Eb], cols=slice(j0, TT), sig=True)
                            mm(3, ones1, E[:, 512 + j0:1024], kt == 0, kt == nkt - 1, [Eb, b_const], cols=slice(j0, TT), sig=True)
                        Es, Esb = Esum, Esumb
                        for c in range(2):
                            P.op("dve", lambda e, c=c: e.tensor_copy(out=Ocp[:, c * 512:(c + 1) * 512], in_=ps[c][:, :]),
                                 reads=[psb[c]], writes=[Ocpb])
                        act(rl1[:, :], ps[3][:, :], AF.Ln, [psb[3]], [rl1b])
                        act(rl1[:, :], rl1[:, :], AF.Exp, [rl1b], [rl1b], scale=-1.0)
                        P.op("dve", lambda e: e.tensor_tensor(out=Ocp[:, 512:1024], in0=Ocp[:, 512:1024], in1=rl1[:, :],
                                                              op=ALU.mult), reads=[Ocpb, rl1b], writes=[Ocpb])
                        P.op("dve", lambda e: e.tensor_copy(out=Ehi[:, :], in_=Esum[:, :]), reads=[Esumb], writes=[Ehib])
                        P.op("dve", lambda e: e.tensor_tensor(out=Elo[:, :], in0=Esum[:, :], in1=Ehi[:, :], op=ALU.subtract),
                             reads=[Esumb, Ehib], writes=[Elob])

                        def f1():
                            mm(2, ones1, Ehi[:, :], True, False, [Ehib, b_const], sig=False)
                            mm(2, ones1, Elo[:, :], False, True, [Elob, b_const], sig=True)

                        def f2():
                            act(rl0[:, :], ps[2][:, :], AF.Ln, [psb[2]], [rl0b])
                            act(rl0[:, :], rl0[:, :], AF.Exp, [rl0b], [rl0b], scale=-1.0)

                        def f2b():
                            P.op("dve", lambda e: e.tensor_tensor(
                                out=Ocp[:, 0:512], in0=Ocp[:, 0:512], in1=rl0[:, :], op=ALU.mult),
                                reads=[Ocpb, rl0b], writes=[Ocpb])

                        def f3():
                            P.op("dve", lambda e: e.scalar_tensor_tensor(
                                out=Ocp[:, 0:512], in0=Ocp[:, 512:1024], scalar=nlam, in1=Ocp[:, 0:512],
                                op0=ALU.mult, op1=ALU.add), reads=[Ocpb, lamb], writes=[Ocpb])
                            act(sqe[:, :], Ocp[:, 0:512], AF.Square, [Ocpb], [sqeb])

                        def f4():
                            mm(2, ones128, sqe[:, :], True, True, [sqeb, b_const])

                        def f5():
                            rstd_from_ms(2, TT, rse[:, :], rseb)

                        def f6(h=h, ts=ts):
                            P.op("dve", lambda e: e.scalar_tensor_tensor(
                                out=DIF3[:, h, ts], in0=Ocp[:, 0:512], scalar=gdo, in1=rse[:, :], op0=ALU.mult, op1=ALU.mult),
                                reads=[Ocpb, rseb, lamb], writes=[difb[h][ts.start // TT]])

                        pending.extend([f1, f2, f2b, f3, f4, f5, f6])
                flush_pending()
                W.HOLD = 2
                P.barrier()
                ph2.close()
                set_psum(ph)
                mb = merge_bufs(ph, "d")
                for t in range(NT):
                    ts = slice(t * TT, (t + 1) * TT)
                    branch_merge(("mdiff", t), 1, t, DIF3[:, :, ts], [difb[h][t] for h in range(8)], 8, mb)
                P.barrier()


        if stage >= 3:
            with ExitStack() as ph:
                set_psum(ph, 8, 0, 0)
                memin = sb("memin", [128, 2 * D], F32, ph)
                memT = sb("memT", [128, KC * 256], F32, ph)
                memh = sb("memh", [128, KC * 256], BF16, ph)
                mkn = sb("mkn", [128, 4 * 2 * 256], BF16, ph)
                mvT = sb("mvT", [128, 2 * 1024], BF16, ph)
                sqm = [sb(f"sqm{i}", [128, TT], BF16, ph) for i in range(2)]
                sqmb = [Buf(f"sqm{i}") for i in range(2)]
                rsm = sb("rsm", [128, TT], F32, ph)
                rsmb = Buf("rsm")
                mqn = sb("mqn", [128, 2 * TT], BF16, ph)
                mqnb = Buf("mqn")
                EM = [sb(f"EM{i}", [128, TT], BF16, ph) for i in range(2)]
                EMb = [Buf(f"EM{i}") for i in range(2)]
                rl = sb("rl", [128, TT], F32, ph)
                rlb = Buf("rl")
                MO = sb("MO", [128, KC * TT], BF16, ph)
                mob = Buf("MO")
                meminb, memTb, memhb, mknb, mvTb = Buf("memin"), Buf("memT"), Buf("memh"), Buf("mkn"), Buf("mvT")
                memin3 = memin[:, :].rearrange("p (a n) -> p a n", a=2)
                memT3 = memT[:, :].rearrange("p (k m) -> p k m", k=KC)
                memh3 = memh[:, :].rearrange("p (k m) -> p k m", k=KC)
                mkn4 = mkn[:, :].rearrange("p (h c m) -> p h c m", h=4, c=2)
                mvT3 = mvT[:, :].rearrange("p (a n) -> p a n", a=2)
                mqn3 = mqn[:, :].rearrange("p (c s) -> p c s", c=2)
                MO3 = MO[:, :].rearrange("p (k s) -> p k s", k=KC)

                P.dma("sp", memin3, mem_d.rearrange("(a p) n -> p a n", p=128), writes=[meminb])
                for mt in range(2):
                    for half in range(2):
                        pk = 4 + half
                        for q in range(4):
                            kc = half * 4 + q
                            P.op("pe", lambda e, kc=kc, q=q, pk=pk, mt=mt: e.transpose(
                                ps[pk][:, q * 128:(q + 1) * 128], memin3[:, mt, kc * 128:(kc + 1) * 128], identf),
                                reads=[meminb, b_const], writes=[psb[pk]], signal=(q == 3))
                        dst = memT3[:, half * 4:half * 4 + 4, mt * 128:(mt + 1) * 128]
                        src = ps[pk][:, :].rearrange("p (a b) -> p a b", a=4)
                        P.op("dve", lambda e, dst=dst, src=src: e.tensor_copy(out=dst, in_=src),
                             reads=[psb[pk]], writes=[memTb])
                for kc in range(KC):
                    s_, sb_ = sqm[kc % 2], sqmb[kc % 2]
                    act(s_[:, 0:256], memT3[:, kc, :], AF.Square, [memTb], [sb_])
                    mm(5, ones_d, s_[:, 0:256], kc == 0, kc == KC - 1, [sb_, b_const], sig=True, cols=slice(0, 256))
                rstd_from_ms(5, 256, rsm[:, 0:256], rsmb)
                for kc in range(KC):
                    P.op("dve", lambda e, kc=kc: e.scalar_tensor_tensor(
                        out=memh3[:, kc, :], in0=memT3[:, kc, :], scalar=pvec[:, PV_GMEM + kc:PV_GMEM + kc + 1],
                        in1=rsm[:, 0:256], op0=ALU.mult, op1=ALU.mult),
                        reads=[memTb, rsmb, b_const], writes=[memhb])
                c256 = slice(0, 256)
                for blk in range(2):
                    wk, wkb = W.take(("mkv", blk))
                    wk3 = v3(wk, KC, 512)
                    for hh in range(2):
                        hm = blk * 2 + hh
                        for dc in range(2):
                            for kc in range(KC):
                                mm(dc, wk3[:, kc, hh * 256 + dc * 128:hh * 256 + (dc + 1) * 128], memh3[:, kc, :],
                                   kc == 0, kc == KC - 1, [wkb, memhb], cols=c256)
                            act(sqm[dc][:, 0:256], ps[dc][:, 0:256], AF.Square, [psb[dc]], [sqmb[dc]])
                            mm(2, ones256, sqm[dc][:, 0:256], dc == 0, dc == 1, [sqmb[dc], b_const], sig=True, cols=c256)
                        rstd_from_ms(2, 256, rsm[:, 0:256], rsmb)
                        for dc in range(2):
                            P.op("dve", lambda e, dc=dc, hm=hm: e.scalar_tensor_tensor(
                                out=mkn4[:, hm, dc, :], in0=ps[dc][:, 0:256], scalar=pvec[:, PV_GMK + dc:PV_GMK + dc + 1],
                                in1=rsm[:, 0:256], op0=ALU.mult, op1=ALU.mult),
                                reads=[psb[dc], rsmb, b_const], writes=[mknb])
                for vb in range(2):
                    wv, wvb = W.take(("mkv", 2 + vb))
                    wv3 = v3(wv, KC, 512)
                    for mt in range(2):
                        for kc in range(KC):
                            mm(3, memh3[:, kc, mt * 128:(mt + 1) * 128], wv3[:, kc, :], kc == 0, kc == KC - 1,
                               [wvb, memhb])
                        act(mvT3[:, mt, vb * 512:(vb + 1) * 512], ps[3][:, :], AF.Copy, [psb[3]], [mvTb])

                mb = merge_bufs(ph, "m")
                mqn2 = [mqn, sb("mqn_b", [128, 2 * TT], BF16, ph)]
                mqnb2 = [mqnb, Buf("mqn_b")]
                wq3_cur = [None, None]

                def stageP(t, hm):
                    ts = slice(t * TT, (t + 1) * TT)
                    if hm % 2 == 0:
                        wq, wqb = W.take(("mq", t, hm // 2))
                        wq3_cur[0], wq3_cur[1] = v3(wq, KC, 512), wqb
                    wq3, wqb = wq3_cur
                    hh = hm % 2
                    ba = (hm % 2) * 2
                    m3 = mqn2[hm % 2][:, :].rearrange("p (c s) -> p c s", c=2)
                    for dc in range(2):
                        for kc in range(KC):
                            mm(ba + dc, wq3[:, kc, hh * 256 + dc * 128:hh * 256 + (dc + 1) * 128], hT3[:, kc, ts],
                               kc == 0, kc == KC - 1, [wqb, hTb[kc][t]])
                        act(sqm[dc][:, :], ps[ba + dc][:, :], AF.Square, [psb[ba + dc]], [sqmb[dc]])
                        mm(4, ones256, sqm[dc][:, :], dc == 0, dc == 1, [sqmb[dc], b_const], sig=True)
                    rstd_from_ms(4, TT, rsm[:, :], rsmb)
                    for dc in range(2):
                        P.op("dve", lambda e, dc=dc: e.scalar_tensor_tensor(
                            out=m3[:, dc, :], in0=ps[ba + dc][:, :], scalar=pvec[:, PV_GMQ + dc:PV_GMQ + dc + 1],
                            in1=rsm[:, :], op0=ALU.mult, op1=ALU.mult),
                            reads=[psb[ba + dc], rsmb, b_const], writes=[mqnb2[hm % 2]])

                def stageA(t, hm):
                    m3 = mqn2[hm % 2][:, :].rearrange("p (c s) -> p c s", c=2)
                    for mt in range(2):
                        for dc in range(2):
                            mm(5 + mt, mkn4[:, hm, dc, mt * 128:(mt + 1) * 128], m3[:, dc, :], dc == 0, dc == 1,
                               [mknb, mqnb2[hm % 2]])
                        act(EM[mt][:, :], ps[5 + mt][:, :], AF.Exp, [psb[5 + mt]], [EMb[mt]], scale=1.0 / 16.0)
                    for mt in range(2):
                        mm(7, ones1, EM[mt][:, :], mt == 0, mt == 1, [EMb[mt], b_const])
                    for ec in range(2):
                        for mt in range(2):
                            mm(5 + ec, mvT3[:, mt, hm * 256 + ec * 128:hm * 256 + (ec + 1) * 128], EM[mt][:, :],
                               mt == 0, mt == 1, [mvTb, EMb[mt]])
                    act(rl[:, :], ps[7][:, :], AF.Ln, [psb[7]], [rlb])
                    act(rl[:, :], rl[:, :], AF.Exp, [rlb], [rlb], scale=-1.0)
                    for ec in range(2):
                        P.op("dve", lambda e, ec=ec: e.tensor_tensor(
                            out=MO3[:, hm * 2 + ec, :], in0=ps[5 + ec][:, :], in1=rl[:, :], op=ALU.mult),
                            reads=[psb[5 + ec], rlb], writes=[mob])

                for t in range(NT):
                    stageP(t, 0)
                    for hm in range(4):
                        if hm + 1 < 4:
                            stageP(t, hm + 1)
                        stageA(t, hm)
                    branch_merge(("mmem", t), 2, t, MO3, [mob] * 8, 8, mb)
                P.barrier()

        if stage >= 4:
            with ExitStack() as ph:
                set_psum(ph, 7, 1, 0)
                pbKb, pbRb = Buf("pbK"), Buf("pbR")
                htile = hT[:, 0:4096]
                RT = hT[:, 4096:12288]
                ypt_r = hT[:, 12288:16384]
                htile3 = htile.rearrange("p (k s) -> p k s", k=KC)
                RT3 = RT.rearrange("p (k s) -> p k s", k=16)
                htb = [Buf(f"ht{k}") for k in range(KC)]
                rtb = [Buf(f"rt{k}") for k in range(16)]
                cret = sb("cret_sb", [128, CR_N], F32, ph)
                cretb = Buf("cret")
                posi = sb("posi", [128, TT], I32, ph)
                posb = Buf("posi")
                Tm = [sb(f"Tm{i}", [128, TT], F32, ph) for i in range(4)]
                Tmb = [Buf(f"Tm{i}") for i in range(4)]
                cosT = sb("cosT", [128, TT], F32, ph)
                sinT = sb("sinT", [128, TT], F32, ph)
                csb = Buf("cossin")
                qr = [sb(f"qr{i}", [128, 2 * TT], BF16, ph) for i in range(2)]
                qd = [sb(f"qd{i}", [128, 2 * TT], BF16, ph) for i in range(2)]
                kr = [sb(f"kr{i}", [128, 2 * TT], BF16, ph) for i in range(2)]
                qrb = [Buf(f"qr{i}") for i in range(2)]
                qdb = [Buf(f"qd{i}") for i in range(2)]
                krb = [Buf(f"kr{i}") for i in range(2)]
                kT = [sb(f"kT{i}", [128, 1024], BF16, ph) for i in range(2)]
                kTb = [Buf(f"kT{i}") for i in range(2)]
                v_sb = [sb(f"v_sb{i}", [128, 4 * TT], BF16, ph) for i in range(2)]
                sg_sb = [sb(f"sg_sb{i}", [128, 4 * TT], BF16, ph) for i in range(2)]
                vbb = [[Buf(f"v{i}_{c}") for c in range(4)] for i in range(2)]
                sgbb = [[Buf(f"sg{i}_{c}") for c in range(4)] for i in range(2)]
                state = sb("state", [128, 8 * TT], F32, ph)
                state3 = state[:, :].rearrange("p (a e) -> p a e", a=8)
                stb = [Buf(f"st{a}") for a in range(8)]
                stbf = sb("stbf", [128, 2 * TT], BF16, ph)
                stbf3 = stbf[:, :].rearrange("p (a e) -> p a e", a=2)
                stbb = [Buf(f"stbf{a}") for a in range(2)]
                scD = [sb(f"scD{i}", [128, 128], BF16, ph) for i in range(2)]
                scDb = [Buf(f"scD{i}") for i in range(2)]
                ssq = sb("ssq", [128, 8], F32, ph)
                ssqb = Buf("ssq")
                junk = sb("junk", [128, TT], BF16, ph)
                junkb = Buf("junk")
                ret_sb = [sb(f"ret_sb{i}", [128, TT], BF16, ph) for i in range(2)]
                retb = [Buf(f"ret_sb{i}") for i in range(2)]
                nsq = [sb(f"rnsq{i}", [128, TT], BF16, ph) for i in range(2)]
                nsqb = [Buf(f"rnsq{i}") for i in range(2)]
                nbufs = (nsq, nsqb, Tm[3], Tmb[3])
                mb_r = ([Tm[0], Tm[1]], [Tmb[0], Tmb[1]], ypt_r, Buf("ypt_r"))

                P.dma("sp", cret[:, :], cret_d[:, :], writes=[cretb])

                def rotary_prep(t):
                    ts = slice(t * TT, (t + 1) * TT)
                    P.dma("sp", posi[:, :], pos_d[0:1, ts].broadcast_to([128, TT]), writes=[posb])
                    ang, ta, tb = Tm[0], Tm[1], Tm[2]
                    ki = Tm[3][:, :].bitcast(I32)
                    P.op("dve", lambda e: e.tensor_scalar(out=ang[:, :], in0=posi[:, :], scalar1=pvec[:, PV_INV:PV_INV + 1],
                                                          scalar2=None, op0=ALU.mult),
                         reads=[posb, b_const], writes=[Tmb[0]])
                    for which, dstT in ((0, sinT), (1, cosT)):
                        if which == 1:
                            P.op("dve", lambda e: e.tensor_scalar(out=ang[:, :], in0=ang[:, :], scalar1=math.pi / 2.0,
                                                                  scalar2=None, op0=ALU.add),
                                 reads=[Tmb[0]], writes=[Tmb[0]])
                        P.op("dve", lambda e: e.tensor_scalar(out=ki, in0=ang[:, :], scalar1=1.0 / TWO_PI,
                                                              scalar2=None, op0=ALU.mult),
                             reads=[Tmb[0]], writes=[Tmb[3]])
                        P.op("dve", lambda e: e.scalar_tensor_tensor(out=ta[:, :], in0=ki, scalar=-CW1, in1=ang[:, :],
                                                                     op0=ALU.mult, op1=ALU.add),
                             reads=[Tmb[3], Tmb[0]], writes=[Tmb[1]])
                        P.op("dve", lambda e: e.scalar_tensor_tensor(out=ta[:, :], in0=ki, scalar=-CW2, in1=ta[:, :],
                                                                     op0=ALU.mult, op1=ALU.add),
                             reads=[Tmb[3], Tmb[1]], writes=[Tmb[1]])
                        P.op("dve", lambda e: e.tensor_scalar(out=tb[:, :], in0=ta[:, :], scalar1=math.pi, scalar2=None,
                                                              op0=ALU.is_gt), reads=[Tmb[1]], writes=[Tmb[2]])
                        P.op("dve", lambda e: e.scalar_tensor_tensor(out=ta[:, :], in0=tb[:, :], scalar=-TWO_PI, in1=ta[:, :],
                                                                     op0=ALU.mult, op1=ALU.add),
                             reads=[Tmb[2], Tmb[1]], writes=[Tmb[1]])
                        P.op("dve", lambda e: e.tensor_scalar(out=tb[:, :], in0=ta[:, :], scalar1=-math.pi, scalar2=None,
                                                              op0=ALU.is_lt), reads=[Tmb[1]], writes=[Tmb[2]])
                        P.op("dve", lambda e: e.scalar_tensor_tensor(out=ta[:, :], in0=tb[:, :], scalar=TWO_PI, in1=ta[:, :],
                                                                     op0=ALU.mult, op1=ALU.add),
                             reads=[Tmb[2], Tmb[1]], writes=[Tmb[1]])
                        P.op("dve", lambda e: e.tensor_scalar(out=ta[:, :], in0=ta[:, :], scalar1=-PI_SAFE, scalar2=PI_SAFE,
                                                              op0=ALU.max, op1=ALU.min), reads=[Tmb[1]], writes=[Tmb[1]])
                        act(dstT[:, :], ta[:, :], AF.Sin, [Tmb[1]], [csb])

                def phase_i(t, h):
                    pp = h % 2
                    qr3 = qr[pp][:, :].rearrange("p (c s) -> p c s", c=2)
                    qd3 = qd[pp][:, :].rearrange("p (c s) -> p c s", c=2)
                    kr3 = kr[pp][:, :].rearrange("p (c s) -> p c s", c=2)
                    v3_ = v_sb[pp][:, :].rearrange("p (c e) -> p c e", c=4)
                    sg3_ = sg_sb[pp][:, :].rearrange("p (c e) -> p c e", c=4)
                    wqk, wqkb = W.take(("rqk", t, h))
                    wqk3 = v3(wqk, KC, 512)
                    qdec_b = cret[:, CR_QDEC + h * 128:CR_QDEC + (h + 1) * 128].rearrange(
                        "p (o d) -> p o d", o=1).broadcast_to([128, 4, 128])
                    for which in range(2):
                        for dc in range(2):
                            for kc in range(KC):
                                mm(dc, wqk3[:, kc, which * 256 + dc * 128:which * 256 + (dc + 1) * 128],
                                   htile3[:, kc, :], kc == 0, kc == KC - 1, [wqkb, htb[kc]])
                            yield
                        for half in range(2):
                            c1, c2 = (cosT, sinT) if half == 0 else (sinT, cosT)
                            P.op("dve", lambda e, c1=c1: e.tensor_tensor(out=Tm[0][:, :], in0=ps[0][:, :], in1=c1[:, :],
                                                                         op=ALU.mult),
                                 reads=[psb[0], csb], writes=[Tmb[0]])
                            P.op("dve", lambda e, c2=c2: e.tensor_tensor(out=Tm[1][:, :], in0=ps[1][:, :], in1=c2[:, :],
                                                                         op=ALU.mult),
                                 reads=[psb[1], csb], writes=[Tmb[1]])
                            op_ = ALU.subtract if half == 0 else ALU.add
                            if which == 0:
                                P.op("dve", lambda e, op_=op_: e.tensor_tensor(out=Tm[2][:, :], in0=Tm[0][:, :],
                                                                               in1=Tm[1][:, :], op=op_),
                                     reads=[Tmb[0], Tmb[1]], writes=[Tmb[2]])
                                act(qr3[:, half, :], Tm[2][:, :], AF.Copy, [Tmb[2]], [qrb[pp]])
                                P.op("dve", lambda e, half=half: e.tensor_tensor(
                                    out=qd3[:, half, :].rearrange("p (c d) -> p c d", c=4),
                                    in0=Tm[2][:, :].rearrange("p (c d) -> p c d", c=4), in1=qdec_b, op=ALU.mult),
                                    reads=[Tmb[2], cretb], writes=[qdb[pp]])
                            else:
                                P.op("dve", lambda e, op_=op_, half=half: e.tensor_tensor(
                                    out=kr3[:, half, :], in0=Tm[0][:, :], in1=Tm[1][:, :], op=op_),
                                    reads=[Tmb[0], Tmb[1]], writes=[krb[pp]])
                            yield
                    for hf in range(2):
                        for i4 in range(4):
                            idx = hf * 4 + i4
                            c, dc = idx // 2, idx % 2
                            P.op("pe", lambda e, c=c, dc=dc, i4=i4: e.transpose(
                                pb[0][:, i4 * 128:(i4 + 1) * 128], kr3[:, dc, c * 128:(c + 1) * 128], identb),
                                reads=[krb[pp], b_const], writes=[pbKb], signal=(i4 == 3))
                        P.op("dve", lambda e, hf=hf: e.tensor_scalar(
                            out=kT[pp][:, hf * 512:(hf + 1) * 512], in0=pb[0][:, 0:512],
                            scalar1=pvec[:, PV_KDEC + h:PV_KDEC + h + 1], scalar2=None, op0=ALU.mult),
                            reads=[pbKb, b_const], writes=[kTb[pp]])
                        yield
                    wg, wgb = W.take(("rg", t, h))
                    wg3 = v3(wg, KC, 512)
                    wv, wvb = W.take(("rv", t, h))
                    wv3 = v3(wv, KC, 512)
                    for c in range(4):
                        for kc in range(KC):
                            mm(2, htile3[:, kc, c * 128:(c + 1) * 128], wg3[:, kc, :], kc == 0, kc == KC - 1,
                               [wgb, htb[kc]])
                        act(Tm[3][:, :], ps[2][:, :], AF.Exp, [psb[2]], [Tmb[3]], scale=-1.0)
                        bk = c % 2
                        for kc in range(KC):
                            mm(bk, htile3[:, kc, c * 128:(c + 1) * 128], wv3[:, kc, :], kc == 0, kc == KC - 1,
                               [wvb, htb[kc]])
                        yield
                        act(Tm[3][:, :], Tm[3][:, :], AF.Ln, [Tmb[3]], [Tmb[3]], bias=onec[:, 0:1])
                        act(v3_[:, c, :], ps[bk][:, :], AF.Copy, [psb[bk]], [vbb[pp][c]])
                        act(Tm[3][:, :], Tm[3][:, :], AF.Exp, [Tmb[3]], [Tmb[3]], scale=-1.0)
                        P.op("dve", lambda e, c=c: e.tensor_tensor(out=sg3_[:, c, :], in0=ps[2][:, :], in1=Tm[3][:, :],
                                                                    op=ALU.mult),
                             reads=[psb[2], Tmb[3]], writes=[sgbb[pp][c]])
                        yield

                it_cnt = [0]

                def phase_ii(t, h):
                    pp = h % 2
                    qr3 = qr[pp][:, :].rearrange("p (c s) -> p c s", c=2)
                    qd3 = qd[pp][:, :].rearrange("p (c s) -> p c s", c=2)
                    kr3 = kr[pp][:, :].rearrange("p (c s) -> p c s", c=2)
                    v3_ = v_sb[pp][:, :].rearrange("p (c e) -> p c e", c=4)
                    sg3_ = sg_sb[pp][:, :].rearrange("p (c e) -> p c e", c=4)
                    if t > 0:
                        for dc in range(2):
                            act(stbf3[:, dc, :], state3[:, h * 2 + dc, :], AF.Copy, [stb[h * 2 + dc]], [stbb[dc]])

                    def tail_norm(c, col):
                        act(junk[:, :], ps[4][:, :], AF.Square, [psb[4]], [junkb, ssqb],
                            accum_out=ssq[:, col:col + 1])
                        act(ssq[:, col:col + 1], ssq[:, col:col + 1], AF.Ln, [ssqb, b_const], [ssqb],
                            scale=1.0 / 512.0, bias=epsc[:, 0:1])
                        act(ssq[:, col:col + 1], ssq[:, col:col + 1], AF.Exp, [ssqb], [ssqb], scale=-0.5)
                        r_, rb_ = ret_sb[c % 2], retb[c % 2]
                        P.op("dve", lambda e: e.scalar_tensor_tensor(
                            out=r_[:, :], in0=ps[4][:, :], scalar=ssq[:, col:col + 1], in1=sg3_[:, c, :],
                            op0=ALU.mult, op1=ALU.mult),
                            reads=[psb[4], ssqb, sgbb[pp][c]], writes=[rb_])

                    def tail_xpose(c):
                        cs = slice(c * 128, (c + 1) * 128)
                        r_, rb_ = ret_sb[c % 2], retb[c % 2]
                        for ec in range(4):
                            P.op("pe", lambda e, ec=ec: e.transpose(
                                pb[0][:, 512 + ec * 128:512 + (ec + 1) * 128], r_[:, ec * 128:(ec + 1) * 128], identb),
                                reads=[rb_, b_const], writes=[pbRb], signal=(ec == 3))
                        act(RT3[:, h * 4:(h + 1) * 4, cs], pb[0][:, 512:1024].rearrange("p (a b) -> p a b", a=4), AF.Copy,
                            [pbRb], [rtb[h * 4 + k] for k in range(4)])

                    for c in range(4):
                        gchunk = t * 4 + c
                        cs = slice(c * 128, (c + 1) * 128)
                        first = gchunk == 0
                        for dc in range(2):
                            mm(3, kr3[:, dc, cs], qr3[:, dc, cs], dc == 0, dc == 1, [krb[pp], qrb[pp]], cols=slice(0, 128))
                        if gchunk < 15:
                            for dc in range(2):
                                mm(5 + dc, kT[pp][:, c * 256 + dc * 128:c * 256 + (dc + 1) * 128], v3_[:, c, :], True, True,
                                   [kTb[pp], vbb[pp][c]])
                        sd, sdb = scD[c % 2], scDb[c % 2]
                        P.op("dve", lambda e, sd=sd: e.tensor_tensor(
                            out=sd[:, :], in0=ps[3][:, 0:128], in1=cret[:, CR_DEC + h * 128:CR_DEC + (h + 1) * 128],
                            op=ALU.mult), reads=[psb[3], cretb], writes=[sdb])
                        yield
                        mm(4, sd[:, :], v3_[:, c, :], True, first, [sdb, vbb[pp][c]], sig=True)
                        if not first:
                            for dc in range(2):
                                mm(4, qd3[:, dc, cs], stbf3[:, dc, :], False, dc == 1, [qdb[pp], stbb[dc]], sig=True)
                        if c > 0:
                            tail_xpose(c - 1)
                        yield
                        if gchunk < 15:
                            for dc in range(2):
                                a = h * 2 + dc
                                if first:
                                    P.op("dve", lambda e, a=a, dc=dc: e.tensor_copy(out=state3[:, a, :], in_=ps[5 + dc][:, :]),
                                         reads=[psb[5 + dc]], writes=[stb[a]])
                                else:
                                    P.op("dve", lambda e, a=a, dc=dc: e.scalar_tensor_tensor(
                                        out=state3[:, a, :], in0=state3[:, a, :], scalar=float(GAMMA[h] ** 128),
                                        in1=ps[5 + dc][:, :], op0=ALU.mult, op1=ALU.add),
                                        reads=[psb[5 + dc], stb[a]], writes=[stb[a]])
                                if c < 3:
                                    act(stbf3[:, dc, :], state3[:, a, :], AF.Copy, [stb[a]], [stbb[dc]])
                        tail_norm(c, it_cnt[0] % 8)
                        it_cnt[0] += 1
                        yield
                    tail_xpose(3)
                    yield

                def run_both(g_main, g_fill):
                    a_done = b_done = False
                    k = 0
                    while not (a_done and b_done):
                        if not a_done:
                            try:
                                next(g_main)
                            except StopIteration:
                                a_done = True
                        for _ in range(2 if k % 2 == 0 else 1):
                            if not b_done:
                                try:
                                    next(g_fill)
                                except StopIteration:
                                    b_done = True
                        k += 1

                def run_one(g):
                    for _ in g:
                        pass

                rotary_prep(0)
                for t in range(NT):
                    rmsnorm_to_hT(PV_GMIX, ph, tiles=[t], dst=(htile3, htb), nbufs=nbufs)
                    run_one(phase_i(t, 0))
                    for h in range(4):
                        if h + 1 < 4:
                            run_both(phase_ii(t, h), phase_i(t, h + 1))
                        else:
                            run_one(phase_ii(t, h))
                    if t + 1 < NT:
                        rotary_prep(t + 1)
                    branch_merge(("mret", t), 0, t, RT3, rtb, 16, mb_r, h3t=htile3, hbt=htb)
                P.barrier()

        if stage >= 5:
            with ExitStack() as ph:
                set_psum(ph)
                wo_sb = sb("wo_sb", [128, KC * D], BF16, ph)
                wob = Buf("wo")
                wo_sb3 = wo_sb[:, :].rearrange("p (k n) -> p k n", k=KC)
                P.dma("pool", wo_sb3, wo_d.rearrange("(k p) n -> p k n", p=128), writes=[wob])
                ypl = [sb(f"ypl{b}", [128, KC * TT], BF16, ph) for b in range(3)]
                yplb = [Buf(f"ypl{b}") for b in range(3)]
                Y = sb("Ysum", [128, KC * TT], BF16, ph)
                Y3 = Y[:, :].rearrange("p (k s) -> p k s", k=KC)
                Yb = [Buf(f"Y{k}") for k in range(KC)]
                tsum = [sb(f"tsum{i}", [128, TT], F32, ph) for i in range(2)]
                tsumb = [Buf(f"tsum{i}") for i in range(2)]
                for t in range(NT):
                    ts = slice(t * TT, (t + 1) * TT)
                    for b in range(3):
                        P.dma("sp", ypl[b][:, :].rearrange("p (k s) -> p k s", k=KC),
                              yp_d[b, :, :, ts].rearrange("k p s -> p k s"), reads=[ypdb[b][t]], writes=[yplb[b]])
                    for k in range(KC):
                        ks = slice(k * TT, (k + 1) * TT)
                        tt_, ttb_ = tsum[k % 2], tsumb[k % 2]
                        P.op("dve", lambda e, ks=ks, tt_=tt_: e.tensor_tensor(out=tt_[:, :], in0=ypl[0][:, ks], in1=ypl[1][:, ks],
                                                                              op=ALU.add),
                             reads=[yplb[0], yplb[1]], writes=[ttb_])
                        P.op("dve", lambda e, ks=ks, tt_=tt_, k=k: e.tensor_tensor(out=Y3[:, k, :], in0=tt_[:, :], in1=ypl[2][:, ks],
                                                                                   op=ALU.add),
                             reads=[ttb_, yplb[2]], writes=[Yb[k]])
                    for c in range(KC):
                        po = 4 + (c % 2)
                        for k in range(KC):
                            mm(po, wo_sb3[:, k, c * 128:(c + 1) * 128], Y3[:, k, :], k == 0, k == KC - 1, [wob, Yb[k]])
                        P.op("dve", lambda e, c=c, po=po: e.tensor_tensor(out=xT3[:, c, ts], in0=ps[po][:, :], in1=xT3[:, c, ts],
                                                                          op=ALU.add),
                             reads=[psb[po], xTb[c][t]], writes=[xTb[c][t]])
                P.barrier()

        if stage >= 6:
            ffn("ffn2", PV_GFFN2)

        with ExitStack() as ph:
            set_psum(ph)
            xo = [sb(f"xo{i}", [128, D], F32, ph) for i in range(2)]
            xob = [Buf(f"xo{i}") for i in range(2)]
            out_toks = []
            for r in range(S // 128):
                t = r // 4
                xo_, xob_ = xo[r % 2], xob[r % 2]
                for half in range(2):
                    pk = 4 + half
                    for q in range(4):
                        kc = half * 4 + q
                        P.op("pe", lambda e, kc=kc, q=q, pk=pk: e.transpose(
                            ps[pk][:, q * 128:(q + 1) * 128], xT3[:, kc, r * 128:(r + 1) * 128], identf),
                            reads=[xTb[kc][t], b_const], writes=[psb[pk]], signal=(q == 3))
                    dst = xo_[:, half * 512:(half + 1) * 512]
                    if half == 0:
                        P.op("dve", lambda e, dst=dst, pk=pk: e.tensor_copy(out=dst, in_=ps[pk][:, :]),
                             reads=[psb[pk]], writes=[xob_])
                    else:
                        act(dst, ps[pk][:, :], AF.Copy, [psb[pk]], [xob_])
                out_toks.append(P.dma("sp", out_d[r * 128:(r + 1) * 128, :], xo_[:, :], reads=[xob_]))
            P.finish(out_toks)
        build_program.stats = (P.n_inst, P.n_wait)
    return nc


def _consts():
    cf32 = np.zeros((128, CF_N), dtype=np.float32)
    cf32[:, CF_ID:CF_ID + 128] = np.eye(128)
    cret = np.zeros((128, CR_N), dtype=np.float32)
    idx = np.arange(128, dtype=np.float64)
    for h in range(4):
        g = GAMMA[h]
        dist = idx[None, :] - idx[:, None]
        dec = np.where(dist >= 0, g ** np.maximum(dist, 0.0), 0.0) / 16.0
        cret[:, CR_DEC + h * 128:CR_DEC + (h + 1) * 128] = dec
        qd = g ** (idx + 1.0)
        cret[:, CR_QDEC + h * 128:CR_QDEC + (h + 1) * 128] = qd[None, :]
    cbf = np.zeros((128, CB_N), dtype=np.float32)
    cbf[:, CB_ID:CB_ID + 128] = np.eye(128)
    cbf[:, CB_O1024:CB_O1024 + 128] = 1.0 / 1024.0
    cbf[0:64, CB_BD64:CB_BD64 + 64] = 1.0 / 64.0
    cbf[64:128, CB_BD64 + 64:CB_BD64 + 128] = 1.0 / 64.0
    cbf[:, CB_O128:CB_O128 + 128] = 1.0 / 128.0
    cbf[:, CB_O256:CB_O256 + 128] = 1.0 / 256.0
    cbf[:, CB_ONE:CB_ONE + 128] = 1.0
    cbf[:, CB_TRI:CB_TRI + 128] = (idx[None, :] >= idx[:, None]).astype(np.float32)
    return cf32, cret, cbf.astype(ml_dtypes.bfloat16)


def _pvec(inp):
    pv = np.zeros((128, 64), dtype=np.float32)

    def fm(v):
        return np.ascontiguousarray(np.asarray(v, dtype=np.float32).reshape(-1, 128).T)

    pv[:, PV_GFFN1:PV_GFFN1 + 8] = fm(inp["g_ffn1"][0])
    pv[:, PV_GMIX:PV_GMIX + 8] = fm(inp["g_mix"][0])
    pv[:, PV_GFFN2:PV_GFFN2 + 8] = fm(inp["g_ffn2"][0])
    pv[:, PV_GMEM:PV_GMEM + 8] = fm(inp["g_mem"][0])
    pv[:, PV_GDQ] = np.tile(np.asarray(inp["g_diff_q"][0], dtype=np.float32), 2)
    pv[:, PV_GDK] = np.tile(np.asarray(inp["g_diff_k"][0], dtype=np.float32), 2)
    pv[:, PV_GDO] = np.asarray(inp["g_diff_out"][0], dtype=np.float32)
    pv[:, PV_GMQ:PV_GMQ + 2] = fm(inp["g_mem_q"][0])
    pv[:, PV_GMK:PV_GMK + 2] = fm(inp["g_mem_k"][0])
    idx = np.arange(128, dtype=np.float64)
    pv[:, PV_INV] = (10000.0 ** (-idx / 128.0)).astype(np.float32)
    for h in range(4):
        pv[:, PV_KDEC + h] = (GAMMA[h] ** (127.0 - idx) / 16.0).astype(np.float32)
    return pv


_SHARED_KEYS = ("w_ffn1_in", "w_ffn1_out", "w_ffn2_in", "w_ffn2_out", "w_in", "w_mem_kv",
                "w_br_ret", "w_br_diff", "w_br_mem", "w_o")


def make_in_maps(inp, ncores=NCORES):
    cf32, cret, cbf = _consts()
    shared = {k: np.ascontiguousarray(inp[k][0]) for k in _SHARED_KEYS}
    shared["cret"] = cret
    shared["pvec"] = _pvec(inp)
    shared["cf32"] = cf32
    shared["cbf"] = cbf
    shared["lamv"] = np.ascontiguousarray(np.concatenate(
        [inp["lam_q1"][0], inp["lam_k1"][0], inp["lam_q2"][0], inp["lam_k2"][0]]).astype(np.float32)[None, :])
    maps = []
    for c in range(ncores):
        m = dict(shared)
        m["x"] = np.ascontiguousarray(inp["x"][c])
        m["mem"] = np.ascontiguousarray(inp["mem"][c])
        m["positions"] = np.ascontiguousarray(inp["positions"][c][None, :].astype(np.int32))
        maps.append(m)
    return maps


_NC_CACHE = {}


def kernel(**inputs):
    inp = {k: np.asarray(v) for k, v in inputs.items()}
    if "nc" not in _NC_CACHE:
        _NC_CACHE["nc"] = build_program()
    nc = _NC_CACHE["nc"]
    in_maps = make_in_maps(inp)
    res = run_bass_kernel_spmd(nc, in_maps, core_ids=list(range(NCORES)))
    out = np.stack([np.asarray(r["out"], dtype=np.float32) for r in res.results], axis=0)
    return out
```

```python
import math
from contextlib import ExitStack

import numpy as np
import ml_dtypes

import concourse.bass as bass
import concourse.mybir as mybir
from concourse.bass_utils import run_bass_kernel_spmd

F32 = mybir.dt.float32
BF16 = mybir.dt.bfloat16
I32 = mybir.dt.int32
AF = mybir.ActivationFunctionType
ALU = mybir.AluOpType

S = 2048
D = 1024
KC = D // 128
DFF = 2816
NFF = DFF // 128
TT = 512
NT = S // TT
EPS = 1e-6
NCORES = 8
STAGE = 99


class Buf:
    __slots__ = ("name", "last_w", "readers")

    def __init__(self, name):
        self.name = name
        self.last_w = None
        self.readers = []


ENG_NAMES = ("pe", "act", "dve", "pool", "sp")
NDMA_SEM = 20


class Prog:
    def __init__(self, nc, es, same_engine_sync=True):
        self.nc = nc
        self.same_engine_sync = same_engine_sync
        self.fuse_waits = True
        self.engs = {"pe": nc.tensor, "act": nc.scalar, "dve": nc.vector,
                     "pool": nc.gpsimd, "sp": nc.sync}
        self.eid = {n: i for i, n in enumerate(ENG_NAMES)}
        self.sems = []
        for n in ENG_NAMES:
            self.sems.append(es.enter_context(nc.semaphore("s_" + n)))
        self.dma_sem_ids = {}
        for q in ("sp", "pool"):
            ids = []
            for i in range(NDMA_SEM):
                ids.append(len(self.sems))
                self.sems.append(es.enter_context(nc.semaphore(f"d_{q}{i}")))
            self.dma_sem_ids[q] = ids
        self.nclk = len(self.sems)
        self.count = [0] * self.nclk
        self.clk = {n: [0] * self.nclk for n in ENG_NAMES}
        self.snap = {}
        self.dma_rr = {"sp": 0, "pool": 0}
        self.n_wait = 0
        self.n_inst = 0

    def _need(self, ename, tok):
        sid, val = tok
        c = self.clk[ename]
        if c[sid] >= val:
            return False
        if sid == self.eid.get(ename, -1):
            if ename == "pe" or not self.same_engine_sync:
                return False
        assert val <= self.count[sid], f"wait for unsignalled token {tok} (count {self.count[sid]})"
        sn = self.snap.get(tok)
        if sn is not None:
            for i in range(self.nclk):
                if sn[i] > c[i]:
                    c[i] = sn[i]
        if c[sid] < val:
            c[sid] = val
        return True

    def _wait(self, ename, tok):
        if self._need(ename, tok):
            self.engs[ename].wait_ge(self.sems[tok[0]], tok[1])
            self.n_wait += 1

    def _deps(self, reads, writes):
        deps = []
        for b in reads:
            if b.last_w is not None:
                deps.append(b.last_w)
        for b in writes:
            if b.last_w is not None:
                deps.append(b.last_w)
            deps.extend(b.readers)
        return deps

    def _record(self, tok, reads, writes):
        for b in reads:
            b.readers.append(tok)
        for b in writes:
            b.last_w = tok
            b.readers = []

    def op(self, ename, fn, reads=(), writes=(), signal=True, fuse=True):
        deps = self._deps(reads, writes)
        deps.sort(key=lambda t: -t[1])
        need = [tok for tok in deps if self._need(ename, tok)]
        fused = None
        if need and fuse and self.fuse_waits and ename in ("act", "dve"):
            fused = need.pop()
        for tok in need:
            self.engs[ename].wait_ge(self.sems[tok[0]], tok[1])
            self.n_wait += 1
        ins = fn(self.engs[ename])
        if fused is not None:
            ins._wait_ge(self.sems[fused[0]], fused[1])
        sid = self.eid[ename]
        self.n_inst += 1
        if signal:
            ins.then_inc(self.sems[sid], 1)
            self.count[sid] += 1
            tok = (sid, self.count[sid])
            self.snap[tok] = list(self.clk[ename])
        else:
            tok = (sid, self.count[sid] + 1)
        self._record(tok, reads, writes)
        return tok

    def dma(self, q, out, in_, reads=(), writes=()):
        for tok in self._deps(reads, writes):
            self._wait(q, tok)
        ids = self.dma_sem_ids[q]
        sid = ids[self.dma_rr[q] % NDMA_SEM]
        self.dma_rr[q] += 1
        if self.count[sid] > 0:
            self._wait(q, (sid, self.count[sid]))
        self.engs[q].dma_start(out=out, in_=in_).then_inc(self.sems[sid], 16)
        self.n_inst += 1
        self.count[sid] += 16
        tok = (sid, self.count[sid])
        self.snap[tok] = list(self.clk[q])
        self._record(tok, reads, writes)
        return tok

    def barrier(self):
        for ename in ENG_NAMES:
            for sid in range(self.nclk):
                if self.count[sid] > 0:
                    own = sid == self.eid[ename]
                    if own and ename in ("pe", "sp"):
                        continue
                    c = self.clk[ename]
                    if c[sid] < self.count[sid]:
                        self.engs[ename].wait_ge(self.sems[sid], self.count[sid])
                        c[sid] = self.count[sid]
                        self.n_wait += 1

    def finish(self, toks):
        for tok in toks:
            self._wait("sp", tok)


class WStream:
    HOLD = 2

    def __init__(self, P, bufs, bufobjs):
        self.P = P
        self.bufs = bufs
        self.bobj = bufobjs
        self.plan = []
        self.issued = 0
        self.taken = 0

    def add(self, tag, parts):
        self.plan.append((tag, parts))

    def _issue(self, i):
        tag, parts = self.plan[i]
        k = i % len(self.bufs)
        for dst_fn, src in parts:
            self.P.dma("pool", dst_fn(self.bufs[k]), src, writes=[self.bobj[k]])

    def take(self, tag):
        i = self.taken
        assert self.plan[i][0] == tag, (self.plan[i][0], tag)
        ahead = len(self.bufs) - self.HOLD
        while self.issued < min(len(self.plan), i + 1 + ahead):
            self._issue(self.issued)
            self.issued += 1
        self.taken += 1
        k = i % len(self.bufs)
        return self.bufs[k], self.bobj[k]


def v3(t, a, b):
    return t[:, 0:a * b].rearrange("p (a b) -> p a b", a=a)


INW = 13312
OFF_RQ, OFF_RK, OFF_RV, OFF_RG = 0, 1024, 2048, 4096
OFF_DQ, OFF_DK, OFF_DV, OFF_MQ, OFF_GT = 6144, 7168, 8192, 9216, 10240
LAM_INIT = 0.8 - 0.6 * math.exp(0.0)
GAMMA = [1.0 - 2.0 ** (-5.0 - h) for h in range(4)]
TWO_PI = 2.0 * math.pi
CW1 = 6.28125
CW2 = TWO_PI - CW1
PI_SAFE = 3.1415925

PV_GFFN1, PV_GMIX, PV_GFFN2, PV_GMEM = 0, 8, 16, 24
PV_GDQ, PV_GDK, PV_GDO, PV_GMQ, PV_GMK, PV_INV, PV_KDEC = 32, 33, 34, 35, 37, 39, 40
CF_ID, CF_N = 0, 128
CR_DEC, CR_QDEC, CR_N = 0, 512, 1024
CB_ID, CB_O1024, CB_BD64, CB_O128, CB_O256, CB_ONE, CB_TRI, CB_N = 0, 128, 256, 384, 512, 640, 768, 896


def build_program(stage=STAGE, same_engine_sync=True, debug=False):
    nc = bass.Bass("TRN2", target_bir_lowering=False)
    es = ExitStack()
    with es:
        def din(name, shape, dt=F32):
            return nc.dram_tensor(name, list(shape), dt, kind="ExternalInput").ap()

        def dscratch(name, shape, dt):
            kind = "ExternalOutput" if debug else "Internal"
            return nc.dram_tensor(name, list(shape), dt, kind=kind).ap()

        x_d = din("x", [S, D])
        mem_d = din("mem", [256, D])
        pos_d = din("positions", [1, S], I32)
        w1a_d = din("w_ffn1_in", [D, 2 * DFF])
        w1b_d = din("w_ffn1_out", [DFF, D])
        w2a_d = din("w_ffn2_in", [D, 2 * DFF])
        w2b_d = din("w_ffn2_out", [DFF, D])
        win_d = din("w_in", [D, INW])
        wkv_d = din("w_mem_kv", [D, 2048])
        wbr_d = din("w_br_ret", [2048, D])
        wbd_d = din("w_br_diff", [D, D])
        wbm_d = din("w_br_mem", [D, D])
        wo_d = din("w_o", [D, D])
        pvec_d = din("pvec", [128, 64])
        lamv_d = din("lamv", [1, 256])
        cf32_d = din("cf32", [128, CF_N])
        cret_d = din("cret", [128, CR_N])
        cbf_d = din("cbf", [128, CB_N], BF16)
        out_d = nc.dram_tensor("out", [S, D], F32, kind="ExternalOutput").ap()
        yp_d = dscratch("yp", [3, KC, 128, S], BF16)
        ypdb = [[Buf(f"ypd{b}_{t}") for t in range(4)] for b in range(3)]

        win3 = win_d.rearrange("(k p) n -> p k n", p=128)

        P = Prog(nc, es, same_engine_sync=same_engine_sync)

        def sb(name, shape, dt, st=es):
            return st.enter_context(nc.sbuf_tensor(name, list(shape), dt))

        xT = sb("xT", [128, KC * S], F32)
        hT = sb("hT", [128, KC * S], BF16)
        pvec = sb("pvec_sb", [128, 64], F32)
        cf32 = sb("cf32_sb", [128, CF_N], F32)
        cbf = sb("cbf_sb", [128, CB_N], BF16)
        epsc = sb("epsc", [128, 1], F32)
        onec = sb("onec", [128, 1], F32)
        NWB = 4
        wbufs = [sb(f"wbuf{i}", [128, 4096], BF16) for i in range(NWB)]
        wbobj = [Buf(f"wbuf{i}") for i in range(NWB)]
        W = WStream(P, wbufs, wbobj)

        xT3 = xT[:, :].rearrange("p (k s) -> p k s", k=KC)
        hT3 = hT[:, :].rearrange("p (k s) -> p k s", k=KC)
        xTb = [[Buf(f"xT{k}_{t}") for t in range(NT)] for k in range(KC)]
        hTb = [[Buf(f"hT{k}_{t}") for t in range(NT)] for k in range(KC)]
        b_const = Buf("const")

        identf = cf32[:, CF_ID:CF_ID + 128]
        identb = cbf[:, CB_ID:CB_ID + 128]
        ones_d = cbf[:, CB_O1024:CB_O1024 + 128]
        bd64 = cbf[:, CB_BD64:CB_BD64 + 128]
        ones128 = cbf[:, CB_O128:CB_O128 + 128]
        ones256 = cbf[:, CB_O256:CB_O256 + 128]
        ones1 = cbf[:, CB_ONE:CB_ONE + 128]
        tri = cbf[:, CB_TRI:CB_TRI + 128]

        ps, psb, pb, pbb, pw, pwb = [], [], [], [], [], []
        psum_ctr = [0]

        def set_psum(ph, n_single=6, n_bf=2, n_wide=0):
            k = psum_ctr[0]
            psum_ctr[0] += 1
            ps[:] = [ph.enter_context(nc.psum_tensor(f"ps{k}_{i}", [128, 512], F32)) for i in range(n_single)]
            psb[:] = [Buf(f"ps{i}") for i in range(n_single)]
            pb[:] = [ph.enter_context(nc.psum_tensor(f"pb{k}_{i}", [128, 1024], BF16)) for i in range(n_bf)]
            pbb[:] = [Buf(f"pb{i}") for i in range(n_bf)]
            pw[:] = [ph.enter_context(nc.psum_tensor(f"pw{k}_{i}", [128, 1024], F32)) for i in range(n_wide)]
            pwb[:] = [Buf(f"pwh{i}") for i in range(2 * n_wide)]

        def mm(psi, lhsT, rhs, start, stop, reads, sig=None, cols=None):
            out = ps[psi][:, :] if cols is None else ps[psi][:, cols]
            P.op("pe", lambda e: e.matmul(out, lhsT=lhsT, rhs=rhs, start=start, stop=stop),
                 reads=reads, writes=[psb[psi]], signal=(stop if sig is None else sig))

        def mm2(out, outbuf, lhsT, rhs, start, stop, reads, sig=None):
            P.op("pe", lambda e: e.matmul(out, lhsT=lhsT, rhs=rhs, start=start, stop=stop),
                 reads=reads, writes=[outbuf], signal=(stop if sig is None else sig))

        def act(out, in_, func, reads, writes, **kw):
            P.op("act", lambda e: e.activation(out=out, in_=in_, func=func, **kw), reads=reads, writes=writes,
                 fuse=("accum_out" not in kw))

        def rstd_from_ms(psi, n, rstd_ap, rstd_buf, prange=slice(0, 128)):
            act(rstd_ap, ps[psi][prange, 0:n], AF.Ln, [psb[psi], b_const], [rstd_buf], bias=epsc[prange, 0:1])
            act(rstd_ap, rstd_ap, AF.Exp, [rstd_buf], [rstd_buf], scale=-0.5)

        def plan_ffn(tag, wa, wb):
            wa3 = wa.rearrange("(k p) n -> p k n", p=128)
            wb3 = wb.rearrange("(j p) n -> p j n", p=128)
            for b in range(NFF // 2):
                W.add((tag, "a", b), [
                    (lambda t: v3(t, KC, 512)[:, :, 0:256], wa3[:, :, b * 256:(b + 1) * 256]),
                    (lambda t: v3(t, KC, 512)[:, :, 256:512],
                     wa3[:, :, DFF + b * 256:DFF + (b + 1) * 256]),
                ])
                W.add((tag, "b", b), [
                    (lambda t: v3(t, 2, 1024), wb3[:, 2 * b:2 * b + 2, :]),
                ])

        def blk_in(c0):
            return [(lambda t: v3(t, KC, 512), win3[:, :, c0:c0 + 512])]

        def plan_merge(tag, wsrc, nk, gate_off):
            w3 = wsrc.rearrange("(k p) n -> p k n", p=128)
            for cb in range(4):
                W.add((tag, "w", cb), [(lambda t, nk=nk: v3(t, nk, 256), w3[:, :, cb * 256:(cb + 1) * 256])])
                if cb % 2 == 0:
                    W.add((tag, "g", cb // 2), blk_in(gate_off + (cb // 2) * 512))

        plan_ffn("ffn1", w1a_d, w1b_d)
        if stage >= 2:
            for g4 in range(2):
                W.add(("dq", g4), blk_in(OFF_DQ + g4 * 512))
                W.add(("dk", g4), blk_in(OFF_DK + g4 * 512))
                W.add(("dv", g4), blk_in(OFF_DV + g4 * 512))
            for t in range(NT):
                plan_merge(("mdiff", t), wbd_d, 8, OFF_GT + 1024)
        if stage >= 3:
            wkv3 = wkv_d.rearrange("(k p) n -> p k n", p=128)
            for i in range(4):
                W.add(("mkv", i), [(lambda t: v3(t, KC, 512), wkv3[:, :, i * 512:(i + 1) * 512])])
            for t in range(NT):
                for i in range(2):
                    W.add(("mq", t, i), blk_in(OFF_MQ + i * 512))
                plan_merge(("mmem", t), wbm_d, 8, OFF_GT + 2048)
        if stage >= 4:
            for t in range(NT):
                for h in range(4):
                    W.add(("rqk", t, h), [
                        (lambda tt: v3(tt, KC, 512)[:, :, 0:256], win3[:, :, OFF_RQ + h * 256:OFF_RQ + (h + 1) * 256]),
                        (lambda tt: v3(tt, KC, 512)[:, :, 256:512], win3[:, :, OFF_RK + h * 256:OFF_RK + (h + 1) * 256]),
                    ])
                    W.add(("rv", t, h), blk_in(OFF_RV + h * 512))
                    W.add(("rg", t, h), blk_in(OFF_RG + h * 512))
                plan_merge(("mret", t), wbr_d, 16, OFF_GT)
        if stage >= 6:
            plan_ffn("ffn2", w2a_d, w2b_d)

        P.dma("sp", pvec[:, :], pvec_d[:, :], writes=[b_const])
        P.dma("sp", cf32[:, :], cf32_d[:, :], writes=[b_const])
        P.dma("sp", cbf[:, :], cbf_d[:, :], writes=[b_const])
        P.op("dve", lambda e: e.memset(epsc[:, :], EPS), writes=[b_const])
        P.op("dve", lambda e: e.memset(onec[:, :], 1.0), writes=[b_const])

        with ExitStack() as ph:
            set_psum(ph)
            xin = [sb(f"xin{i}", [128, D], F32, ph) for i in range(2)]
            xinb = [Buf(f"xin{i}") for i in range(2)]
            for r in range(S // 128):
                t = r // 4
                xi, xib = xin[r % 2], xinb[r % 2]
                P.dma("sp", xi[:, :], x_d[r * 128:(r + 1) * 128, :], writes=[xib])
                for half in range(2):
                    pk = 4 + half
                    for q in range(4):
                        kc = half * 4 + q
                        P.op("pe", lambda e, kc=kc, q=q, pk=pk, xi=xi: e.transpose(
                            ps[pk][:, q * 128:(q + 1) * 128], xi[:, kc * 128:(kc + 1) * 128], identf),
                            reads=[xib, b_const], writes=[psb[pk]], signal=(q == 3))
                    dst = xT3[:, half * 4:half * 4 + 4, r * 128:(r + 1) * 128]
                    src = ps[pk][:, :].rearrange("p (a b) -> p a b", a=4)
                    wr = [xTb[half * 4 + q][t] for q in range(4)]
                    if half == 0:
                        P.op("dve", lambda e, dst=dst, src=src: e.tensor_copy(out=dst, in_=src),
                             reads=[psb[pk]], writes=wr)
                    else:
                        act(dst, src, AF.Copy, [psb[pk]], wr)
            P.barrier()

        ph_names = []
        def rmsnorm_to_hT(gcol0, ph, tiles=None, dst=None, nbufs=None):
            key = f"{gcol0}_{len(ph_names)}"
            ph_names.append(key)
            if nbufs is None:
                sq = [sb(f"nsq{i}_{key}", [128, TT], BF16, ph) for i in range(2)]
                sqb = [Buf(f"nsq{i}") for i in range(2)]
                rstd = sb(f"nrstd_{key}", [128, TT], F32, ph)
                rstdb = Buf("nrstd")
            else:
                sq, sqb, rstd, rstdb = nbufs
            for t in (range(NT) if tiles is None else tiles):
                ts = slice(t * TT, (t + 1) * TT)
                for kc in range(KC):
                    s_, sb_ = sq[kc % 2], sqb[kc % 2]
                    act(s_[:, :], xT3[:, kc, ts], AF.Square, [xTb[kc][t]], [sb_])
                    mm(5, ones_d, s_[:, :], kc == 0, kc == KC - 1, [sb_, b_const], sig=True)
                rstd_from_ms(5, TT, rstd[:, :], rstdb)
                for kc in range(KC):
                    o_ap = hT3[:, kc, ts] if dst is None else dst[0][:, kc, :]
                    o_b = hTb[kc][t] if dst is None else dst[1][kc]
                    P.op("dve", lambda e, kc=kc, o_ap=o_ap: e.scalar_tensor_tensor(
                        out=o_ap, in0=xT3[:, kc, ts], scalar=pvec[:, gcol0 + kc:gcol0 + kc + 1],
                        in1=rstd[:, :], op0=ALU.mult, op1=ALU.mult),
                        reads=[xTb[kc][t], rstdb, b_const], writes=[o_b])

        def ffn(tag, gcol0):
            with ExitStack() as ph:
                set_psum(ph, 8, 0, 0)
                rmsnorm_to_hT(gcol0, ph)
                sg = [sb(f"sg{i}_{tag}", [128, TT], F32, ph) for i in range(2)]
                sgb = [Buf(f"sg{i}") for i in range(2)]
                uu = [sb(f"uu{i}_{tag}", [128, TT], BF16, ph) for i in range(4)]
                uub = [Buf(f"uu{i}") for i in range(4)]
                it = 0
                W.HOLD = 3
                pend = []

                def out_proj(wb3, wbb, ub, t):
                    ts = slice(t * TT, (t + 1) * TT)
                    for c in range(KC):
                        po = 4 + (c % 4)
                        for j in range(2):
                            mm(po, wb3[:, j, c * 128:(c + 1) * 128], ub[j][0][:, :], j == 0, j == 1,
                               [wbb, ub[j][1]])
                        P.op("dve", lambda e, c=c, po=po: e.scalar_tensor_tensor(
                            out=xT3[:, c, ts], in0=ps[po][:, :], scalar=0.5, in1=xT3[:, c, ts],
                            op0=ALU.mult, op1=ALU.add),
                            reads=[psb[po], xTb[c][t]], writes=[xTb[c][t]])

                for b in range(NFF // 2):
                    wa, wab = W.take((tag, "a", b))
                    wbt, wbb = W.take((tag, "b", b))
                    wa3 = v3(wa, KC, 512)
                    wb3 = v3(wbt, 2, 1024)
                    for t in range(NT):
                        ts = slice(t * TT, (t + 1) * TT)
                        ub = []
                        for j in range(2):
                            pg, pu = (it % 2) * 2, (it % 2) * 2 + 1
                            for kc in range(KC):
                                mm(pg, wa3[:, kc, j * 128:(j + 1) * 128], hT3[:, kc, ts], kc == 0, kc == KC - 1,
                                   [wab, hTb[kc][t]])
                            for kc in range(KC):
                                mm(pu, wa3[:, kc, 256 + j * 128:256 + (j + 1) * 128], hT3[:, kc, ts],
                                   kc == 0, kc == KC - 1, [wab, hTb[kc][t]])
                            s_, sb_ = sg[it % 2], sgb[it % 2]
                            u_, ub_ = uu[it % 4], uub[it % 4]
                            act(s_[:, :], ps[pg][:, :], AF.Silu, [psb[pg]], [sb_])
                            P.op("dve", lambda e, s_=s_, u_=u_, pu=pu: e.tensor_tensor(
                                out=u_[:, :], in0=ps[pu][:, :], in1=s_[:, :], op=ALU.mult),
                                reads=[psb[pu], sb_], writes=[ub_])
                            ub.append((u_, ub_))
                            it += 1
                        if pend:
                            out_proj(*pend.pop(0))
                        pend.append((wb3, wbb, ub, t))
                out_proj(*pend.pop(0))
                W.HOLD = 2
                P.barrier()

        def branch_merge(tag, bi, t, src3, src_bufs, nk, ph_bufs, h3t=None, hbt=None):
            gsb, gsbb, ypt, yptb = ph_bufs
            ts = slice(t * TT, (t + 1) * TT)
            if h3t is None:
                h3t = hT3[:, :, ts]
                hbt = [hTb[kc][t] for kc in range(KC)]
            wg3 = None
            for cb in range(4):
                wt, wtb = W.take((tag, "w", cb))
                w3 = v3(wt, nk, 256)
                if cb % 2 == 0:
                    wg, wgb = W.take((tag, "g", cb // 2))
                    wg3 = v3(wg, KC, 512)
                for cc in range(2):
                    c = cb * 2 + cc
                    pz, pg = (c % 2) * 2, (c % 2) * 2 + 1
                    for k in range(nk):
                        mm(pz, w3[:, k, cc * 128:(cc + 1) * 128], src3[:, k, :], k == 0, k == nk - 1,
                           [wtb, src_bufs[k]])
                    gc = (c % 4) * 128
                    for kc in range(KC):
                        mm(pg, wg3[:, kc, gc:gc + 128], h3t[:, kc, :], kc == 0, kc == KC - 1,
                           [wgb, hbt[kc]])
                    g_, gb_ = gsb[c % 2], gsbb[c % 2]
                    act(g_[:, :], ps[pg][:, :], AF.Sigmoid, [psb[pg]], [gb_])
                    P.op("dve", lambda e, c=c, pz=pz, g_=g_: e.tensor_tensor(
                        out=ypt[:, c * TT:(c + 1) * TT], in0=ps[pz][:, :], in1=g_[:, :], op=ALU.mult),
                        reads=[psb[pz], gb_], writes=[yptb])
            P.dma("sp", yp_d[bi, :, :, ts].rearrange("k p s -> p k s"),
                  ypt[:, 0:KC * TT].rearrange("p (k s) -> p k s", k=KC), reads=[yptb], writes=[ypdb[bi][t]])

        def merge_bufs(ph, nm):
            gsb = [sb(f"gsb{i}_{nm}", [128, TT], F32, ph) for i in range(2)]
            gsbb = [Buf(f"gsb{i}") for i in range(2)]
            ypt = sb(f"ypt_{nm}", [128, KC * TT], BF16, ph)
            return gsb, gsbb, ypt, Buf("ypt")

        if stage >= 1:
            ffn("ffn1", PV_GFFN1)

        if stage >= 2:
            with ExitStack() as ph:
                set_psum(ph)
                rmsnorm_to_hT(PV_GMIX, ph)
                P.barrier()

            with ExitStack() as ph:
                DIF = sb("DIF", [128, 8 * S], BF16, ph)
                DIF3 = DIF[:, :].rearrange("p (h s) -> p h s", h=8)
                difb = [[Buf(f"dif{h}_{t}") for t in range(NT)] for h in range(8)]
                ph2 = ExitStack()
                dq0 = sb("dq0", [128, S], BF16, ph2)
                dq1 = sb("dq1", [128, S], BF16, ph2)
                dkn = sb("dkn", [128, S], BF16, ph2)
                dvT = sb("dvT", [128, 16 * 128], BF16, ph2)
                dqb = [Buf(f"dq_{t}") for t in range(NT)]
                dkb = [Buf(f"dk_{t}") for t in range(NT)]
                dvb = [Buf(f"dv_{t}") for t in range(NT)]
                set_psum(ph2, 4, 0, 2)
                sqd = [sb(f"sqd{i}", [128, TT], BF16, ph2) for i in range(2)]
                sqdb = [Buf(f"sqd{i}") for i in range(2)]
                rsd = [sb(f"rsd{i}", [128, TT], F32, ph2) for i in range(2)]
                rsdb = [Buf(f"rsd{i}") for i in range(2)]
                ET = [sb(f"ET{i}", [128, 2 * TT], BF16, ph2) for i in range(2)]
                ETb = [Buf(f"ET{i}") for i in range(2)]
                Esum = sb("Esum", [128, TT], F32, ph2)
                Esumb = Buf("Esum")
                rl1 = sb("rl1", [128, TT], F32, ph2)
                rl1b = Buf("rl1")
                rl0 = sb("rl0", [128, TT], F32, ph2)
                rl0b = Buf("rl0")
                rse = sb("rse", [128, TT], F32, ph2)
                rseb = Buf("rse")
                sqe = sb("sqe", [128, TT], BF16, ph2)
                sqeb = Buf("sqe")
                Ehi = sb("Ehi", [128, TT], BF16, ph2)
                Elo = sb("Elo", [128, TT], BF16, ph2)
                Ehib, Elob = Buf("Ehi"), Buf("Elo")
                Ocp = sb("Ocp", [128, 2 * TT], F32, ph2)
                Ocpb = Buf("Ocp")
                lamv = sb("lamv_sb", [128, 256], F32, ph2)
                lamt = sb("lamt", [128, 64], F32, ph2)
                lcol = sb("lcol", [128, 4], F32, ph2)
                lamb = Buf("lam")

                P.dma("sp", lamv[:, :], lamv_d[0:1, :].broadcast_to([128, 256]), writes=[lamb])
                for i in range(2):
                    P.op("dve", lambda e, i=i: e.tensor_tensor(
                        out=lamt[:, :], in0=lamv[:, i * 128:i * 128 + 64], in1=lamv[:, i * 128 + 64:i * 128 + 128],
                        op=ALU.mult), reads=[lamb], writes=[lamb])
                    P.op("dve", lambda e, i=i: e.reduce_sum(out=lcol[:, i:i + 1], in_=lamt[:, :],
                                                            axis=mybir.AxisListType.X),
                         reads=[lamb], writes=[lamb])
                act(lcol[:, 0:2], lcol[:, 0:2], AF.Exp, [lamb], [lamb])
                P.op("dve", lambda e: e.tensor_tensor(out=lcol[:, 2:3], in0=lcol[:, 1:2], in1=lcol[:, 0:1],
                                                      op=ALU.subtract), reads=[lamb], writes=[lamb])
                P.op("dve", lambda e: e.tensor_scalar(out=lcol[:, 2:3], in0=lcol[:, 2:3], scalar1=-LAM_INIT,
                                                      scalar2=None, op0=ALU.add), reads=[lamb], writes=[lamb])
                P.op("dve", lambda e: e.tensor_scalar(out=lcol[:, 3:4], in0=pvec[:, PV_GDO:PV_GDO + 1],
                                                      scalar1=1.0 - LAM_INIT, scalar2=None, op0=ALU.mult),
                     reads=[lamb, b_const], writes=[lamb])
                nlam = lcol[:, 2:3]
                gdo = lcol[:, 3:4]
                P.op("dve", lambda e: e.memset(dq0[64:128, :], 0.0), writes=dqb)
                P.op("dve", lambda e: e.memset(dq1[0:64, :], 0.0), writes=dqb)

                W.HOLD = 4
                wq = wk = wv = None
                pending = []

                def flush_pending():
                    while pending:
                        pending.pop(0)()

                def pop_pending(n):
                    for _ in range(n):
                        if pending:
                            pending.pop(0)()

                def pwh(i):
                    return pw[i // 2][:, (i % 2) * 512:(i % 2 + 1) * 512]

                for h in range(8):
                    hl = h % 4
                    if hl == 0:
                        wq, wqb = W.take(("dq", h // 4))
                        wk, wkb = W.take(("dk", h // 4))
                        wv, wvb = W.take(("dv", h // 4))
                    wq3, wk3, wv3 = v3(wq, KC, 512), v3(wk, KC, 512), v3(wv, KC, 512)
                    groups = [(t, which) for t in range(NT) for which in (0, 1)]

                    def stageA(g):
                        t, which = groups[g]
                        ts = slice(t * TT, (t + 1) * TT)
                        w3, wb_ = (wq3, wqb) if which == 0 else (wk3, wkb)
                        i = g % 4
                        for kc in range(KC):
                            mm2(pwh(i), pwb[i], w3[:, kc, hl * 128:(hl + 1) * 128], hT3[:, kc, ts], kc == 0, kc == KC - 1,
                                [wb_, hTb[kc][t]])

                    def stageB(g):
                        t, which = groups[g]
                        ts = slice(t * TT, (t + 1) * TT)
                        i, j = g % 4, g % 2
                        act(sqd[j][:, :], pwh(i), AF.Square, [pwb[i]], [sqdb[j]])
                        mm(3, bd64, sqd[j][:, :], True, True, [sqdb[j], b_const])
                        rstd_from_ms(3, TT, rsd[j][:, :], rsdb[j])
                        if which == 0:
                            for c, dq in enumerate((dq0, dq1)):
                                pr = slice(c * 64, (c + 1) * 64)
                                P.op("dve", lambda e, dq=dq, pr=pr: e.scalar_tensor_tensor(
                                    out=dq[pr, ts], in0=pwh(i)[pr, :], scalar=pvec[pr, PV_GDQ:PV_GDQ + 1],
                                    in1=rsd[j][pr, :], op0=ALU.mult, op1=ALU.mult),
                                    reads=[pwb[i], rsdb[j], b_const], writes=[dqb[t]])
                        else:
                            P.op("dve", lambda e: e.scalar_tensor_tensor(
                                out=dkn[:, ts], in0=pwh(i), scalar=pvec[:, PV_GDK:PV_GDK + 1],
                                in1=rsd[j][:, :], op0=ALU.mult, op1=ALU.mult),
                                reads=[pwb[i], rsdb[j], b_const], writes=[dkb[t]])

                    def stageV(t):
                        for c4 in range(4):
                            tok = slice(t * TT + c4 * 128, t * TT + (c4 + 1) * 128)
                            for kc in range(KC):
                                mm(t % 2, hT3[:, kc, tok], wv3[:, kc, hl * 128:(hl + 1) * 128], kc == 0, kc == KC - 1,
                                   [wvb, hTb[kc][t]], cols=slice(c4 * 128, (c4 + 1) * 128), sig=(kc == KC - 1))
                        act(dvT[:, t * 512:(t + 1) * 512], ps[t % 2][:, :], AF.Copy, [psb[t % 2]], [dvb[t]])

                    for g in range(8 + 2):
                        if g < 8:
                            stageA(g)
                        if g >= 1:
                            pop_pending(2)
                        if g >= 2:
                            stageB(g - 2)
                        if g % 2 == 1 and g < 8:
                            stageV(g // 2)

                    for qt in range(NT):
                        nkt = 4 * qt + 4
                        ts = slice(qt * TT, (qt + 1) * TT)

                        def emitS(kt):
                            w = kt % 2
                            j0 = max(0, kt - 4 * qt) * 128
                            qs = slice(qt * TT + j0, (qt + 1) * TT)
                            for c, dq in enumerate((dq0, dq1)):
                                mm2(pw[w][:, c * 512 + j0:(c + 1) * 512], pwb[2 * w + c], dkn[:, kt * 128:(kt + 1) * 128],
                                    dq[:, qs], True, True, [dkb[kt // 4], dqb[qt]], sig=(c == 1))

                        emitS(0)
                        for kt in range(nkt):
                            if kt + 1 < nkt:
                                emitS(kt + 1)
                            if kt >= 1:
                                pop_pending(2 if len(pending) > 6 else 1)
                            w = kt % 2
                            j0 = max(0, kt - 4 * qt) * 128
                            diag = kt >= 4 * qt
                            E, Eb = ET[w], ETb[w]
                            s3 = pw[w][:, :].rearrange("p (c s) -> p c s", c=2)[:, :, j0:TT]
                            e3 = E[:, :].rearrange("p (c s) -> p c s", c=2)[:, :, j0:TT]
                            act(e3, s3, AF.Exp, [pwb[2 * w], pwb[2 * w + 1]], [Eb], scale=0.125)
                            if diag:
                                for c in range(2):
                                    blk = E[:, c * 512 + j0:c * 512 + j0 + 128]
                                    P.op("dve", lambda e, blk=blk: e.tensor_tensor(out=blk, in0=blk, in1=tri, op=ALU.mult),
                                         reads=[Eb, b_const], writes=[Eb])
                            e0, es0 = E[:, j0:TT], Esum[:, j0:TT]
                            if kt == 0:
                                P.op("dve", lambda e, e0=e0, es0=es0: e.tensor_copy(out=es0, in_=e0), reads=[Eb], writes=[Esumb])
                            else:
                                P.op("dve", lambda e, e0=e0, es0=es0: e.tensor_tensor(out=es0, in0=e0, in1=es0, op=ALU.add),
                                     reads=[Eb, Esumb], writes=[Esumb])
                            for c in range(2):
                                mm(c, dvT[:, kt * 128:(kt + 1) * 128], E[:, c * 512 + j0:(c + 1) * 512], kt == 0, kt == nkt - 1,
                                   [dvb[kt // 4], Eb], cols=slice(j0, TT), sig=True)
                            mm(3, ones1, E[:, 512 + j0:1024], kt == 0, kt == nkt - 1, [Eb, b_const], cols=slice(j0, TT), sig=True)
                        Es, Esb = Esum, Esumb
                        for c in range(2):
                            P.op("dve", lambda e, c=c: e.tensor_copy(out=Ocp[:, c * 512:(c + 1) * 512], in_=ps[c][:, :]),
                                 reads=[psb[c]], writes=[Ocpb])
                        act(rl1[:, :], ps[3][:, :], AF.Ln, [psb[3]], [rl1b])
                        act(rl1[:, :], rl1[:, :], AF.Exp, [rl1b], [rl1b], scale=-1.0)
                        P.op("dve", lambda e: e.tensor_tensor(out=Ocp[:, 512:1024], in0=Ocp[:, 512:1024], in1=rl1[:, :],
                                                              op=ALU.mult), reads=[Ocpb, rl1b], writes=[Ocpb])
                        P.op("dve", lambda e: e.tensor_copy(out=Ehi[:, :], in_=Esum[:, :]), reads=[Esumb], writes=[Ehib])
                        P.op("dve", lambda e: e.tensor_tensor(out=Elo[:, :], in0=Esum[:, :], in1=Ehi[:, :], op=ALU.subtract),
                             reads=[Esumb, Ehib], writes=[Elob])

                        def f1():
                            mm(2, ones1, Ehi[:, :], True, False, [Ehib, b_const], sig=False)
                            mm(2, ones1, Elo[:, :], False, True, [Elob, b_const], sig=True)

                        def f2():
                            act(rl0[:, :], ps[2][:, :], AF.Ln, [psb[2]], [rl0b])
                            act(rl0[:, :], rl0[:, :], AF.Exp, [rl0b], [rl0b], scale=-1.0)

                        def f2b():
                            P.op("dve", lambda e: e.tensor_tensor(
                                out=Ocp[:, 0:512], in0=Ocp[:, 0:512], in1=rl0[:, :], op=ALU.mult),
                                reads=[Ocpb, rl0b], writes=[Ocpb])

                        def f3():
                            P.op("dve", lambda e: e.scalar_tensor_tensor(
                                out=Ocp[:, 0:512], in0=Ocp[:, 512:1024], scalar=nlam, in1=Ocp[:, 0:512],
                                op0=ALU.mult, op1=ALU.add), reads=[Ocpb, lamb], writes=[Ocpb])
                            act(sqe[:, :], Ocp[:, 0:512], AF.Square, [Ocpb], [sqeb])

                        def f4():
                            mm(2, ones128, sqe[:, :], True, True, [sqeb, b_const])

                        def f5():
                            rstd_from_ms(2, TT, rse[:, :], rseb)

                        def f6(h=h, ts=ts):
                            P.op("dve", lambda e: e.scalar_tensor_tensor(
                                out=DIF3[:, h, ts], in0=Ocp[:, 0:512], scalar=gdo, in1=rse[:, :], op0=ALU.mult, op1=ALU.mult),
                                reads=[Ocpb, rseb, lamb], writes=[difb[h][ts.start // TT]])

                        pending.extend([f1, f2, f2b, f3, f4, f5, f6])
                flush_pending()
                W.HOLD = 2
                P.barrier()
                ph2.close()
                set_psum(ph)
                mb = merge_bufs(ph, "d")
                for t in range(NT):
                    ts = slice(t * TT, (t + 1) * TT)
                    branch_merge(("mdiff", t), 1, t, DIF3[:, :, ts], [difb[h][t] for h in range(8)], 8, mb)
                P.barrier()


        if stage >= 3:
            with ExitStack() as ph:
                set_psum(ph, 8, 0, 0)
                memin = sb("memin", [128, 2 * D], F32, ph)
                memT = sb("memT", [128, KC * 256], F32, ph)
                memh = sb("memh", [128, KC * 256], BF16, ph)
                mkn = sb("mkn", [128, 4 * 2 * 256], BF16, ph)
                mvT = sb("mvT", [128, 2 * 1024], BF16, ph)
                sqm = [sb(f"sqm{i}", [128, TT], BF16, ph) for i in range(2)]
                sqmb = [Buf(f"sqm{i}") for i in range(2)]
                rsm = sb("rsm", [128, TT], F32, ph)
                rsmb = Buf("rsm")
                mqn = sb("mqn", [128, 2 * TT], BF16, ph)
                mqnb = Buf("mqn")
                EM = [sb(f"EM{i}", [128, TT], BF16, ph) for i in range(2)]
                EMb = [Buf(f"EM{i}") for i in range(2)]
                rl = sb("rl", [128, TT], F32, ph)
                rlb = Buf("rl")
                MO = sb("MO", [128, KC * TT], BF16, ph)
                mob = Buf("MO")
                meminb, memTb, memhb, mknb, mvTb = Buf("memin"), Buf("memT"), Buf("memh"), Buf("mkn"), Buf("mvT")
                memin3 = memin[:, :].rearrange("p (a n) -> p a n", a=2)
                memT3 = memT[:, :].rearrange("p (k m) -> p k m", k=KC)
                memh3 = memh[:, :].rearrange("p (k m) -> p k m", k=KC)
                mkn4 = mkn[:, :].rearrange("p (h c m) -> p h c m", h=4, c=2)
                mvT3 = mvT[:, :].rearrange("p (a n) -> p a n", a=2)
                mqn3 = mqn[:, :].rearrange("p (c s) -> p c s", c=2)
                MO3 = MO[:, :].rearrange("p (k s) -> p k s", k=KC)

                P.dma("sp", memin3, mem_d.rearrange("(a p) n -> p a n", p=128), writes=[meminb])
                for mt in range(2):
                    for half in range(2):
                        pk = 4 + half
                        for q in range(4):
                            kc = half * 4 + q
                            P.op("pe", lambda e, kc=kc, q=q, pk=pk, mt=mt: e.transpose(
                                ps[pk][:, q * 128:(q + 1) * 128], memin3[:, mt, kc * 128:(kc + 1) * 128], identf),
                                reads=[meminb, b_const], writes=[psb[pk]], signal=(q == 3))
                        dst = memT3[:, half * 4:half * 4 + 4, mt * 128:(mt + 1) * 128]
                        src = ps[pk][:, :].rearrange("p (a b) -> p a b", a=4)
                        P.op("dve", lambda e, dst=dst, src=src: e.tensor_copy(out=dst, in_=src),
                             reads=[psb[pk]], writes=[memTb])
                for kc in range(KC):
                    s_, sb_ = sqm[kc % 2], sqmb[kc % 2]
                    act(s_[:, 0:256], memT3[:, kc, :], AF.Square, [memTb], [sb_])
                    mm(5, ones_d, s_[:, 0:256], kc == 0, kc == KC - 1, [sb_, b_const], sig=True, cols=slice(0, 256))
                rstd_from_ms(5, 256, rsm[:, 0:256], rsmb)
                for kc in range(KC):
                    P.op("dve", lambda e, kc=kc: e.scalar_tensor_tensor(
                        out=memh3[:, kc, :], in0=memT3[:, kc, :], scalar=pvec[:, PV_GMEM + kc:PV_GMEM + kc + 1],
                        in1=rsm[:, 0:256], op0=ALU.mult, op1=ALU.mult),
                        reads=[memTb, rsmb, b_const], writes=[memhb])
                c256 = slice(0, 256)
                for blk in range(2):
                    wk, wkb = W.take(("mkv", blk))
                    wk3 = v3(wk, KC, 512)
                    for hh in range(2):
                        hm = blk * 2 + hh
                        for dc in range(2):
                            for kc in range(KC):
                                mm(dc, wk3[:, kc, hh * 256 + dc * 128:hh * 256 + (dc + 1) * 128], memh3[:, kc, :],
                                   kc == 0, kc == KC - 1, [wkb, memhb], cols=c256)
                            act(sqm[dc][:, 0:256], ps[dc][:, 0:256], AF.Square, [psb[dc]], [sqmb[dc]])
                            mm(2, ones256, sqm[dc][:, 0:256], dc == 0, dc == 1, [sqmb[dc], b_const], sig=True, cols=c256)
                        rstd_from_ms(2, 256, rsm[:, 0:256], rsmb)
                        for dc in range(2):
                            P.op("dve", lambda e, dc=dc, hm=hm: e.scalar_tensor_tensor(
                                out=mkn4[:, hm, dc, :], in0=ps[dc][:, 0:256], scalar=pvec[:, PV_GMK + dc:PV_GMK + dc + 1],
                                in1=rsm[:, 0:256], op0=ALU.mult, op1=ALU.mult),
                                reads=[psb[dc], rsmb, b_const], writes=[mknb])
                for vb in range(2):
                    wv, wvb = W.take(("mkv", 2 + vb))
                    wv3 = v3(wv, KC, 512)
                    for mt in range(2):
                        for kc in range(KC):
                            mm(3, memh3[:, kc, mt * 128:(mt + 1) * 128], wv3[:, kc, :], kc == 0, kc == KC - 1,
                               [wvb, memhb])
                        act(mvT3[:, mt, vb * 512:(vb + 1) * 512], ps[3][:, :], AF.Copy, [psb[3]], [mvTb])

                mb = merge_bufs(ph, "m")
                mqn2 = [mqn, sb("mqn_b", [128, 2 * TT], BF16, ph)]
                mqnb2 = [mqnb, Buf("mqn_b")]
                wq3_cur = [None, None]

                def stageP(t, hm):
                    ts = slice(t * TT, (t + 1) * TT)
                    if hm % 2 == 0:
                        wq, wqb = W.take(("mq", t, hm // 2))
                        wq3_cur[0], wq3_cur[1] = v3(wq, KC, 512), wqb
                    wq3, wqb = wq3_cur
                    hh = hm % 2
                    ba = (hm % 2) * 2
                    m3 = mqn2[hm % 2][:, :].rearrange("p (c s) -> p c s", c=2)
                    for dc in range(2):
                        for kc in range(KC):
                            mm(ba + dc, wq3[:, kc, hh * 256 + dc * 128:hh * 256 + (dc + 1) * 128], hT3[:, kc, ts],
                               kc == 0, kc == KC - 1, [wqb, hTb[kc][t]])
                        act(sqm[dc][:, :], ps[ba + dc][:, :], AF.Square, [psb[ba + dc]], [sqmb[dc]])
                        mm(4, ones256, sqm[dc][:, :], dc == 0, dc == 1, [sqmb[dc], b_const], sig=True)
                    rstd_from_ms(4, TT, rsm[:, :], rsmb)
                    for dc in range(2):
                        P.op("dve", lambda e, dc=dc: e.scalar_tensor_tensor(
                            out=m3[:, dc, :], in0=ps[ba + dc][:, :], scalar=pvec[:, PV_GMQ + dc:PV_GMQ + dc + 1],
                            in1=rsm[:, :], op0=ALU.mult, op1=ALU.mult),
                            reads=[psb[ba + dc], rsmb, b_const], writes=[mqnb2[hm % 2]])

                def stageA(t, hm):
                    m3 = mqn2[hm % 2][:, :].rearrange("p (c s) -> p c s", c=2)
                    for mt in range(2):
                        for dc in range(2):
                            mm(5 + mt, mkn4[:, hm, dc, mt * 128:(mt + 1) * 128], m3[:, dc, :], dc == 0, dc == 1,
                               [mknb, mqnb2[hm % 2]])
                        act(EM[mt][:, :], ps[5 + mt][:, :], AF.Exp, [psb[5 + mt]], [EMb[mt]], scale=1.0 / 16.0)
                    for mt in range(2):
                        mm(7, ones1, EM[mt][:, :], mt == 0, mt == 1, [EMb[mt], b_const])
                    for ec in range(2):
                        for mt in range(2):
                            mm(5 + ec, mvT3[:, mt, hm * 256 + ec * 128:hm * 256 + (ec + 1) * 128], EM[mt][:, :],
                               mt == 0, mt == 1, [mvTb, EMb[mt]])
                    act(rl[:, :], ps[7][:, :], AF.Ln, [psb[7]], [rlb])
                    act(rl[:, :], rl[:, :], AF.Exp, [rlb], [rlb], scale=-1.0)
                    for ec in range(2):
                        P.op("dve", lambda e, ec=ec: e.tensor_tensor(
                            out=MO3[:, hm * 2 + ec, :], in0=ps[5 + ec][:, :], in1=rl[:, :], op=ALU.mult),
                            reads=[psb[5 + ec], rlb], writes=[mob])

                for t in range(NT):
                    stageP(t, 0)
                    for hm in range(4):
                        if hm + 1 < 4:
                            stageP(t, hm + 1)
                        stageA(t, hm)
                    branch_merge(("mmem", t), 2, t, MO3, [mob] * 8, 8, mb)
                P.barrier()

        if stage >= 4:
            with ExitStack() as ph:
                set_psum(ph, 7, 1, 0)
                pbKb, pbRb = Buf("pbK"), Buf("pbR")
                htile = hT[:, 0:4096]
                RT = hT[:, 4096:12288]
                ypt_r = hT[:, 12288:16384]
                htile3 = htile.rearrange("p (k s) -> p k s", k=KC)
                RT3 = RT.rearrange("p (k s) -> p k s", k=16)
                htb = [Buf(f"ht{k}") for k in range(KC)]
                rtb = [Buf(f"rt{k}") for k in range(16)]
                cret = sb("cret_sb", [128, CR_N], F32, ph)
                cretb = Buf("cret")
                posi = sb("posi", [128, TT], I32, ph)
                posb = Buf("posi")
                Tm = [sb(f"Tm{i}", [128, TT], F32, ph) for i in range(4)]
                Tmb = [Buf(f"Tm{i}") for i in range(4)]
                cosT = sb("cosT", [128, TT], F32, ph)
                sinT = sb("sinT", [128, TT], F32, ph)
                csb = Buf("cossin")
                qr = [sb(f"qr{i}", [128, 2 * TT], BF16, ph) for i in range(2)]
                qd = [sb(f"qd{i}", [128, 2 * TT], BF16, ph) for i in range(2)]
                kr = [sb(f"kr{i}", [128, 2 * TT], BF16, ph) for i in range(2)]
                qrb = [Buf(f"qr{i}") for i in range(2)]
                qdb = [Buf(f"qd{i}") for i in range(2)]
                krb = [Buf(f"kr{i}") for i in range(2)]
                kT = [sb(f"kT{i}", [128, 1024], BF16, ph) for i in range(2)]
                kTb = [Buf(f"kT{i}") for i in range(2)]
                v_sb = [sb(f"v_sb{i}", [128, 4 * TT], BF16, ph) for i in range(2)]
                sg_sb = [sb(f"sg_sb{i}", [128, 4 * TT], BF16, ph) for i in range(2)]
                vbb = [[Buf(f"v{i}_{c}") for c in range(4)] for i in range(2)]
                sgbb = [[Buf(f"sg{i}_{c}") for c in range(4)] for i in range(2)]
                state = sb("state", [128, 8 * TT], F32, ph)
                state3 = state[:, :].rearrange("p (a e) -> p a e", a=8)
                stb = [Buf(f"st{a}") for a in range(8)]
                stbf = sb("stbf", [128, 2 * TT], BF16, ph)
                stbf3 = stbf[:, :].rearrange("p (a e) -> p a e", a=2)
                stbb = [Buf(f"stbf{a}") for a in range(2)]
                scD = [sb(f"scD{i}", [128, 128], BF16, ph) for i in range(2)]
                scDb = [Buf(f"scD{i}") for i in range(2)]
                ssq = sb("ssq", [128, 8], F32, ph)
                ssqb = Buf("ssq")
                junk = sb("junk", [128, TT], BF16, ph)
                junkb = Buf("junk")
                ret_sb = [sb(f"ret_sb{i}", [128, TT], BF16, ph) for i in range(2)]
                retb = [Buf(f"ret_sb{i}") for i in range(2)]
                nsq = [sb(f"rnsq{i}", [128, TT], BF16, ph) for i in range(2)]
                nsqb = [Buf(f"rnsq{i}") for i in range(2)]
                nbufs = (nsq, nsqb, Tm[3], Tmb[3])
                mb_r = ([Tm[0], Tm[1]], [Tmb[0], Tmb[1]], ypt_r, Buf("ypt_r"))

                P.dma("sp", cret[:, :], cret_d[:, :], writes=[cretb])

                def rotary_prep(t):
                    ts = slice(t * TT, (t + 1) * TT)
                    P.dma("sp", posi[:, :], pos_d[0:1, ts].broadcast_to([128, TT]), writes=[posb])
                    ang, ta, tb = Tm[0], Tm[1], Tm[2]
                    ki = Tm[3][:, :].bitcast(I32)
                    P.op("dve", lambda e: e.tensor_scalar(out=ang[:, :], in0=posi[:, :], scalar1=pvec[:, PV_INV:PV_INV + 1],
                                                          scalar2=None, op0=ALU.mult),
                         reads=[posb, b_const], writes=[Tmb[0]])
                    for which, dstT in ((0, sinT), (1, cosT)):
                        if which == 1:
                            P.op("dve", lambda e: e.tensor_scalar(out=ang[:, :], in0=ang[:, :], scalar1=math.pi / 2.0,
                                                                  scalar2=None, op0=ALU.add),
                                 reads=[Tmb[0]], writes=[Tmb[0]])
                        P.op("dve", lambda e: e.tensor_scalar(out=ki, in0=ang[:, :], scalar1=1.0 / TWO_PI,
                                                              scalar2=None, op0=ALU.mult),
                             reads=[Tmb[0]], writes=[Tmb[3]])
                        P.op("dve", lambda e: e.scalar_tensor_tensor(out=ta[:, :], in0=ki, scalar=-CW1, in1=ang[:, :],
                                                                     op0=ALU.mult, op1=ALU.add),
                             reads=[Tmb[3], Tmb[0]], writes=[Tmb[1]])
                        P.op("dve", lambda e: e.scalar_tensor_tensor(out=ta[:, :], in0=ki, scalar=-CW2, in1=ta[:, :],
                                                                     op0=ALU.mult, op1=ALU.add),
                             reads=[Tmb[3], Tmb[1]], writes=[Tmb[1]])
                        P.op("dve", lambda e: e.tensor_scalar(out=tb[:, :], in0=ta[:, :], scalar1=math.pi, scalar2=None,
                                                              op0=ALU.is_gt), reads=[Tmb[1]], writes=[Tmb[2]])
                        P.op("dve", lambda e: e.scalar_tensor_tensor(out=ta[:, :], in0=tb[:, :], scalar=-TWO_PI, in1=ta[:, :],
                                                                     op0=ALU.mult, op1=ALU.add),
                             reads=[Tmb[2], Tmb[1]], writes=[Tmb[1]])
                        P.op("dve", lambda e: e.tensor_scalar(out=tb[:, :], in0=ta[:, :], scalar1=-math.pi, scalar2=None,
                                                              op0=ALU.is_lt), reads=[Tmb[1]], writes=[Tmb[2]])
                        P.op("dve", lambda e: e.scalar_tensor_tensor(out=ta[:, :], in0=tb[:, :], scalar=TWO_PI, in1=ta[:, :],
                                                                     op0=ALU.mult, op1=ALU.add),
                             reads=[Tmb[2], Tmb[1]], writes=[Tmb[1]])
                        P.op("dve", lambda e: e.tensor_scalar(out=ta[:, :], in0=ta[:, :], scalar1=-PI_SAFE, scalar2=PI_SAFE,
                                                              op0=ALU.max, op1=ALU.min), reads=[Tmb[1]], writes=[Tmb[1]])
                        act(dstT[:, :], ta[:, :], AF.Sin, [Tmb[1]], [csb])

                def phase_i(t, h):
                    pp = h % 2
                    qr3 = qr[pp][:, :].rearrange("p (c s) -> p c s", c=2)
                    qd3 = qd[pp][:, :].rearrange("p (c s) -> p c s", c=2)
                    kr3 = kr[pp][:, :].rearrange("p (c s) -> p c s", c=2)
                    v3_ = v_sb[pp][:, :].rearrange("p (c e) -> p c e", c=4)
                    sg3_ = sg_sb[pp][:, :].rearrange("p (c e) -> p c e", c=4)
                    wqk, wqkb = W.take(("rqk", t, h))
                    wqk3 = v3(wqk, KC, 512)
                    qdec_b = cret[:, CR_QDEC + h * 128:CR_QDEC + (h + 1) * 128].rearrange(
                        "p (o d) -> p o d", o=1).broadcast_to([128, 4, 128])
                    for which in range(2):
                        for dc in range(2):
                            for kc in range(KC):
                                mm(dc, wqk3[:, kc, which * 256 + dc * 128:which * 256 + (dc + 1) * 128],
                                   htile3[:, kc, :], kc == 0, kc == KC - 1, [wqkb, htb[kc]])
                            yield
                        for half in range(2):
                            c1, c2 = (cosT, sinT) if half == 0 else (sinT, cosT)
                            P.op("dve", lambda e, c1=c1: e.tensor_tensor(out=Tm[0][:, :], in0=ps[0][:, :], in1=c1[:, :],
                                                                         op=ALU.mult),
                                 reads=[psb[0], csb], writes=[Tmb[0]])
                            P.op("dve", lambda e, c2=c2: e.tensor_tensor(out=Tm[1][:, :], in0=ps[1][:, :], in1=c2[:, :],
                                                                         op=ALU.mult),
                                 reads=[psb[1], csb], writes=[Tmb[1]])
                            op_ = ALU.subtract if half == 0 else ALU.add
                            if which == 0:
                                P.op("dve", lambda e, op_=op_: e.tensor_tensor(out=Tm[2][:, :], in0=Tm[0][:, :],
                                                                               in1=Tm[1][:, :], op=op_),
                                     reads=[Tmb[0], Tmb[1]], writes=[Tmb[2]])
                                act(qr3[:, half, :], Tm[2][:, :], AF.Copy, [Tmb[2]], [qrb[pp]])
                                P.op("dve", lambda e, half=half: e.tensor_tensor(
                                    out=qd3[:, half, :].rearrange("p (c d) -> p c d", c=4),
                                    in0=Tm[2][:, :].rearrange("p (c d) -> p c d", c=4), in1=qdec_b, op=ALU.mult),
                                    reads=[Tmb[2], cretb], writes=[qdb[pp]])
                            else:
                                P.op("dve", lambda e, op_=op_, half=half: e.tensor_tensor(
                                    out=kr3[:, half, :], in0=Tm[0][:, :], in1=Tm[1][:, :], op=op_),
                                    reads=[Tmb[0], Tmb[1]], writes=[krb[pp]])
                            yield
                    for hf in range(2):
                        for i4 in range(4):
                            idx = hf * 4 + i4
                            c, dc = idx // 2, idx % 2
                            P.op("pe", lambda e, c=c, dc=dc, i4=i4: e.transpose(
                                pb[0][:, i4 * 128:(i4 + 1) * 128], kr3[:, dc, c * 128:(c + 1) * 128], identb),
                                reads=[krb[pp], b_const], writes=[pbKb], signal=(i4 == 3))
                        P.op("dve", lambda e, hf=hf: e.tensor_scalar(
                            out=kT[pp][:, hf * 512:(hf + 1) * 512], in0=pb[0][:, 0:512],
                            scalar1=pvec[:, PV_KDEC + h:PV_KDEC + h + 1], scalar2=None, op0=ALU.mult),
                            reads=[pbKb, b_const], writes=[kTb[pp]])
                        yield
                    wv, wvb = W.take(("rv", t, h))
                    wv3 = v3(wv, KC, 512)
                    for c in range(4):
                        bk = (c + 2) % 3
                        for kc in range(KC):
                            mm(bk, htile3[:, kc, c * 128:(c + 1) * 128], wv3[:, kc, :], kc == 0, kc == KC - 1,
                               [wvb, htb[kc]])
                        act(v3_[:, c, :], ps[bk][:, :], AF.Copy, [psb[bk]], [vbb[pp][c]])
                        yield
                    wg, wgb = W.take(("rg", t, h))
                    wg3 = v3(wg, KC, 512)
                    for c in range(4):
                        bk = (c + 0) % 3
                        for kc in range(KC):
                            mm(bk, htile3[:, kc, c * 128:(c + 1) * 128], wg3[:, kc, :], kc == 0, kc == KC - 1,
                               [wgb, htb[kc]])
                        act(Tm[3][:, :], ps[bk][:, :], AF.Exp, [psb[bk]], [Tmb[3]], scale=-1.0)
                        act(Tm[3][:, :], Tm[3][:, :], AF.Ln, [Tmb[3]], [Tmb[3]], bias=onec[:, 0:1])
                        act(Tm[3][:, :], Tm[3][:, :], AF.Exp, [Tmb[3]], [Tmb[3]], scale=-1.0)
                        P.op("dve", lambda e, c=c, bk=bk: e.tensor_tensor(out=sg3_[:, c, :], in0=ps[bk][:, :], in1=Tm[3][:, :],
                                                                           op=ALU.mult),
                             reads=[psb[bk], Tmb[3]], writes=[sgbb[pp][c]])
                        yield

                it_cnt = [0]

                def phase_ii(t, h):
                    pp = h % 2
                    qr3 = qr[pp][:, :].rearrange("p (c s) -> p c s", c=2)
                    qd3 = qd[pp][:, :].rearrange("p (c s) -> p c s", c=2)
                    kr3 = kr[pp][:, :].rearrange("p (c s) -> p c s", c=2)
                    v3_ = v_sb[pp][:, :].rearrange("p (c e) -> p c e", c=4)
                    sg3_ = sg_sb[pp][:, :].rearrange("p (c e) -> p c e", c=4)
                    if t > 0:
                        for dc in range(2):
                            act(stbf3[:, dc, :], state3[:, h * 2 + dc, :], AF.Copy, [stb[h * 2 + dc]], [stbb[dc]])

                    def tail_norm(c, col):
                        act(junk[:, :], ps[4][:, :], AF.Square, [psb[4]], [junkb, ssqb],
                            accum_out=ssq[:, col:col + 1])
                        act(ssq[:, col:col + 1], ssq[:, col:col + 1], AF.Ln, [ssqb, b_const], [ssqb],
                            scale=1.0 / 512.0, bias=epsc[:, 0:1])
                        act(ssq[:, col:col + 1], ssq[:, col:col + 1], AF.Exp, [ssqb], [ssqb], scale=-0.5)
                        r_, rb_ = ret_sb[c % 2], retb[c % 2]
                        P.op("dve", lambda e: e.scalar_tensor_tensor(
                            out=r_[:, :], in0=ps[4][:, :], scalar=ssq[:, col:col + 1], in1=sg3_[:, c, :],
                            op0=ALU.mult, op1=ALU.mult),
                            reads=[psb[4], ssqb, sgbb[pp][c]], writes=[rb_])

                    def tail_xpose(c):
                        cs = slice(c * 128, (c + 1) * 128)
                        r_, rb_ = ret_sb[c % 2], retb[c % 2]
                        for ec in range(4):
                            P.op("pe", lambda e, ec=ec: e.transpose(
                                pb[0][:, 512 + ec * 128:512 + (ec + 1) * 128], r_[:, ec * 128:(ec + 1) * 128], identb),
                                reads=[rb_, b_const], writes=[pbRb], signal=(ec == 3))
                        act(RT3[:, h * 4:(h + 1) * 4, cs], pb[0][:, 512:1024].rearrange("p (a b) -> p a b", a=4), AF.Copy,
                            [pbRb], [rtb[h * 4 + k] for k in range(4)])

                    for c in range(4):
                        gchunk = t * 4 + c
                        cs = slice(c * 128, (c + 1) * 128)
                        first = gchunk == 0
                        for dc in range(2):
                            mm(3, kr3[:, dc, cs], qr3[:, dc, cs], dc == 0, dc == 1, [krb[pp], qrb[pp]], cols=slice(0, 128))
                        if gchunk < 15:
                            for dc in range(2):
                                mm(5 + dc, kT[pp][:, c * 256 + dc * 128:c * 256 + (dc + 1) * 128], v3_[:, c, :], True, True,
                                   [kTb[pp], vbb[pp][c]])
                        sd, sdb = scD[c % 2], scDb[c % 2]
                        P.op("dve", lambda e, sd=sd: e.tensor_tensor(
                            out=sd[:, :], in0=ps[3][:, 0:128], in1=cret[:, CR_DEC + h * 128:CR_DEC + (h + 1) * 128],
                            op=ALU.mult), reads=[psb[3], cretb], writes=[sdb])
                        yield
                        mm(4, sd[:, :], v3_[:, c, :], True, first, [sdb, vbb[pp][c]], sig=True)
                        if not first:
                            for dc in range(2):
                                mm(4, qd3[:, dc, cs], stbf3[:, dc, :], False, dc == 1, [qdb[pp], stbb[dc]], sig=True)
                        if c > 0:
                            tail_xpose(c - 1)
                        yield
                        if gchunk < 15:
                            for dc in range(2):
                                a = h * 2 + dc
                                if first:
                                    P.op("dve", lambda e, a=a, dc=dc: e.tensor_copy(out=state3[:, a, :], in_=ps[5 + dc][:, :]),
                                         reads=[psb[5 + dc]], writes=[stb[a]])
                                else:
                                    P.op("dve", lambda e, a=a, dc=dc: e.scalar_tensor_tensor(
                                        out=state3[:, a, :], in0=state3[:, a, :], scalar=float(GAMMA[h] ** 128),
                                        in1=ps[5 + dc][:, :], op0=ALU.mult, op1=ALU.add),
                                        reads=[psb[5 + dc], stb[a]], writes=[stb[a]])
                                if c < 3:
                                    act(stbf3[:, dc, :], state3[:, a, :], AF.Copy, [stb[a]], [stbb[dc]])
                        tail_norm(c, it_cnt[0] % 8)
                        it_cnt[0] += 1
                        yield
                    tail_xpose(3)
                    yield

                def run_both(g_main, g_fill):
                    a_done = b_done = False
                    k = 0
                    while not (a_done and b_done):
                        if not a_done:
                            try:
                                next(g_main)
                            except StopIteration:
                                a_done = True
                        for _ in range(2 if k % 2 == 0 else 1):
                            if not b_done:
                                try:
                                    next(g_fill)
                                except StopIteration:
                                    b_done = True
                        k += 1

                def run_one(g):
                    for _ in g:
                        pass

                rotary_prep(0)
                for t in range(NT):
                    rmsnorm_to_hT(PV_GMIX, ph, tiles=[t], dst=(htile3, htb), nbufs=nbufs)
                    run_one(phase_i(t, 0))
                    for h in range(4):
                        if h + 1 < 4:
                            run_both(phase_ii(t, h), phase_i(t, h + 1))
                        else:
                            run_one(phase_ii(t, h))
                    if t + 1 < NT:
                        rotary_prep(t + 1)
                    branch_merge(("mret", t), 0, t, RT3, rtb, 16, mb_r, h3t=htile3, hbt=htb)
                P.barrier()

        if stage >= 5:
            with ExitStack() as ph:
                set_psum(ph)
                wo_sb = sb("wo_sb", [128, KC * D], BF16, ph)
                wob = Buf("wo")
                wo_sb3 = wo_sb[:, :].rearrange("p (k n) -> p k n", k=KC)
                P.dma("pool", wo_sb3, wo_d.rearrange("(k p) n -> p k n", p=128), writes=[wob])
                ypl = [sb(f"ypl{b}", [128, KC * TT], BF16, ph) for b in range(3)]
                yplb = [Buf(f"ypl{b}") for b in range(3)]
                Y = sb("Ysum", [128, KC * TT], BF16, ph)
                Y3 = Y[:, :].rearrange("p (k s) -> p k s", k=KC)
                Yb = [Buf(f"Y{k}") for k in range(KC)]
                tsum = [sb(f"tsum{i}", [128, TT], F32, ph) for i in range(2)]
                tsumb = [Buf(f"tsum{i}") for i in range(2)]
                for t in range(NT):
                    ts = slice(t * TT, (t + 1) * TT)
                    for b in range(3):
                        P.dma("sp", ypl[b][:, :].rearrange("p (k s) -> p k s", k=KC),
                              yp_d[b, :, :, ts].rearrange("k p s -> p k s"), reads=[ypdb[b][t]], writes=[yplb[b]])
                    for k in range(KC):
                        ks = slice(k * TT, (k + 1) * TT)
                        tt_, ttb_ = tsum[k % 2], tsumb[k % 2]
                        P.op("dve", lambda e, ks=ks, tt_=tt_: e.tensor_tensor(out=tt_[:, :], in0=ypl[0][:, ks], in1=ypl[1][:, ks],
                                                                              op=ALU.add),
                             reads=[yplb[0], yplb[1]], writes=[ttb_])
                        P.op("dve", lambda e, ks=ks, tt_=tt_, k=k: e.tensor_tensor(out=Y3[:, k, :], in0=tt_[:, :], in1=ypl[2][:, ks],
                                                                                   op=ALU.add),
                             reads=[ttb_, yplb[2]], writes=[Yb[k]])
                    for c in range(KC):
                        po = 4 + (c % 2)
                        for k in range(KC):
                            mm(po, wo_sb3[:, k, c * 128:(c + 1) * 128], Y3[:, k, :], k == 0, k == KC - 1, [wob, Yb[k]])
                        P.op("dve", lambda e, c=c, po=po: e.tensor_tensor(out=xT3[:, c, ts], in0=ps[po][:, :], in1=xT3[:, c, ts],
                                                                          op=ALU.add),
                             reads=[psb[po], xTb[c][t]], writes=[xTb[c][t]])
                P.barrier()

        if stage >= 6:
            ffn("ffn2", PV_GFFN2)

        with ExitStack() as ph:
            set_psum(ph)
            xo = [sb(f"xo{i}", [128, D], F32, ph) for i in range(2)]
            xob = [Buf(f"xo{i}") for i in range(2)]
            out_toks = []
            for r in range(S // 128):
                t = r // 4
                xo_, xob_ = xo[r % 2], xob[r % 2]
                for half in range(2):
                    pk = 4 + half
                    for q in range(4):
                        kc = half * 4 + q
                        P.op("pe", lambda e, kc=kc, q=q, pk=pk: e.transpose(
                            ps[pk][:, q * 128:(q + 1) * 128], xT3[:, kc, r * 128:(r + 1) * 128], identf),
                            reads=[xTb[kc][t], b_const], writes=[psb[pk]], signal=(q == 3))
                    dst = xo_[:, half * 512:(half + 1) * 512]
                    if half == 0:
                        P.op("dve", lambda e, dst=dst, pk=pk: e.tensor_copy(out=dst, in_=ps[pk][:, :]),
                             reads=[psb[pk]], writes=[xob_])
                    else:
                        act(dst, ps[pk][:, :], AF.Copy, [psb[pk]], [xob_])
                out_toks.append(P.dma("sp", out_d[r * 128:(r + 1) * 128, :], xo_[:, :], reads=[xob_]))
            P.finish(out_toks)
        build_program.stats = (P.n_inst, P.n_wait)
    return nc


def _consts():
    cf32 = np.zeros((128, CF_N), dtype=np.float32)
    cf32[:, CF_ID:CF_ID + 128] = np.eye(128)
    cret = np.zeros((128, CR_N), dtype=np.float32)
    idx = np.arange(128, dtype=np.float64)
    for h in range(4):
        g = GAMMA[h]
        dist = idx[None, :] - idx[:, None]
        dec = np.where(dist >= 0, g ** np.maximum(dist, 0.0), 0.0) / 16.0
        cret[:, CR_DEC + h * 128:CR_DEC + (h + 1) * 128] = dec
        qd = g ** (idx + 1.0)
        cret[:, CR_QDEC + h * 128:CR_QDEC + (h + 1) * 128] = qd[None, :]
    cbf = np.zeros((128, CB_N), dtype=np.float32)
    cbf[:, CB_ID:CB_ID + 128] = np.eye(128)
    cbf[:, CB_O1024:CB_O1024 + 128] = 1.0 / 1024.0
    cbf[0:64, CB_BD64:CB_BD64 + 64] = 1.0 / 64.0
    cbf[64:128, CB_BD64 + 64:CB_BD64 + 128] = 1.0 / 64.0
    cbf[:, CB_O128:CB_O128 + 128] = 1.0 / 128.0
    cbf[:, CB_O256:CB_O256 + 128] = 1.0 / 256.0
    cbf[:, CB_ONE:CB_ONE + 128] = 1.0
    cbf[:, CB_TRI:CB_TRI + 128] = (idx[None, :] >= idx[:, None]).astype(np.float32)
    return cf32, cret, cbf.astype(ml_dtypes.bfloat16)


def _pvec(inp):
    pv = np.zeros((128, 64), dtype=np.float32)

    def fm(v):
        return np.ascontiguousarray(np.asarray(v, dtype=np.float32).reshape(-1, 128).T)

    pv[:, PV_GFFN1:PV_GFFN1 + 8] = fm(inp["g_ffn1"][0])
    pv[:, PV_GMIX:PV_GMIX + 8] = fm(inp["g_mix"][0])
    pv[:, PV_GFFN2:PV_GFFN2 + 8] = fm(inp["g_ffn2"][0])
    pv[:, PV_GMEM:PV_GMEM + 8] = fm(inp["g_mem"][0])
    pv[:, PV_GDQ] = np.tile(np.asarray(inp["g_diff_q"][0], dtype=np.float32), 2)
    pv[:, PV_GDK] = np.tile(np.asarray(inp["g_diff_k"][0], dtype=np.float32), 2)
    pv[:, PV_GDO] = np.asarray(inp["g_diff_out"][0], dtype=np.float32)
    pv[:, PV_GMQ:PV_GMQ + 2] = fm(inp["g_mem_q"][0])
    pv[:, PV_GMK:PV_GMK + 2] = fm(inp["g_mem_k"][0])
    idx = np.arange(128, dtype=np.float64)
    pv[:, PV_INV] = (10000.0 ** (-idx / 128.0)).astype(np.float32)
    for h in range(4):
        pv[:, PV_KDEC + h] = (GAMMA[h] ** (127.0 - idx) / 16.0).astype(np.float32)
    return pv


_SHARED_KEYS = ("w_ffn1_in", "w_ffn1_out", "w_ffn2_in", "w_ffn2_out", "w_in", "w_mem_kv",
                "w_br_ret", "w_br_diff", "w_br_mem", "w_o")


def make_in_maps(inp, ncores=NCORES):
    cf32, cret, cbf = _consts()
    shared = {k: np.ascontiguousarray(inp[k][0]) for k in _SHARED_KEYS}
    shared["cret"] = cret
    shared["pvec"] = _pvec(inp)
    shared["cf32"] = cf32
    shared["cbf"] = cbf
    shared["lamv"] = np.ascontiguousarray(np.concatenate(
        [inp["lam_q1"][0], inp["lam_k1"][0], inp["lam_q2"][0], inp["lam_k2"][0]]).astype(np.float32)[None, :])
    maps = []
    for c in range(ncores):
        m = dict(shared)
        m["x"] = np.ascontiguousarray(inp["x"][c])
        m["mem"] = np.ascontiguousarray(inp["mem"][c])
        m["positions"] = np.ascontiguousarray(inp["positions"][c][None, :].astype(np.int32))
        maps.append(m)
    return maps


_NC_CACHE = {}


def kernel(**inputs):
    inp = {k: np.asarray(v) for k, v in inputs.items()}
    if "nc" not in _NC_CACHE:
        _NC_CACHE["nc"] = build_program()
    nc = _NC_CACHE["nc"]
    in_maps = make_in_maps(inp)
    res = run_bass_kernel_spmd(nc, in_maps, core_ids=list(range(NCORES)))
    out = np.stack([np.asarray(r["out"], dtype=np.float32) for r in res.results], axis=0)
    return out
```

```python
import math
from contextlib import ExitStack

import numpy as np
import ml_dtypes

import concourse.bass as bass
import concourse.mybir as mybir
from concourse.bass_utils import run_bass_kernel_spmd

F32 = mybir.dt.float32
BF16 = mybir.dt.bfloat16
I32 = mybir.dt.int32
AF = mybir.ActivationFunctionType
ALU = mybir.AluOpType

S = 2048
D = 1024
KC = D // 128
DFF = 2816
NFF = DFF // 128
TT = 512
NT = S // TT
EPS = 1e-6
NCORES = 8
STAGE = 99


class Buf:
    __slots__ = ("name", "last_w", "readers")

    def __init__(self, name):
        self.name = name
        self.last_w = None
        self.readers = []


ENG_NAMES = ("pe", "act", "dve", "pool", "sp")
NDMA_SEM = 20


class Prog:
    def __init__(self, nc, es, same_engine_sync=True):
        self.nc = nc
        self.same_engine_sync = same_engine_sync
        self.fuse_waits = True
        self.engs = {"pe": nc.tensor, "act": nc.scalar, "dve": nc.vector,
                     "pool": nc.gpsimd, "sp": nc.sync}
        self.eid = {n: i for i, n in enumerate(ENG_NAMES)}
        self.sems = []
        for n in ENG_NAMES:
            self.sems.append(es.enter_context(nc.semaphore("s_" + n)))
        self.dma_sem_ids = {}
        for q in ("sp", "pool"):
            ids = []
            for i in range(NDMA_SEM):
                ids.append(len(self.sems))
                self.sems.append(es.enter_context(nc.semaphore(f"d_{q}{i}")))
            self.dma_sem_ids[q] = ids
        self.nclk = len(self.sems)
        self.count = [0] * self.nclk
        self.clk = {n: [0] * self.nclk for n in ENG_NAMES}
        self.snap = {}
        self.dma_rr = {"sp": 0, "pool": 0}
        self.n_wait = 0
        self.n_inst = 0

    def _need(self, ename, tok):
        sid, val = tok
        c = self.clk[ename]
        if c[sid] >= val:
            return False
        if sid == self.eid.get(ename, -1):
            if ename == "pe" or not self.same_engine_sync:
                return False
        assert val <= self.count[sid], f"wait for unsignalled token {tok} (count {self.count[sid]})"
        sn = self.snap.get(tok)
        if sn is not None:
            for i in range(self.nclk):
                if sn[i] > c[i]:
                    c[i] = sn[i]
        if c[sid] < val:
            c[sid] = val
        return True

    def _wait(self, ename, tok):
        if self._need(ename, tok):
            self.engs[ename].wait_ge(self.sems[tok[0]], tok[1])
            self.n_wait += 1

    def _deps(self, reads, writes):
        deps = []
        for b in reads:
            if b.last_w is not None:
                deps.append(b.last_w)
        for b in writes:
            if b.last_w is not None:
                deps.append(b.last_w)
            deps.extend(b.readers)
        return deps

    def _record(self, tok, reads, writes):
        for b in reads:
            b.readers.append(tok)
        for b in writes:
            b.last_w = tok
            b.readers = []

    def op(self, ename, fn, reads=(), writes=(), signal=True, fuse=True):
        deps = self._deps(reads, writes)
        deps.sort(key=lambda t: -t[1])
        need = [tok for tok in deps if self._need(ename, tok)]
        fused = None
        if need and fuse and self.fuse_waits and ename in ("act", "dve"):
            fused = need.pop()
        for tok in need:
            self.engs[ename].wait_ge(self.sems[tok[0]], tok[1])
            self.n_wait += 1
        ins = fn(self.engs[ename])
        if fused is not None:
            ins._wait_ge(self.sems[fused[0]], fused[1])
        sid = self.eid[ename]
        self.n_inst += 1
        if signal:
            ins.then_inc(self.sems[sid], 1)
            self.count[sid] += 1
            tok = (sid, self.count[sid])
            self.snap[tok] = list(self.clk[ename])
        else:
            tok = (sid, self.count[sid] + 1)
        self._record(tok, reads, writes)
        return tok

    def dma(self, q, out, in_, reads=(), writes=()):
        for tok in self._deps(reads, writes):
            self._wait(q, tok)
        ids = self.dma_sem_ids[q]
        sid = ids[self.dma_rr[q] % NDMA_SEM]
        self.dma_rr[q] += 1
        if self.count[sid] > 0:
            self._wait(q, (sid, self.count[sid]))
        self.engs[q].dma_start(out=out, in_=in_).then_inc(self.sems[sid], 16)
        self.n_inst += 1
        self.count[sid] += 16
        tok = (sid, self.count[sid])
        self.snap[tok] = list(self.clk[q])
        self._record(tok, reads, writes)
        return tok

    def barrier(self):
        for ename in ENG_NAMES:
            for sid in range(self.nclk):
                if self.count[sid] > 0:
                    own = sid == self.eid[ename]
                    if own and ename in ("pe", "sp"):
                        continue
                    c = self.clk[ename]
                    if c[sid] < self.count[sid]:
                        self.engs[ename].wait_ge(self.sems[sid], self.count[sid])
                        c[sid] = self.count[sid]
                        self.n_wait += 1

    def finish(self, toks):
        for tok in toks:
            self._wait("sp", tok)


class WStream:
    HOLD = 2

    def __init__(self, P, bufs, bufobjs):
        self.P = P
        self.bufs = bufs
        self.bobj = bufobjs
        self.plan = []
        self.issued = 0
        self.taken = 0

    def add(self, tag, parts):
        self.plan.append((tag, parts))

    def _issue(self, i):
        tag, parts = self.plan[i]
        k = i % len(self.bufs)
        for dst_fn, src in parts:
            self.P.dma("pool", dst_fn(self.bufs[k]), src, writes=[self.bobj[k]])

    def take(self, tag):
        i = self.taken
        assert self.plan[i][0] == tag, (self.plan[i][0], tag)
        ahead = len(self.bufs) - self.HOLD
        while self.issued < min(len(self.plan), i + 1 + ahead):
            self._issue(self.issued)
            self.issued += 1
        self.taken += 1
        k = i % len(self.bufs)
        return self.bufs[k], self.bobj[k]


def v3(t, a, b):
    return t[:, 0:a * b].rearrange("p (a b) -> p a b", a=a)


INW = 13312
OFF_RQ, OFF_RK, OFF_RV, OFF_RG = 0, 1024, 2048, 4096
OFF_DQ, OFF_DK, OFF_DV, OFF_MQ, OFF_GT = 6144, 7168, 8192, 9216, 10240
LAM_INIT = 0.8 - 0.6 * math.exp(0.0)
GAMMA = [1.0 - 2.0 ** (-5.0 - h) for h in range(4)]
TWO_PI = 2.0 * math.pi
CW1 = 6.28125
CW2 = TWO_PI - CW1
PI_SAFE = 3.1415925

PV_GFFN1, PV_GMIX, PV_GFFN2, PV_GMEM = 0, 8, 16, 24
PV_GDQ, PV_GDK, PV_GDO, PV_GMQ, PV_GMK, PV_INV, PV_KDEC = 32, 33, 34, 35, 37, 39, 40
CF_ID, CF_N = 0, 128
CR_DEC, CR_QDEC, CR_N = 0, 512, 1024
CB_ID, CB_O1024, CB_BD64, CB_O128, CB_O256, CB_ONE, CB_TRI, CB_N = 0, 128, 256, 384, 512, 640, 768, 896


def build_program(stage=STAGE, same_engine_sync=True, debug=False):
    nc = bass.Bass("TRN2", target_bir_lowering=False)
    es = ExitStack()
    with es:
        def din(name, shape, dt=F32):
            return nc.dram_tensor(name, list(shape), dt, kind="ExternalInput").ap()

        def dscratch(name, shape, dt):
            kind = "ExternalOutput" if debug else "Internal"
            return nc.dram_tensor(name, list(shape), dt, kind=kind).ap()

        x_d = din("x", [S, D])
        mem_d = din("mem", [256, D])
        pos_d = din("positions", [1, S], I32)
        w1a_d = din("w_ffn1_in", [D, 2 * DFF])
        w1b_d = din("w_ffn1_out", [DFF, D])
        w2a_d = din("w_ffn2_in", [D, 2 * DFF])
        w2b_d = din("w_ffn2_out", [DFF, D])
        win_d = din("w_in", [D, INW])
        wkv_d = din("w_mem_kv", [D, 2048])
        wbr_d = din("w_br_ret", [2048, D])
        wbd_d = din("w_br_diff", [D, D])
        wbm_d = din("w_br_mem", [D, D])
        wo_d = din("w_o", [D, D])
        pvec_d = din("pvec", [128, 64])
        lamv_d = din("lamv", [1, 256])
        cf32_d = din("cf32", [128, CF_N])
        cret_d = din("cret", [128, CR_N])
        cbf_d = din("cbf", [128, CB_N], BF16)
        out_d = nc.dram_tensor("out", [S, D], F32, kind="ExternalOutput").ap()
        yp_d = dscratch("yp", [3, KC, 128, S], BF16)
        ypdb = [[Buf(f"ypd{b}_{t}") for t in range(4)] for b in range(3)]

        win3 = win_d.rearrange("(k p) n -> p k n", p=128)

        P = Prog(nc, es, same_engine_sync=same_engine_sync)

        def sb(name, shape, dt, st=es):
            return st.enter_context(nc.sbuf_tensor(name, list(shape), dt))

        xT = sb("xT", [128, KC * S], F32)
        hT = sb("hT", [128, KC * S], BF16)
        pvec = sb("pvec_sb", [128, 64], F32)
        cf32 = sb("cf32_sb", [128, CF_N], F32)
        cbf = sb("cbf_sb", [128, CB_N], BF16)
        epsc = sb("epsc", [128, 1], F32)
        onec = sb("onec", [128, 1], F32)
        NWB = 4
        wbufs = [sb(f"wbuf{i}", [128, 4096], BF16) for i in range(NWB)]
        wbobj = [Buf(f"wbuf{i}") for i in range(NWB)]
        W = WStream(P, wbufs, wbobj)

        xT3 = xT[:, :].rearrange("p (k s) -> p k s", k=KC)
        hT3 = hT[:, :].rearrange("p (k s) -> p k s", k=KC)
        xTb = [[Buf(f"xT{k}_{t}") for t in range(NT)] for k in range(KC)]
        hTb = [[Buf(f"hT{k}_{t}") for t in range(NT)] for k in range(KC)]
        b_const = Buf("const")

        identf = cf32[:, CF_ID:CF_ID + 128]
        identb = cbf[:, CB_ID:CB_ID + 128]
        ones_d = cbf[:, CB_O1024:CB_O1024 + 128]
        bd64 = cbf[:, CB_BD64:CB_BD64 + 128]
        ones128 = cbf[:, CB_O128:CB_O128 + 128]
        ones256 = cbf[:, CB_O256:CB_O256 + 128]
        ones1 = cbf[:, CB_ONE:CB_ONE + 128]
        tri = cbf[:, CB_TRI:CB_TRI + 128]

        ps, psb, pb, pbb, pw, pwb = [], [], [], [], [], []
        psum_ctr = [0]

        def set_psum(ph, n_single=6, n_bf=2, n_wide=0):
            k = psum_ctr[0]
            psum_ctr[0] += 1
            ps[:] = [ph.enter_context(nc.psum_tensor(f"ps{k}_{i}", [128, 512], F32)) for i in range(n_single)]
            psb[:] = [Buf(f"ps{i}") for i in range(n_single)]
            pb[:] = [ph.enter_context(nc.psum_tensor(f"pb{k}_{i}", [128, 1024], BF16)) for i in range(n_bf)]
            pbb[:] = [Buf(f"pb{i}") for i in range(n_bf)]
            pw[:] = [ph.enter_context(nc.psum_tensor(f"pw{k}_{i}", [128, 1024], F32)) for i in range(n_wide)]
            pwb[:] = [Buf(f"pwh{i}") for i in range(2 * n_wide)]

        def mm(psi, lhsT, rhs, start, stop, reads, sig=None, cols=None):
            out = ps[psi][:, :] if cols is None else ps[psi][:, cols]
            P.op("pe", lambda e: e.matmul(out, lhsT=lhsT, rhs=rhs, start=start, stop=stop),
                 reads=reads, writes=[psb[psi]], signal=(stop if sig is None else sig))

        def mm2(out, outbuf, lhsT, rhs, start, stop, reads, sig=None):
            P.op("pe", lambda e: e.matmul(out, lhsT=lhsT, rhs=rhs, start=start, stop=stop),
                 reads=reads, writes=[outbuf], signal=(stop if sig is None else sig))

        def act(out, in_, func, reads, writes, **kw):
            P.op("act", lambda e: e.activation(out=out, in_=in_, func=func, **kw), reads=reads, writes=writes,
                 fuse=("accum_out" not in kw))

        def rstd_from_ms(psi, n, rstd_ap, rstd_buf, prange=slice(0, 128)):
            act(rstd_ap, ps[psi][prange, 0:n], AF.Ln, [psb[psi], b_const], [rstd_buf], bias=epsc[prange, 0:1])
            act(rstd_ap, rstd_ap, AF.Exp, [rstd_buf], [rstd_buf], scale=-0.5)

        def plan_ffn(tag, wa, wb):
            wa3 = wa.rearrange("(k p) n -> p k n", p=128)
            wb3 = wb.rearrange("(j p) n -> p j n", p=128)
            for b in range(NFF // 2):
                W.add((tag, "a", b), [
                    (lambda t: v3(t, KC, 512)[:, :, 0:256], wa3[:, :, b * 256:(b + 1) * 256]),
                    (lambda t: v3(t, KC, 512)[:, :, 256:512],
                     wa3[:, :, DFF + b * 256:DFF + (b + 1) * 256]),
                ])
                W.add((tag, "b", b), [
                    (lambda t: v3(t, 2, 1024), wb3[:, 2 * b:2 * b + 2, :]),
                ])

        def blk_in(c0):
            return [(lambda t: v3(t, KC, 512), win3[:, :, c0:c0 + 512])]

        def plan_merge(tag, wsrc, nk, gate_off):
            w3 = wsrc.rearrange("(k p) n -> p k n", p=128)
            for cb in range(4):
                W.add((tag, "w", cb), [(lambda t, nk=nk: v3(t, nk, 256), w3[:, :, cb * 256:(cb + 1) * 256])])
                if cb % 2 == 0:
                    W.add((tag, "g", cb // 2), blk_in(gate_off + (cb // 2) * 512))

        plan_ffn("ffn1", w1a_d, w1b_d)
        if stage >= 2:
            for g4 in range(2):
                W.add(("dq", g4), blk_in(OFF_DQ + g4 * 512))
                W.add(("dk", g4), blk_in(OFF_DK + g4 * 512))
                W.add(("dv", g4), blk_in(OFF_DV + g4 * 512))
            for t in range(NT):
                plan_merge(("mdiff", t), wbd_d, 8, OFF_GT + 1024)
        if stage >= 3:
            wkv3 = wkv_d.rearrange("(k p) n -> p k n", p=128)
            for i in range(4):
                W.add(("mkv", i), [(lambda t: v3(t, KC, 512), wkv3[:, :, i * 512:(i + 1) * 512])])
            for t in range(NT):
                for i in range(2):
                    W.add(("mq", t, i), blk_in(OFF_MQ + i * 512))
                plan_merge(("mmem", t), wbm_d, 8, OFF_GT + 2048)
        if stage >= 4:
            for t in range(NT):
                for h in range(4):
                    W.add(("rqk", t, h), [
                        (lambda tt: v3(tt, KC, 512)[:, :, 0:256], win3[:, :, OFF_RQ + h * 256:OFF_RQ + (h + 1) * 256]),
                        (lambda tt: v3(tt, KC, 512)[:, :, 256:512], win3[:, :, OFF_RK + h * 256:OFF_RK + (h + 1) * 256]),
                    ])
                    W.add(("rv", t, h), blk_in(OFF_RV + h * 512))
                    W.add(("rg", t, h), blk_in(OFF_RG + h * 512))
                plan_merge(("mret", t), wbr_d, 16, OFF_GT)
        if stage >= 6:
            plan_ffn("ffn2", w2a_d, w2b_d)

        P.dma("sp", pvec[:, :], pvec_d[:, :], writes=[b_const])
        P.dma("sp", cf32[:, :], cf32_d[:, :], writes=[b_const])
        P.dma("sp", cbf[:, :], cbf_d[:, :], writes=[b_const])
        P.op("dve", lambda e: e.memset(epsc[:, :], EPS), writes=[b_const])
        P.op("dve", lambda e: e.memset(onec[:, :], 1.0), writes=[b_const])

        def load_x(ph):
            xin = [sb(f"xin{i}", [128, D], F32, ph) for i in range(2)]
            xinb = [Buf(f"xin{i}") for i in range(2)]
            for r in range(S // 128):
                t = r // 4
                xi, xib = xin[r % 2], xinb[r % 2]
                P.dma("sp", xi[:, :], x_d[r * 128:(r + 1) * 128, :], writes=[xib])
                for half in range(2):
                    pk = 6 + half
                    for q in range(4):
                        kc = half * 4 + q
                        P.op("pe", lambda e, kc=kc, q=q, pk=pk, xi=xi: e.transpose(
                            ps[pk][:, q * 128:(q + 1) * 128], xi[:, kc * 128:(kc + 1) * 128], identf),
                            reads=[xib, b_const], writes=[psb[pk]], signal=(q == 3))
                    dst = xT3[:, half * 4:half * 4 + 4, r * 128:(r + 1) * 128]
                    src = ps[pk][:, :].rearrange("p (a b) -> p a b", a=4)
                    wr = [xTb[half * 4 + q][t] for q in range(4)]
                    if half == 0:
                        P.op("dve", lambda e, dst=dst, src=src: e.tensor_copy(out=dst, in_=src),
                             reads=[psb[pk]], writes=wr)
                    else:
                        act(dst, src, AF.Copy, [psb[pk]], wr)

        out_toks = []

        def store_tile(t, xo, xob):
            for r in range(t * 4, t * 4 + 4):
                xo_, xob_ = xo[r % 2], xob[r % 2]
                for half in range(2):
                    pk = 6 + half
                    for q in range(4):
                        kc = half * 4 + q
                        P.op("pe", lambda e, kc=kc, q=q, pk=pk, r=r: e.transpose(
                            ps[pk][:, q * 128:(q + 1) * 128], xT3[:, kc, r * 128:(r + 1) * 128], identf),
                            reads=[xTb[kc][t], b_const], writes=[psb[pk]], signal=(q == 3))
                    dst = xo_[:, half * 512:(half + 1) * 512]
                    if half == 0:
                        P.op("dve", lambda e, dst=dst, pk=pk: e.tensor_copy(out=dst, in_=ps[pk][:, :]),
                             reads=[psb[pk]], writes=[xob_])
                    else:
                        act(dst, ps[pk][:, :], AF.Copy, [psb[pk]], [xob_])
                out_toks.append(P.dma("sp", out_d[r * 128:(r + 1) * 128, :], xo_[:, :], reads=[xob_]))

        ph_names = []
        def rmsnorm_to_hT(gcol0, ph, tiles=None, dst=None, nbufs=None):
            key = f"{gcol0}_{len(ph_names)}"
            ph_names.append(key)
            if nbufs is None:
                sq = [sb(f"nsq{i}_{key}", [128, TT], BF16, ph) for i in range(2)]
                sqb = [Buf(f"nsq{i}") for i in range(2)]
                rstd = sb(f"nrstd_{key}", [128, TT], F32, ph)
                rstdb = Buf("nrstd")
            else:
                sq, sqb, rstd, rstdb = nbufs
            for t in (range(NT) if tiles is None else tiles):
                ts = slice(t * TT, (t + 1) * TT)
                for kc in range(KC):
                    s_, sb_ = sq[kc % 2], sqb[kc % 2]
                    act(s_[:, :], xT3[:, kc, ts], AF.Square, [xTb[kc][t]], [sb_])
                    mm(5, ones_d, s_[:, :], kc == 0, kc == KC - 1, [sb_, b_const], sig=True)
                rstd_from_ms(5, TT, rstd[:, :], rstdb)
                for kc in range(KC):
                    o_ap = hT3[:, kc, ts] if dst is None else dst[0][:, kc, :]
                    o_b = hTb[kc][t] if dst is None else dst[1][kc]
                    P.op("dve", lambda e, kc=kc, o_ap=o_ap: e.scalar_tensor_tensor(
                        out=o_ap, in0=xT3[:, kc, ts], scalar=pvec[:, gcol0 + kc:gcol0 + kc + 1],
                        in1=rstd[:, :], op0=ALU.mult, op1=ALU.mult),
                        reads=[xTb[kc][t], rstdb, b_const], writes=[o_b])

        def ffn(tag, gcol0, do_load=False, do_store=False):
            with ExitStack() as ph:
                set_psum(ph, 8, 0, 0)
                if do_load:
                    load_x(ph)
                if do_store:
                    xo = [sb(f"xo{i}", [128, D], F32, ph) for i in range(2)]
                    xob = [Buf(f"xo{i}") for i in range(2)]
                rmsnorm_to_hT(gcol0, ph)
                sg = [sb(f"sg{i}_{tag}", [128, TT], F32, ph) for i in range(2)]
                sgb = [Buf(f"sg{i}") for i in range(2)]
                uu = [sb(f"uu{i}_{tag}", [128, TT], BF16, ph) for i in range(4)]
                uub = [Buf(f"uu{i}") for i in range(4)]
                it = 0
                W.HOLD = 3
                pend = []

                def out_proj(wb3, wbb, ub, t):
                    ts = slice(t * TT, (t + 1) * TT)
                    for c in range(KC):
                        po = 4 + (c % 4)
                        for j in range(2):
                            mm(po, wb3[:, j, c * 128:(c + 1) * 128], ub[j][0][:, :], j == 0, j == 1,
                               [wbb, ub[j][1]])
                        P.op("dve", lambda e, c=c, po=po: e.scalar_tensor_tensor(
                            out=xT3[:, c, ts], in0=ps[po][:, :], scalar=0.5, in1=xT3[:, c, ts],
                            op0=ALU.mult, op1=ALU.add),
                            reads=[psb[po], xTb[c][t]], writes=[xTb[c][t]])

                for b in range(NFF // 2):
                    wa, wab = W.take((tag, "a", b))
                    wbt, wbb = W.take((tag, "b", b))
                    wa3 = v3(wa, KC, 512)
                    wb3 = v3(wbt, 2, 1024)
                    for t in range(NT):
                        ts = slice(t * TT, (t + 1) * TT)
                        ub = []
                        for j in range(2):
                            pg, pu = (it % 2) * 2, (it % 2) * 2 + 1
                            for kc in range(KC):
                                mm(pg, wa3[:, kc, j * 128:(j + 1) * 128], hT3[:, kc, ts], kc == 0, kc == KC - 1,
                                   [wab, hTb[kc][t]])
                            for kc in range(KC):
                                mm(pu, wa3[:, kc, 256 + j * 128:256 + (j + 1) * 128], hT3[:, kc, ts],
                                   kc == 0, kc == KC - 1, [wab, hTb[kc][t]])
                            s_, sb_ = sg[it % 2], sgb[it % 2]
                            u_, ub_ = uu[it % 4], uub[it % 4]
                            act(s_[:, :], ps[pg][:, :], AF.Silu, [psb[pg]], [sb_])
                            P.op("dve", lambda e, s_=s_, u_=u_, pu=pu: e.tensor_tensor(
                                out=u_[:, :], in0=ps[pu][:, :], in1=s_[:, :], op=ALU.mult),
                                reads=[psb[pu], sb_], writes=[ub_])
                            ub.append((u_, ub_))
                            it += 1
                        if pend:
                            pt = pend.pop(0)
                            out_proj(*pt)
                            if do_store and b == NFF // 2 - 1:
                                store_tile(pt[3], xo, xob)
                        pend.append((wb3, wbb, ub, t))
                pt = pend.pop(0)
                out_proj(*pt)
                if do_store:
                    store_tile(pt[3], xo, xob)
                W.HOLD = 2
                P.barrier()

        def branch_merge(tag, bi, t, src3, src_bufs, nk, ph_bufs, h3t=None, hbt=None):
            gsb, gsbb, ypt, yptb = ph_bufs
            ts = slice(t * TT, (t + 1) * TT)
            if h3t is None:
                h3t = hT3[:, :, ts]
                hbt = [hTb[kc][t] for kc in range(KC)]
            wg3 = None
            for cb in range(4):
                wt, wtb = W.take((tag, "w", cb))
                w3 = v3(wt, nk, 256)
                if cb % 2 == 0:
                    wg, wgb = W.take((tag, "g", cb // 2))
                    wg3 = v3(wg, KC, 512)
                for cc in range(2):
                    c = cb * 2 + cc
                    pz, pg = (c % 2) * 2, (c % 2) * 2 + 1
                    for k in range(nk):
                        mm(pz, w3[:, k, cc * 128:(cc + 1) * 128], src3[:, k, :], k == 0, k == nk - 1,
                           [wtb, src_bufs[k]])
                    gc = (c % 4) * 128
                    for kc in range(KC):
                        mm(pg, wg3[:, kc, gc:gc + 128], h3t[:, kc, :], kc == 0, kc == KC - 1,
                           [wgb, hbt[kc]])
                    g_, gb_ = gsb[c % 2], gsbb[c % 2]
                    act(g_[:, :], ps[pg][:, :], AF.Sigmoid, [psb[pg]], [gb_])
                    P.op("dve", lambda e, c=c, pz=pz, g_=g_: e.tensor_tensor(
                        out=ypt[:, c * TT:(c + 1) * TT], in0=ps[pz][:, :], in1=g_[:, :], op=ALU.mult),
                        reads=[psb[pz], gb_], writes=[yptb])
            P.dma("sp", yp_d[bi, :, :, ts].rearrange("k p s -> p k s"),
                  ypt[:, 0:KC * TT].rearrange("p (k s) -> p k s", k=KC), reads=[yptb], writes=[ypdb[bi][t]])

        def merge_bufs(ph, nm):
            gsb = [sb(f"gsb{i}_{nm}", [128, TT], F32, ph) for i in range(2)]
            gsbb = [Buf(f"gsb{i}") for i in range(2)]
            ypt = sb(f"ypt_{nm}", [128, KC * TT], BF16, ph)
            return gsb, gsbb, ypt, Buf("ypt")

        if stage >= 1:
            ffn("ffn1", PV_GFFN1, do_load=True)

        if stage >= 2:
            with ExitStack() as ph:
                set_psum(ph)
                rmsnorm_to_hT(PV_GMIX, ph)
                P.barrier()

            with ExitStack() as ph:
                DIF = sb("DIF", [128, 8 * S], BF16, ph)
                DIF3 = DIF[:, :].rearrange("p (h s) -> p h s", h=8)
                difb = [[Buf(f"dif{h}_{t}") for t in range(NT)] for h in range(8)]
                ph2 = ExitStack()
                dq0 = sb("dq0", [128, S], BF16, ph2)
                dq1 = sb("dq1", [128, S], BF16, ph2)
                dkn = sb("dkn", [128, S], BF16, ph2)
                dvT = sb("dvT", [128, 16 * 128], BF16, ph2)
                dqb = [Buf(f"dq_{t}") for t in range(NT)]
                dkb = [Buf(f"dk_{t}") for t in range(NT)]
                dvb = [Buf(f"dv_{t}") for t in range(NT)]
                set_psum(ph2, 4, 0, 2)
                sqd = [sb(f"sqd{i}", [128, TT], BF16, ph2) for i in range(2)]
                sqdb = [Buf(f"sqd{i}") for i in range(2)]
                rsd = [sb(f"rsd{i}", [128, TT], F32, ph2) for i in range(2)]
                rsdb = [Buf(f"rsd{i}") for i in range(2)]
                ET = [sb(f"ET{i}", [128, 2 * TT], BF16, ph2) for i in range(2)]
                ETb = [Buf(f"ET{i}") for i in range(2)]
                Esum = sb("Esum", [128, TT], F32, ph2)
                Esumb = Buf("Esum")
                rl1 = sb("rl1", [128, TT], F32, ph2)
                rl1b = Buf("rl1")
                rl0 = sb("rl0", [128, TT], F32, ph2)
                rl0b = Buf("rl0")
                rse = sb("rse", [128, TT], F32, ph2)
                rseb = Buf("rse")
                sqe = sb("sqe", [128, TT], BF16, ph2)
                sqeb = Buf("sqe")
                Ehi = sb("Ehi", [128, TT], BF16, ph2)
                Elo = sb("Elo", [128, TT], BF16, ph2)
                Ehib, Elob = Buf("Ehi"), Buf("Elo")
                Ocp = sb("Ocp", [128, 2 * TT], F32, ph2)
                Ocpb = Buf("Ocp")
                lamv = sb("lamv_sb", [128, 256], F32, ph2)
                lamt = sb("lamt", [128, 64], F32, ph2)
                lcol = sb("lcol", [128, 4], F32, ph2)
                lamb = Buf("lam")

                P.dma("sp", lamv[:, :], lamv_d[0:1, :].broadcast_to([128, 256]), writes=[lamb])
                for i in range(2):
                    P.op("dve", lambda e, i=i: e.tensor_tensor(
                        out=lamt[:, :], in0=lamv[:, i * 128:i * 128 + 64], in1=lamv[:, i * 128 + 64:i * 128 + 128],
                        op=ALU.mult), reads=[lamb], writes=[lamb])
                    P.op("dve", lambda e, i=i: e.reduce_sum(out=lcol[:, i:i + 1], in_=lamt[:, :],
                                                            axis=mybir.AxisListType.X),
                         reads=[lamb], writes=[lamb])
                act(lcol[:, 0:2], lcol[:, 0:2], AF.Exp, [lamb], [lamb])
                P.op("dve", lambda e: e.tensor_tensor(out=lcol[:, 2:3], in0=lcol[:, 1:2], in1=lcol[:, 0:1],
                                                      op=ALU.subtract), reads=[lamb], writes=[lamb])
                P.op("dve", lambda e: e.tensor_scalar(out=lcol[:, 2:3], in0=lcol[:, 2:3], scalar1=-LAM_INIT,
                                                      scalar2=None, op0=ALU.add), reads=[lamb], writes=[lamb])
                P.op("dve", lambda e: e.tensor_scalar(out=lcol[:, 3:4], in0=pvec[:, PV_GDO:PV_GDO + 1],
                                                      scalar1=1.0 - LAM_INIT, scalar2=None, op0=ALU.mult),
                     reads=[lamb, b_const], writes=[lamb])
                nlam = lcol[:, 2:3]
                gdo = lcol[:, 3:4]
                P.op("dve", lambda e: e.memset(dq0[64:128, :], 0.0), writes=dqb)
                P.op("dve", lambda e: e.memset(dq1[0:64, :], 0.0), writes=dqb)

                W.HOLD = 4
                wq = wk = wv = None
                pending = []

                def flush_pending():
                    while pending:
                        pending.pop(0)()

                def pop_pending(n):
                    for _ in range(n):
                        if pending:
                            pending.pop(0)()

                def pwh(i):
                    return pw[i // 2][:, (i % 2) * 512:(i % 2 + 1) * 512]

                for h in range(8):
                    hl = h % 4
                    if hl == 0:
                        wq, wqb = W.take(("dq", h // 4))
                        wk, wkb = W.take(("dk", h // 4))
                        wv, wvb = W.take(("dv", h // 4))
                    wq3, wk3, wv3 = v3(wq, KC, 512), v3(wk, KC, 512), v3(wv, KC, 512)
                    groups = [(t, which) for t in range(NT) for which in (0, 1)]

                    def stageA(g):
                        t, which = groups[g]
                        ts = slice(t * TT, (t + 1) * TT)
                        w3, wb_ = (wq3, wqb) if which == 0 else (wk3, wkb)
                        i = g % 4
                        for kc in range(KC):
                            mm2(pwh(i), pwb[i], w3[:, kc, hl * 128:(hl + 1) * 128], hT3[:, kc, ts], kc == 0, kc == KC - 1,
                                [wb_, hTb[kc][t]])

                    def stageB(g):
                        t, which = groups[g]
                        ts = slice(t * TT, (t + 1) * TT)
                        i, j = g % 4, g % 2
                        act(sqd[j][:, :], pwh(i), AF.Square, [pwb[i]], [sqdb[j]])
                        mm(3, bd64, sqd[j][:, :], True, True, [sqdb[j], b_const])
                        rstd_from_ms(3, TT, rsd[j][:, :], rsdb[j])
                        if which == 0:
                            for c, dq in enumerate((dq0, dq1)):
                                pr = slice(c * 64, (c + 1) * 64)
                                P.op("dve", lambda e, dq=dq, pr=pr: e.scalar_tensor_tensor(
                                    out=dq[pr, ts], in0=pwh(i)[pr, :], scalar=pvec[pr, PV_GDQ:PV_GDQ + 1],
                                    in1=rsd[j][pr, :], op0=ALU.mult, op1=ALU.mult),
                                    reads=[pwb[i], rsdb[j], b_const], writes=[dqb[t]])
                        else:
                            P.op("dve", lambda e: e.scalar_tensor_tensor(
                                out=dkn[:, ts], in0=pwh(i), scalar=pvec[:, PV_GDK:PV_GDK + 1],
                                in1=rsd[j][:, :], op0=ALU.mult, op1=ALU.mult),
                                reads=[pwb[i], rsdb[j], b_const], writes=[dkb[t]])

                    def stageV(t):
                        for c4 in range(4):
                            tok = slice(t * TT + c4 * 128, t * TT + (c4 + 1) * 128)
                            for kc in range(KC):
                                mm(t % 2, hT3[:, kc, tok], wv3[:, kc, hl * 128:(hl + 1) * 128], kc == 0, kc == KC - 1,
                                   [wvb, hTb[kc][t]], cols=slice(c4 * 128, (c4 + 1) * 128), sig=(kc == KC - 1))
                        act(dvT[:, t * 512:(t + 1) * 512], ps[t % 2][:, :], AF.Copy, [psb[t % 2]], [dvb[t]])

                    for g in range(8 + 2):
                        if g < 8:
                            stageA(g)
                        if g >= 1:
                            pop_pending(2)
                        if g >= 2:
                            stageB(g - 2)
                        if g % 2 == 1 and g < 8:
                            stageV(g // 2)

                    for qt in range(NT):
                        nkt = 4 * qt + 4
                        ts = slice(qt * TT, (qt + 1) * TT)

                        def emitS(kt):
                            w = kt % 2
                            j0 = max(0, kt - 4 * qt) * 128
                            qs = slice(qt * TT + j0, (qt + 1) * TT)
                            for c, dq in enumerate((dq0, dq1)):
                                mm2(pw[w][:, c * 512 + j0:(c + 1) * 512], pwb[2 * w + c], dkn[:, kt * 128:(kt + 1) * 128],
                                    dq[:, qs], True, True, [dkb[kt // 4], dqb[qt]], sig=(c == 1))

                        emitS(0)
                        for kt in range(nkt):
                            if kt + 1 < nkt:
                                emitS(kt + 1)
                            if kt >= 1:
                                pop_pending(2 if len(pending) > 6 else 1)
                            w = kt % 2
                            j0 = max(0, kt - 4 * qt) * 128
                            diag = kt >= 4 * qt
                            E, Eb = ET[w], ETb[w]
                            s3 = pw[w][:, :].rearrange("p (c s) -> p c s", c=2)[:, :, j0:TT]
                            e3 = E[:, :].rearrange("p (c s) -> p c s", c=2)[:, :, j0:TT]
                            act(e3, s3, AF.Exp, [pwb[2 * w], pwb[2 * w + 1]], [Eb], scale=0.125)
                            if diag:
                                for c in range(2):
                                    blk = E[:, c * 512 + j0:c * 512 + j0 + 128]
                                    P.op("dve", lambda e, blk=blk: e.tensor_tensor(out=blk, in0=blk, in1=tri, op=ALU.mult),
                                         reads=[Eb, b_const], writes=[Eb])
                            e0, es0 = E[:, j0:TT], Esum[:, j0:TT]
                            if kt == 0:
                                P.op("dve", lambda e, e0=e0, es0=es0: e.tensor_copy(out=es0, in_=e0), reads=[Eb], writes=[Esumb])
                            else:
                                P.op("dve", lambda e, e0=e0, es0=es0: e.tensor_tensor(out=es0, in0=e0, in1=es0, op=ALU.add),
                                     reads=[Eb, Esumb], writes=[Esumb])
                            for c in range(2):
                                mm(c, dvT[:, kt * 128:(kt + 1) * 128], E[:, c * 512 + j0:(c + 1) * 512], kt == 0, kt == nkt - 1,
                                   [dvb[kt // 4], Eb], cols=slice(j0, TT), sig=True)
                            mm(3, ones1, E[:, 512 + j0:1024], kt == 0, kt == nkt - 1, [Eb, b_const], cols=slice(j0, TT), sig=True)
                        Es, Esb = Esum, Esumb
                        for c in range(2):
                            P.op("dve", lambda e, c=c: e.tensor_copy(out=Ocp[:, c * 512:(c + 1) * 512], in_=ps[c][:, :]),
                                 reads=[psb[c]], writes=[Ocpb])
                        act(rl1[:, :], ps[3][:, :], AF.Ln, [psb[3]], [rl1b])
                        act(rl1[:, :], rl1[:, :], AF.Exp, [rl1b], [rl1b], scale=-1.0)
                        P.op("dve", lambda e: e.tensor_tensor(out=Ocp[:, 512:1024], in0=Ocp[:, 512:1024], in1=rl1[:, :],
                                                              op=ALU.mult), reads=[Ocpb, rl1b], writes=[Ocpb])
                        P.op("dve", lambda e: e.tensor_copy(out=Ehi[:, :], in_=Esum[:, :]), reads=[Esumb], writes=[Ehib])
                        P.op("dve", lambda e: e.tensor_tensor(out=Elo[:, :], in0=Esum[:, :], in1=Ehi[:, :], op=ALU.subtract),
                             reads=[Esumb, Ehib], writes=[Elob])

                        def f1():
                            mm(2, ones1, Ehi[:, :], True, False, [Ehib, b_const], sig=False)
                            mm(2, ones1, Elo[:, :], False, True, [Elob, b_const], sig=True)

                        def f2():
                            act(rl0[:, :], ps[2][:, :], AF.Ln, [psb[2]], [rl0b])
                            act(rl0[:, :], rl0[:, :], AF.Exp, [rl0b], [rl0b], scale=-1.0)

                        def f2b():
                            P.op("dve", lambda e: e.tensor_tensor(
                                out=Ocp[:, 0:512], in0=Ocp[:, 0:512], in1=rl0[:, :], op=ALU.mult),
                                reads=[Ocpb, rl0b], writes=[Ocpb])

                        def f3():
                            P.op("dve", lambda e: e.scalar_tensor_tensor(
                                out=Ocp[:, 0:512], in0=Ocp[:, 512:1024], scalar=nlam, in1=Ocp[:, 0:512],
                                op0=ALU.mult, op1=ALU.add), reads=[Ocpb, lamb], writes=[Ocpb])
                            act(sqe[:, :], Ocp[:, 0:512], AF.Square, [Ocpb], [sqeb])

                        def f4():
                            mm(2, ones128, sqe[:, :], True, True, [sqeb, b_const])

                        def f5():
                            rstd_from_ms(2, TT, rse[:, :], rseb)

                        def f6(h=h, ts=ts):
                            P.op("dve", lambda e: e.scalar_tensor_tensor(
                                out=DIF3[:, h, ts], in0=Ocp[:, 0:512], scalar=gdo, in1=rse[:, :], op0=ALU.mult, op1=ALU.mult),
                                reads=[Ocpb, rseb, lamb], writes=[difb[h][ts.start // TT]])

                        pending.extend([f1, f2, f2b, f3, f4, f5, f6])
                flush_pending()
                W.HOLD = 2
                P.barrier()
                ph2.close()
                set_psum(ph)
                mb = merge_bufs(ph, "d")
                for t in range(NT):
                    ts = slice(t * TT, (t + 1) * TT)
                    branch_merge(("mdiff", t), 1, t, DIF3[:, :, ts], [difb[h][t] for h in range(8)], 8, mb)
                P.barrier()


        if stage >= 3:
            with ExitStack() as ph:
                set_psum(ph, 8, 0, 0)
                memin = sb("memin", [128, 2 * D], F32, ph)
                memT = sb("memT", [128, KC * 256], F32, ph)
                memh = sb("memh", [128, KC * 256], BF16, ph)
                mkn = sb("mkn", [128, 4 * 2 * 256], BF16, ph)
                mvT = sb("mvT", [128, 2 * 1024], BF16, ph)
                sqm = [sb(f"sqm{i}", [128, TT], BF16, ph) for i in range(2)]
                sqmb = [Buf(f"sqm{i}") for i in range(2)]
                rsm = sb("rsm", [128, TT], F32, ph)
                rsmb = Buf("rsm")
                mqn = sb("mqn", [128, 2 * TT], BF16, ph)
                mqnb = Buf("mqn")
                EM = [sb(f"EM{i}", [128, TT], BF16, ph) for i in range(2)]
                EMb = [Buf(f"EM{i}") for i in range(2)]
                rl = sb("rl", [128, TT], F32, ph)
                rlb = Buf("rl")
                MO = sb("MO", [128, KC * TT], BF16, ph)
                mob = Buf("MO")
                meminb, memTb, memhb, mknb, mvTb = Buf("memin"), Buf("memT"), Buf("memh"), Buf("mkn"), Buf("mvT")
                memin3 = memin[:, :].rearrange("p (a n) -> p a n", a=2)
                memT3 = memT[:, :].rearrange("p (k m) -> p k m", k=KC)
                memh3 = memh[:, :].rearrange("p (k m) -> p k m", k=KC)
                mkn4 = mkn[:, :].rearrange("p (h c m) -> p h c m", h=4, c=2)
                mvT3 = mvT[:, :].rearrange("p (a n) -> p a n", a=2)
                mqn3 = mqn[:, :].rearrange("p (c s) -> p c s", c=2)
                MO3 = MO[:, :].rearrange("p (k s) -> p k s", k=KC)

                P.dma("sp", memin3, mem_d.rearrange("(a p) n -> p a n", p=128), writes=[meminb])
                for mt in range(2):
                    for half in range(2):
                        pk = 4 + half
                        for q in range(4):
                            kc = half * 4 + q
                            P.op("pe", lambda e, kc=kc, q=q, pk=pk, mt=mt: e.transpose(
                                ps[pk][:, q * 128:(q + 1) * 128], memin3[:, mt, kc * 128:(kc + 1) * 128], identf),
                                reads=[meminb, b_const], writes=[psb[pk]], signal=(q == 3))
                        dst = memT3[:, half * 4:half * 4 + 4, mt * 128:(mt + 1) * 128]
                        src = ps[pk][:, :].rearrange("p (a b) -> p a b", a=4)
                        P.op("dve", lambda e, dst=dst, src=src: e.tensor_copy(out=dst, in_=src),
                             reads=[psb[pk]], writes=[memTb])
                for kc in range(KC):
                    s_, sb_ = sqm[kc % 2], sqmb[kc % 2]
                    act(s_[:, 0:256], memT3[:, kc, :], AF.Square, [memTb], [sb_])
                    mm(5, ones_d, s_[:, 0:256], kc == 0, kc == KC - 1, [sb_, b_const], sig=True, cols=slice(0, 256))
                rstd_from_ms(5, 256, rsm[:, 0:256], rsmb)
                for kc in range(KC):
                    P.op("dve", lambda e, kc=kc: e.scalar_tensor_tensor(
                        out=memh3[:, kc, :], in0=memT3[:, kc, :], scalar=pvec[:, PV_GMEM + kc:PV_GMEM + kc + 1],
                        in1=rsm[:, 0:256], op0=ALU.mult, op1=ALU.mult),
                        reads=[memTb, rsmb, b_const], writes=[memhb])
                c256 = slice(0, 256)
                for blk in range(2):
                    wk, wkb = W.take(("mkv", blk))
                    wk3 = v3(wk, KC, 512)
                    for hh in range(2):
                        hm = blk * 2 + hh
                        for dc in range(2):
                            for kc in range(KC):
                                mm(dc, wk3[:, kc, hh * 256 + dc * 128:hh * 256 + (dc + 1) * 128], memh3[:, kc, :],
                                   kc == 0, kc == KC - 1, [wkb, memhb], cols=c256)
                            act(sqm[dc][:, 0:256], ps[dc][:, 0:256], AF.Square, [psb[dc]], [sqmb[dc]])
                            mm(2, ones256, sqm[dc][:, 0:256], dc == 0, dc == 1, [sqmb[dc], b_const], sig=True, cols=c256)
                        rstd_from_ms(2, 256, rsm[:, 0:256], rsmb)
                        for dc in range(2):
                            P.op("dve", lambda e, dc=dc, hm=hm: e.scalar_tensor_tensor(
                                out=mkn4[:, hm, dc, :], in0=ps[dc][:, 0:256], scalar=pvec[:, PV_GMK + dc:PV_GMK + dc + 1],
                                in1=rsm[:, 0:256], op0=ALU.mult, op1=ALU.mult),
                                reads=[psb[dc], rsmb, b_const], writes=[mknb])
                for vb in range(2):
                    wv, wvb = W.take(("mkv", 2 + vb))
                    wv3 = v3(wv, KC, 512)
                    for mt in range(2):
                        for kc in range(KC):
                            mm(3, memh3[:, kc, mt * 128:(mt + 1) * 128], wv3[:, kc, :], kc == 0, kc == KC - 1,
                               [wvb, memhb])
                        act(mvT3[:, mt, vb * 512:(vb + 1) * 512], ps[3][:, :], AF.Copy, [psb[3]], [mvTb])

                mb = merge_bufs(ph, "m")
                mqn2 = [mqn, sb("mqn_b", [128, 2 * TT], BF16, ph)]
                mqnb2 = [mqnb, Buf("mqn_b")]
                wq3_cur = [None, None]

                def stageP(t, hm):
                    ts = slice(t * TT, (t + 1) * TT)
                    if hm % 2 == 0:
                        wq, wqb = W.take(("mq", t, hm // 2))
                        wq3_cur[0], wq3_cur[1] = v3(wq, KC, 512), wqb
                    wq3, wqb = wq3_cur
                    hh = hm % 2
                    ba = (hm % 2) * 2
                    m3 = mqn2[hm % 2][:, :].rearrange("p (c s) -> p c s", c=2)
                    for dc in range(2):
                        for kc in range(KC):
                            mm(ba + dc, wq3[:, kc, hh * 256 + dc * 128:hh * 256 + (dc + 1) * 128], hT3[:, kc, ts],
                               kc == 0, kc == KC - 1, [wqb, hTb[kc][t]])
                        act(sqm[dc][:, :], ps[ba + dc][:, :], AF.Square, [psb[ba + dc]], [sqmb[dc]])
                        mm(4, ones256, sqm[dc][:, :], dc == 0, dc == 1, [sqmb[dc], b_const], sig=True)
                    rstd_from_ms(4, TT, rsm[:, :], rsmb)
                    for dc in range(2):
                        P.op("dve", lambda e, dc=dc: e.scalar_tensor_tensor(
                            out=m3[:, dc, :], in0=ps[ba + dc][:, :], scalar=pvec[:, PV_GMQ + dc:PV_GMQ + dc + 1],
                            in1=rsm[:, :], op0=ALU.mult, op1=ALU.mult),
                            reads=[psb[ba + dc], rsmb, b_const], writes=[mqnb2[hm % 2]])

                def stageA(t, hm):
                    m3 = mqn2[hm % 2][:, :].rearrange("p (c s) -> p c s", c=2)
                    for mt in range(2):
                        for dc in range(2):
                            mm(5 + mt, mkn4[:, hm, dc, mt * 128:(mt + 1) * 128], m3[:, dc, :], dc == 0, dc == 1,
                               [mknb, mqnb2[hm % 2]])
                        act(EM[mt][:, :], ps[5 + mt][:, :], AF.Exp, [psb[5 + mt]], [EMb[mt]], scale=1.0 / 16.0)
                    for mt in range(2):
                        mm(7, ones1, EM[mt][:, :], mt == 0, mt == 1, [EMb[mt], b_const])
                    for ec in range(2):
                        for mt in range(2):
                            mm(5 + ec, mvT3[:, mt, hm * 256 + ec * 128:hm * 256 + (ec + 1) * 128], EM[mt][:, :],
                               mt == 0, mt == 1, [mvTb, EMb[mt]])
                    act(rl[:, :], ps[7][:, :], AF.Ln, [psb[7]], [rlb])
                    act(rl[:, :], rl[:, :], AF.Exp, [rlb], [rlb], scale=-1.0)
                    for ec in range(2):
                        P.op("dve", lambda e, ec=ec: e.tensor_tensor(
                            out=MO3[:, hm * 2 + ec, :], in0=ps[5 + ec][:, :], in1=rl[:, :], op=ALU.mult),
                            reads=[psb[5 + ec], rlb], writes=[mob])

                for t in range(NT):
                    stageP(t, 0)
                    for hm in range(4):
                        if hm + 1 < 4:
                            stageP(t, hm + 1)
                        stageA(t, hm)
                    branch_merge(("mmem", t), 2, t, MO3, [mob] * 8, 8, mb)
                P.barrier()

        if stage >= 4:
            with ExitStack() as ph:
                set_psum(ph, 7, 1, 0)
                pbKb = pbRb = Buf("pbKR")
                htile = hT[:, 0:4096]
                RT = hT[:, 4096:12288]
                ypt_r = hT[:, 12288:16384]
                htile3 = htile.rearrange("p (k s) -> p k s", k=KC)
                RT3 = RT.rearrange("p (k s) -> p k s", k=16)
                htb = [Buf(f"ht{k}") for k in range(KC)]
                rtb = [Buf(f"rt{k}") for k in range(16)]
                cret = sb("cret_sb", [128, CR_N], F32, ph)
                cretb = Buf("cret")
                posi = sb("posi", [128, TT], I32, ph)
                posb = Buf("posi")
                Tm = [sb(f"Tm{i}", [128, TT], F32, ph) for i in range(4)]
                Tmb = [Buf(f"Tm{i}") for i in range(4)]
                cosT = sb("cosT", [128, TT], F32, ph)
                sinT = sb("sinT", [128, TT], F32, ph)
                csb = Buf("cossin")
                qr = [sb(f"qr{i}", [128, 2 * TT], BF16, ph) for i in range(2)]
                qd = [sb(f"qd{i}", [128, 2 * TT], BF16, ph) for i in range(2)]
                kr = [sb(f"kr{i}", [128, 2 * TT], BF16, ph) for i in range(2)]
                qrb = [Buf(f"qr{i}") for i in range(2)]
                qdb = [Buf(f"qd{i}") for i in range(2)]
                krb = [Buf(f"kr{i}") for i in range(2)]
                kT = [sb(f"kT{i}", [128, 1024], BF16, ph) for i in range(2)]
                kTb = [Buf(f"kT{i}") for i in range(2)]
                v_sb = [sb(f"v_sb{i}", [128, 4 * TT], BF16, ph) for i in range(2)]
                sg_sb = [sb(f"sg_sb{i}", [128, 4 * TT], BF16, ph) for i in range(2)]
                vbb = [[Buf(f"v{i}_{c}") for c in range(4)] for i in range(2)]
                sgbb = [[Buf(f"sg{i}_{c}") for c in range(4)] for i in range(2)]
                state = sb("state", [128, 8 * TT], F32, ph)
                state3 = state[:, :].rearrange("p (a e) -> p a e", a=8)
                stb = [Buf(f"st{a}") for a in range(8)]
                stbf = sb("stbf", [128, 2 * TT], BF16, ph)
                stbf3 = stbf[:, :].rearrange("p (a e) -> p a e", a=2)
                stbb = [Buf(f"stbf{a}") for a in range(2)]
                scD = [sb(f"scD{i}", [128, 128], BF16, ph) for i in range(2)]
                scDb = [Buf(f"scD{i}") for i in range(2)]
                ssq = sb("ssq", [128, 8], F32, ph)
                ssqb = Buf("ssq")
                junk = sb("junk", [128, TT], BF16, ph)
                junkb = Buf("junk")
                ret_sb = [sb(f"ret_sb{i}", [128, TT], BF16, ph) for i in range(2)]
                retb = [Buf(f"ret_sb{i}") for i in range(2)]
                nsq = [sb(f"rnsq{i}", [128, TT], BF16, ph) for i in range(2)]
                nsqb = [Buf(f"rnsq{i}") for i in range(2)]
                nbufs = (nsq, nsqb, Tm[3], Tmb[3])
                mb_r = ([Tm[0], Tm[1]], [Tmb[0], Tmb[1]], ypt_r, Buf("ypt_r"))

                P.dma("sp", cret[:, :], cret_d[:, :], writes=[cretb])

                def rotary_prep(t):
                    ts = slice(t * TT, (t + 1) * TT)
                    P.dma("sp", posi[:, :], pos_d[0:1, ts].broadcast_to([128, TT]), writes=[posb])
                    ang, ta, tb = Tm[0], Tm[1], Tm[2]
                    ki = Tm[3][:, :].bitcast(I32)
                    P.op("dve", lambda e: e.tensor_scalar(out=ang[:, :], in0=posi[:, :], scalar1=pvec[:, PV_INV:PV_INV + 1],
                                                          scalar2=None, op0=ALU.mult),
                         reads=[posb, b_const], writes=[Tmb[0]])
                    for which, dstT in ((0, sinT), (1, cosT)):
                        if which == 1:
                            P.op("dve", lambda e: e.tensor_scalar(out=ang[:, :], in0=ang[:, :], scalar1=math.pi / 2.0,
                                                                  scalar2=None, op0=ALU.add),
                                 reads=[Tmb[0]], writes=[Tmb[0]])
                        P.op("dve", lambda e: e.tensor_scalar(out=ki, in0=ang[:, :], scalar1=1.0 / TWO_PI,
                                                              scalar2=None, op0=ALU.mult),
                             reads=[Tmb[0]], writes=[Tmb[3]])
                        P.op("dve", lambda e: e.scalar_tensor_tensor(out=ta[:, :], in0=ki, scalar=-CW1, in1=ang[:, :],
                                                                     op0=ALU.mult, op1=ALU.add),
                             reads=[Tmb[3], Tmb[0]], writes=[Tmb[1]])
                        P.op("dve", lambda e: e.scalar_tensor_tensor(out=ta[:, :], in0=ki, scalar=-CW2, in1=ta[:, :],
                                                                     op0=ALU.mult, op1=ALU.add),
                             reads=[Tmb[3], Tmb[1]], writes=[Tmb[1]])
                        P.op("dve", lambda e: e.tensor_scalar(out=tb[:, :], in0=ta[:, :], scalar1=math.pi, scalar2=None,
                                                              op0=ALU.is_gt), reads=[Tmb[1]], writes=[Tmb[2]])
                        P.op("dve", lambda e: e.scalar_tensor_tensor(out=ta[:, :], in0=tb[:, :], scalar=-TWO_PI, in1=ta[:, :],
                                                                     op0=ALU.mult, op1=ALU.add),
                             reads=[Tmb[2], Tmb[1]], writes=[Tmb[1]])
                        P.op("dve", lambda e: e.tensor_scalar(out=tb[:, :], in0=ta[:, :], scalar1=-math.pi, scalar2=None,
                                                              op0=ALU.is_lt), reads=[Tmb[1]], writes=[Tmb[2]])
                        P.op("dve", lambda e: e.scalar_tensor_tensor(out=ta[:, :], in0=tb[:, :], scalar=TWO_PI, in1=ta[:, :],
                                                                     op0=ALU.mult, op1=ALU.add),
                             reads=[Tmb[2], Tmb[1]], writes=[Tmb[1]])
                        P.op("dve", lambda e: e.tensor_scalar(out=ta[:, :], in0=ta[:, :], scalar1=-PI_SAFE, scalar2=PI_SAFE,
                                                              op0=ALU.max, op1=ALU.min), reads=[Tmb[1]], writes=[Tmb[1]])
                        act(dstT[:, :], ta[:, :], AF.Sin, [Tmb[1]], [csb])

                def phase_i(t, h):
                    pp = h % 2
                    qr3 = qr[pp][:, :].rearrange("p (c s) -> p c s", c=2)
                    qd3 = qd[pp][:, :].rearrange("p (c s) -> p c s", c=2)
                    kr3 = kr[pp][:, :].rearrange("p (c s) -> p c s", c=2)
                    v3_ = v_sb[pp][:, :].rearrange("p (c e) -> p c e", c=4)
                    sg3_ = sg_sb[pp][:, :].rearrange("p (c e) -> p c e", c=4)
                    wqk, wqkb = W.take(("rqk", t, h))
                    wqk3 = v3(wqk, KC, 512)
                    qdec_b = cret[:, CR_QDEC + h * 128:CR_QDEC + (h + 1) * 128].rearrange(
                        "p (o d) -> p o d", o=1).broadcast_to([128, 4, 128])
                    for which in range(2):
                        for dc in range(2):
                            for kc in range(KC):
                                mm(dc, wqk3[:, kc, which * 256 + dc * 128:which * 256 + (dc + 1) * 128],
                                   htile3[:, kc, :], kc == 0, kc == KC - 1, [wqkb, htb[kc]])
                            yield
                        for half in range(2):
                            c1, c2 = (cosT, sinT) if half == 0 else (sinT, cosT)
                            P.op("dve", lambda e, c1=c1: e.tensor_tensor(out=Tm[0][:, :], in0=ps[0][:, :], in1=c1[:, :],
                                                                         op=ALU.mult),
                                 reads=[psb[0], csb], writes=[Tmb[0]])
                            P.op("dve", lambda e, c2=c2: e.tensor_tensor(out=Tm[1][:, :], in0=ps[1][:, :], in1=c2[:, :],
                                                                         op=ALU.mult),
                                 reads=[psb[1], csb], writes=[Tmb[1]])
                            op_ = ALU.subtract if half == 0 else ALU.add
                            if which == 0:
                                P.op("dve", lambda e, op_=op_: e.tensor_tensor(out=Tm[2][:, :], in0=Tm[0][:, :],
                                                                               in1=Tm[1][:, :], op=op_),
                                     reads=[Tmb[0], Tmb[1]], writes=[Tmb[2]])
                                act(qr3[:, half, :], Tm[2][:, :], AF.Copy, [Tmb[2]], [qrb[pp]])
                                P.op("dve", lambda e, half=half: e.tensor_tensor(
                                    out=qd3[:, half, :].rearrange("p (c d) -> p c d", c=4),
                                    in0=Tm[2][:, :].rearrange("p (c d) -> p c d", c=4), in1=qdec_b, op=ALU.mult),
                                    reads=[Tmb[2], cretb], writes=[qdb[pp]])
                            else:
                                P.op("dve", lambda e, op_=op_, half=half: e.tensor_tensor(
                                    out=kr3[:, half, :], in0=Tm[0][:, :], in1=Tm[1][:, :], op=op_),
                                    reads=[Tmb[0], Tmb[1]], writes=[krb[pp]])
                            yield
                    for hf in range(2):
                        for i4 in range(4):
                            idx = hf * 4 + i4
                            c, dc = idx // 2, idx % 2
                            P.op("pe", lambda e, c=c, dc=dc, i4=i4: e.transpose(
                                pb[0][:, i4 * 128:(i4 + 1) * 128], kr3[:, dc, c * 128:(c + 1) * 128], identb),
                                reads=[krb[pp], b_const], writes=[pbKb], signal=(i4 == 3))
                        P.op("dve", lambda e, hf=hf: e.tensor_scalar(
                            out=kT[pp][:, hf * 512:(hf + 1) * 512], in0=pb[0][:, 0:512],
                            scalar1=pvec[:, PV_KDEC + h:PV_KDEC + h + 1], scalar2=None, op0=ALU.mult),
                            reads=[pbKb, b_const], writes=[kTb[pp]])
                        yield
                    wv, wvb = W.take(("rv", t, h))
                    wv3 = v3(wv, KC, 512)
                    for c in range(4):
                        bk = (c + 2) % 3
                        for kc in range(KC):
                            mm(bk, htile3[:, kc, c * 128:(c + 1) * 128], wv3[:, kc, :], kc == 0, kc == KC - 1,
                               [wvb, htb[kc]])
                        act(v3_[:, c, :], ps[bk][:, :], AF.Copy, [psb[bk]], [vbb[pp][c]])
                        yield
                    wg, wgb = W.take(("rg", t, h))
                    wg3 = v3(wg, KC, 512)
                    for c in range(4):
                        bk = (c + 0) % 3
                        for kc in range(KC):
                            mm(bk, htile3[:, kc, c * 128:(c + 1) * 128], wg3[:, kc, :], kc == 0, kc == KC - 1,
                               [wgb, htb[kc]])
                        act(Tm[3][:, :], ps[bk][:, :], AF.Exp, [psb[bk]], [Tmb[3]], scale=-1.0)
                        act(Tm[3][:, :], Tm[3][:, :], AF.Ln, [Tmb[3]], [Tmb[3]], bias=onec[:, 0:1])
                        act(Tm[3][:, :], Tm[3][:, :], AF.Exp, [Tmb[3]], [Tmb[3]], scale=-1.0)
                        P.op("dve", lambda e, c=c, bk=bk: e.tensor_tensor(out=sg3_[:, c, :], in0=ps[bk][:, :], in1=Tm[3][:, :],
                                                                           op=ALU.mult),
                             reads=[psb[bk], Tmb[3]], writes=[sgbb[pp][c]])
                        yield

                it_cnt = [0]

                def phase_ii(t, h):
                    pp = h % 2
                    qr3 = qr[pp][:, :].rearrange("p (c s) -> p c s", c=2)
                    qd3 = qd[pp][:, :].rearrange("p (c s) -> p c s", c=2)
                    kr3 = kr[pp][:, :].rearrange("p (c s) -> p c s", c=2)
                    v3_ = v_sb[pp][:, :].rearrange("p (c e) -> p c e", c=4)
                    sg3_ = sg_sb[pp][:, :].rearrange("p (c e) -> p c e", c=4)
                    if t > 0:
                        for dc in range(2):
                            act(stbf3[:, dc, :], state3[:, h * 2 + dc, :], AF.Copy, [stb[h * 2 + dc]], [stbb[dc]])

                    def tail_norm(c, col):
                        act(junk[:, :], ps[4][:, :], AF.Square, [psb[4]], [junkb, ssqb],
                            accum_out=ssq[:, col:col + 1])
                        act(ssq[:, col:col + 1], ssq[:, col:col + 1], AF.Ln, [ssqb, b_const], [ssqb],
                            scale=1.0 / 512.0, bias=epsc[:, 0:1])
                        act(ssq[:, col:col + 1], ssq[:, col:col + 1], AF.Exp, [ssqb], [ssqb], scale=-0.5)
                        r_, rb_ = ret_sb[c % 2], retb[c % 2]
                        P.op("dve", lambda e: e.scalar_tensor_tensor(
                            out=r_[:, :], in0=ps[4][:, :], scalar=ssq[:, col:col + 1], in1=sg3_[:, c, :],
                            op0=ALU.mult, op1=ALU.mult),
                            reads=[psb[4], ssqb, sgbb[pp][c]], writes=[rb_])

                    def tail_xpose(c):
                        cs = slice(c * 128, (c + 1) * 128)
                        r_, rb_ = ret_sb[c % 2], retb[c % 2]
                        for ec in range(4):
                            P.op("pe", lambda e, ec=ec: e.transpose(
                                pb[0][:, 512 + ec * 128:512 + (ec + 1) * 128], r_[:, ec * 128:(ec + 1) * 128], identb),
                                reads=[rb_, b_const], writes=[pbRb], signal=(ec == 3))
                        act(RT3[:, h * 4:(h + 1) * 4, cs], pb[0][:, 512:1024].rearrange("p (a b) -> p a b", a=4), AF.Copy,
                            [pbRb], [rtb[h * 4 + k] for k in range(4)])

                    for c in range(4):
                        gchunk = t * 4 + c
                        cs = slice(c * 128, (c + 1) * 128)
                        first = gchunk == 0
                        for dc in range(2):
                            mm(3, kr3[:, dc, cs], qr3[:, dc, cs], dc == 0, dc == 1, [krb[pp], qrb[pp]], cols=slice(0, 128))
                        if gchunk < 15:
                            for dc in range(2):
                                mm(5 + dc, kT[pp][:, c * 256 + dc * 128:c * 256 + (dc + 1) * 128], v3_[:, c, :], True, True,
                                   [kTb[pp], vbb[pp][c]])
                        sd, sdb = scD[c % 2], scDb[c % 2]
                        P.op("dve", lambda e, sd=sd: e.tensor_tensor(
                            out=sd[:, :], in0=ps[3][:, 0:128], in1=cret[:, CR_DEC + h * 128:CR_DEC + (h + 1) * 128],
                            op=ALU.mult), reads=[psb[3], cretb], writes=[sdb])
                        yield
                        mm(4, sd[:, :], v3_[:, c, :], True, first, [sdb, vbb[pp][c]], sig=True)
                        if not first:
                            for dc in range(2):
                                mm(4, qd3[:, dc, cs], stbf3[:, dc, :], False, dc == 1, [qdb[pp], stbb[dc]], sig=True)
                        if c > 0:
                            tail_xpose(c - 1)
                        yield
                        if gchunk < 15:
                            for dc in range(2):
                                a = h * 2 + dc
                                if first:
                                    P.op("dve", lambda e, a=a, dc=dc: e.tensor_copy(out=state3[:, a, :], in_=ps[5 + dc][:, :]),
                                         reads=[psb[5 + dc]], writes=[stb[a]])
                                else:
                                    P.op("dve", lambda e, a=a, dc=dc: e.scalar_tensor_tensor(
                                        out=state3[:, a, :], in0=state3[:, a, :], scalar=float(GAMMA[h] ** 128),
                                        in1=ps[5 + dc][:, :], op0=ALU.mult, op1=ALU.add),
                                        reads=[psb[5 + dc], stb[a]], writes=[stb[a]])
                                if c < 3:
                                    act(stbf3[:, dc, :], state3[:, a, :], AF.Copy, [stb[a]], [stbb[dc]])
                        tail_norm(c, it_cnt[0] % 8)
                        it_cnt[0] += 1
                        yield
                    tail_xpose(3)
                    yield

                def run_both(g_main, g_fill):
                    a_done = b_done = False
                    k = 0
                    while not (a_done and b_done):
                        if not a_done:
                            try:
                                next(g_main)
                            except StopIteration:
                                a_done = True
                        for _ in range(2 if k % 2 == 0 else 1):
                            if not b_done:
                                try:
                                    next(g_fill)
                                except StopIteration:
                                    b_done = True
                        k += 1

                def run_one(g):
                    for _ in g:
                        pass

                rotary_prep(0)
                for t in range(NT):
                    rmsnorm_to_hT(PV_GMIX, ph, tiles=[t], dst=(htile3, htb), nbufs=nbufs)
                    run_one(phase_i(t, 0))
                    for h in range(4):
                        if h + 1 < 4:
                            run_both(phase_ii(t, h), phase_i(t, h + 1))
                        else:
                            run_one(phase_ii(t, h))
                    if t + 1 < NT:
                        rotary_prep(t + 1)
                    branch_merge(("mret", t), 0, t, RT3, rtb, 16, mb_r, h3t=htile3, hbt=htb)
                P.barrier()

        if stage >= 5:
            with ExitStack() as ph:
                set_psum(ph)
                wo_sb = sb("wo_sb", [128, KC * D], BF16, ph)
                wob = Buf("wo")
                wo_sb3 = wo_sb[:, :].rearrange("p (k n) -> p k n", k=KC)
                P.dma("pool", wo_sb3, wo_d.rearrange("(k p) n -> p k n", p=128), writes=[wob])
                ypl = [sb(f"ypl{b}", [128, KC * TT], BF16, ph) for b in range(3)]
                yplb = [Buf(f"ypl{b}") for b in range(3)]
                Y = sb("Ysum", [128, KC * TT], BF16, ph)
                Y3 = Y[:, :].rearrange("p (k s) -> p k s", k=KC)
                Yb = [Buf(f"Y{k}") for k in range(KC)]
                tsum = [sb(f"tsum{i}", [128, TT], F32, ph) for i in range(2)]
                tsumb = [Buf(f"tsum{i}") for i in range(2)]
                for t in range(NT):
                    ts = slice(t * TT, (t + 1) * TT)
                    for b in range(3):
                        P.dma("sp", ypl[b][:, :].rearrange("p (k s) -> p k s", k=KC),
                              yp_d[b, :, :, ts].rearrange("k p s -> p k s"), reads=[ypdb[b][t]], writes=[yplb[b]])
                    for k in range(KC):
                        ks = slice(k * TT, (k + 1) * TT)
                        tt_, ttb_ = tsum[k % 2], tsumb[k % 2]
                        P.op("dve", lambda e, ks=ks, tt_=tt_: e.tensor_tensor(out=tt_[:, :], in0=ypl[0][:, ks], in1=ypl[1][:, ks],
                                                                              op=ALU.add),
                             reads=[yplb[0], yplb[1]], writes=[ttb_])
                        P.op("dve", lambda e, ks=ks, tt_=tt_, k=k: e.tensor_tensor(out=Y3[:, k, :], in0=tt_[:, :], in1=ypl[2][:, ks],
                                                                                   op=ALU.add),
                             reads=[ttb_, yplb[2]], writes=[Yb[k]])
                    for c in range(KC):
                        po = 4 + (c % 2)
                        for k in range(KC):
                            mm(po, wo_sb3[:, k, c * 128:(c + 1) * 128], Y3[:, k, :], k == 0, k == KC - 1, [wob, Yb[k]])
                        P.op("dve", lambda e, c=c, po=po: e.tensor_tensor(out=xT3[:, c, ts], in0=ps[po][:, :], in1=xT3[:, c, ts],
                                                                          op=ALU.add),
                             reads=[psb[po], xTb[c][t]], writes=[xTb[c][t]])
                P.barrier()

        if stage >= 6:
            ffn("ffn2", PV_GFFN2, do_store=True)

        if stage < 6:
            with ExitStack() as ph:
                set_psum(ph, 8, 0, 0)
                if stage < 1:
                    load_x(ph)
                xo = [sb(f"xo{i}", [128, D], F32, ph) for i in range(2)]
                xob = [Buf(f"xo{i}") for i in range(2)]
                for t in range(NT):
                    store_tile(t, xo, xob)
        P.finish(out_toks)
        build_program.stats = (P.n_inst, P.n_wait)
    return nc


def _consts():
    cf32 = np.zeros((128, CF_N), dtype=np.float32)
    cf32[:, CF_ID:CF_ID + 128] = np.eye(128)
    cret = np.zeros((128, CR_N), dtype=np.float32)
    idx = np.arange(128, dtype=np.float64)
    for h in range(4):
        g = GAMMA[h]
        dist = idx[None, :] - idx[:, None]
        dec = np.where(dist >= 0, g ** np.maximum(dist, 0.0), 0.0) / 16.0
        cret[:, CR_DEC + h * 128:CR_DEC + (h + 1) * 128] = dec
        qd = g ** (idx + 1.0)
        cret[:, CR_QDEC + h * 128:CR_QDEC + (h + 1) * 128] = qd[None, :]
    cbf = np.zeros((128, CB_N), dtype=np.float32)
    cbf[:, CB_ID:CB_ID + 128] = np.eye(128)
    cbf[:, CB_O1024:CB_O1024 + 128] = 1.0 / 1024.0
    cbf[0:64, CB_BD64:CB_BD64 + 64] = 1.0 / 64.0
    cbf[64:128, CB_BD64 + 64:CB_BD64 + 128] = 1.0 / 64.0
    cbf[:, CB_O128:CB_O128 + 128] = 1.0 / 128.0
    cbf[:, CB_O256:CB_O256 + 128] = 1.0 / 256.0
    cbf[:, CB_ONE:CB_ONE + 128] = 1.0
    cbf[:, CB_TRI:CB_TRI + 128] = (idx[None, :] >= idx[:, None]).astype(np.float32)
    return cf32, cret, cbf.astype(ml_dtypes.bfloat16)


def _pvec(inp):
    pv = np.zeros((128, 64), dtype=np.float32)

    def fm(v):
        return np.ascontiguousarray(np.asarray(v, dtype=np.float32).reshape(-1, 128).T)

    pv[:, PV_GFFN1:PV_GFFN1 + 8] = fm(inp["g_ffn1"][0])
    pv[:, PV_GMIX:PV_GMIX + 8] = fm(inp["g_mix"][0])
    pv[:, PV_GFFN2:PV_GFFN2 + 8] = fm(inp["g_ffn2"][0])
    pv[:, PV_GMEM:PV_GMEM + 8] = fm(inp["g_mem"][0])
    pv[:, PV_GDQ] = np.tile(np.asarray(inp["g_diff_q"][0], dtype=np.float32), 2)
    pv[:, PV_GDK] = np.tile(np.asarray(inp["g_diff_k"][0], dtype=np.float32), 2)
    pv[:, PV_GDO] = np.asarray(inp["g_diff_out"][0], dtype=np.float32)
    pv[:, PV_GMQ:PV_GMQ + 2] = fm(inp["g_mem_q"][0])
    pv[:, PV_GMK:PV_GMK + 2] = fm(inp["g_mem_k"][0])
    idx = np.arange(128, dtype=np.float64)
    pv[:, PV_INV] = (10000.0 ** (-idx / 128.0)).astype(np.float32)
    for h in range(4):
        pv[:, PV_KDEC + h] = (GAMMA[h] ** (127.0 - idx) / 16.0).astype(np.float32)
    return pv


_SHARED_KEYS = ("w_ffn1_in", "w_ffn1_out", "w_ffn2_in", "w_ffn2_out", "w_in", "w_mem_kv",
                "w_br_ret", "w_br_diff", "w_br_mem", "w_o")


def make_in_maps(inp, ncores=NCORES):
    cf32, cret, cbf = _consts()
    shared = {k: np.ascontiguousarray(inp[k][0]) for k in _SHARED_KEYS}
    shared["cret"] = cret
    shared["pvec"] = _pvec(inp)
    shared["cf32"] = cf32
    shared["cbf"] = cbf
    shared["lamv"] = np.ascontiguousarray(np.concatenate(
        [inp["lam_q1"][0], inp["lam_k1"][0], inp["lam_q2"][0], inp["lam_k2"][0]]).astype(np.float32)[None, :])
    maps = []
    for c in range(ncores):
        m = dict(shared)
        m["x"] = np.ascontiguousarray(inp["x"][c])
        m["mem"] = np.ascontiguousarray(inp["mem"][c])
        m["positions"] = np.ascontiguousarray(inp["positions"][c][None, :].astype(np.int32))
        maps.append(m)
    return maps


_NC_CACHE = {}


def kernel(**inputs):
    inp = {k: np.asarray(v) for k, v in inputs.items()}
    if "nc" not in _NC_CACHE:
        _NC_CACHE["nc"] = build_program()
    nc = _NC_CACHE["nc"]
    in_maps = make_in_maps(inp)
    res = run_bass_kernel_spmd(nc, in_maps, core_ids=list(range(NCORES)))
    out = np.stack([np.asarray(r["out"], dtype=np.float32) for r in res.results], axis=0)
    return out
```

```python
import math
from contextlib import ExitStack

import numpy as np
import ml_dtypes

import concourse.bass as bass
import concourse.mybir as mybir
from concourse.bass_utils import run_bass_kernel_spmd

F32 = mybir.dt.float32
BF16 = mybir.dt.bfloat16
I32 = mybir.dt.int32
AF = mybir.ActivationFunctionType
ALU = mybir.AluOpType

S = 2048
D = 1024
KC = D // 128
DFF = 2816
NFF = DFF // 128
TT = 512
NT = S // TT
EPS = 1e-6
NCORES = 8
STAGE = 99


class Buf:
    __slots__ = ("name", "last_w", "readers")

    def __init__(self, name):
        self.name = name
        self.last_w = None
        self.readers = []


ENG_NAMES = ("pe", "act", "dve", "pool", "sp")
NDMA_SEM = 20


class Prog:
    def __init__(self, nc, es, same_engine_sync=True):
        self.nc = nc
        self.same_engine_sync = same_engine_sync
        self.fuse_waits = True
        self.engs = {"pe": nc.tensor, "act": nc.scalar, "dve": nc.vector,
                     "pool": nc.gpsimd, "sp": nc.sync}
        self.eid = {n: i for i, n in enumerate(ENG_NAMES)}
        self.sems = []
        for n in ENG_NAMES:
            self.sems.append(es.enter_context(nc.semaphore("s_" + n)))
        self.dma_sem_ids = {}
        for q in ("sp", "pool"):
            ids = []
            for i in range(NDMA_SEM):
                ids.append(len(self.sems))
                self.sems.append(es.enter_context(nc.semaphore(f"d_{q}{i}")))
            self.dma_sem_ids[q] = ids
        self.nclk = len(self.sems)
        self.count = [0] * self.nclk
        self.clk = {n: [0] * self.nclk for n in ENG_NAMES}
        self.snap = {}
        self.dma_rr = {"sp": 0, "pool": 0}
        self.n_wait = 0
        self.n_inst = 0

    def _need(self, ename, tok):
        sid, val = tok
        c = self.clk[ename]
        if c[sid] >= val:
            return False
        if sid == self.eid.get(ename, -1):
            if ename == "pe" or not self.same_engine_sync:
                return False
        assert val <= self.count[sid], f"wait for unsignalled token {tok} (count {self.count[sid]})"
        sn = self.snap.get(tok)
        if sn is not None:
            for i in range(self.nclk):
                if sn[i] > c[i]:
                    c[i] = sn[i]
        if c[sid] < val:
            c[sid] = val
        return True

    def _wait(self, ename, tok):
        if self._need(ename, tok):
            self.engs[ename].wait_ge(self.sems[tok[0]], tok[1])
            self.n_wait += 1

    def _deps(self, reads, writes):
        deps = []
        for b in reads:
            if b.last_w is not None:
                deps.append(b.last_w)
        for b in writes:
            if b.last_w is not None:
                deps.append(b.last_w)
            deps.extend(b.readers)
        return deps

    def _record(self, tok, reads, writes):
        for b in reads:
            b.readers.append(tok)
        for b in writes:
            b.last_w = tok
            b.readers = []

    def op(self, ename, fn, reads=(), writes=(), signal=True, fuse=True):
        deps = self._deps(reads, writes)
        deps.sort(key=lambda t: -t[1])
        need = [tok for tok in deps if self._need(ename, tok)]
        fused = None
        if need and fuse and self.fuse_waits and ename in ("act", "dve"):
            fused = need.pop()
        for tok in need:
            self.engs[ename].wait_ge(self.sems[tok[0]], tok[1])
            self.n_wait += 1
        ins = fn(self.engs[ename])
        if fused is not None:
            ins._wait_ge(self.sems[fused[0]], fused[1])
        sid = self.eid[ename]
        self.n_inst += 1
        if signal:
            ins.then_inc(self.sems[sid], 1)
            self.count[sid] += 1
            tok = (sid, self.count[sid])
            self.snap[tok] = list(self.clk[ename])
        else:
            tok = (sid, self.count[sid] + 1)
        self._record(tok, reads, writes)
        return tok

    def dma(self, q, out, in_, reads=(), writes=()):
        for tok in self._deps(reads, writes):
            self._wait(q, tok)
        ids = self.dma_sem_ids[q]
        sid = ids[self.dma_rr[q] % NDMA_SEM]
        self.dma_rr[q] += 1
        if self.count[sid] > 0:
            self._wait(q, (sid, self.count[sid]))
        self.engs[q].dma_start(out=out, in_=in_).then_inc(self.sems[sid], 16)
        self.n_inst += 1
        self.count[sid] += 16
        tok = (sid, self.count[sid])
        self.snap[tok] = list(self.clk[q])
        self._record(tok, reads, writes)
        return tok

    def barrier(self):
        for ename in ENG_NAMES:
            for sid in range(self.nclk):
                if self.count[sid] > 0:
                    own = sid == self.eid[ename]
                    if own and ename in ("pe", "sp"):
                        continue
                    c = self.clk[ename]
                    if c[sid] < self.count[sid]:
                        self.engs[ename].wait_ge(self.sems[sid], self.count[sid])
                        c[sid] = self.count[sid]
                        self.n_wait += 1

    def finish(self, toks):
        for tok in toks:
            self._wait("sp", tok)


class WStream:
    HOLD = 2

    def __init__(self, P, bufs, bufobjs):
        self.P = P
        self.bufs = bufs
        self.bobj = bufobjs
        self.plan = []
        self.issued = 0
        self.taken = 0

    def add(self, tag, parts):
        self.plan.append((tag, parts))

    def _issue(self, i):
        tag, parts = self.plan[i]
        k = i % len(self.bufs)
        for dst_fn, src in parts:
            self.P.dma("pool", dst_fn(self.bufs[k]), src, writes=[self.bobj[k]])

    def take(self, tag):
        i = self.taken
        assert self.plan[i][0] == tag, (self.plan[i][0], tag)
        ahead = len(self.bufs) - self.HOLD
        while self.issued < min(len(self.plan), i + 1 + ahead):
            self._issue(self.issued)
            self.issued += 1
        self.taken += 1
        k = i % len(self.bufs)
        return self.bufs[k], self.bobj[k]


def v3(t, a, b):
    return t[:, 0:a * b].rearrange("p (a b) -> p a b", a=a)


INW = 13312
OFF_RQ, OFF_RK, OFF_RV, OFF_RG = 0, 1024, 2048, 4096
OFF_DQ, OFF_DK, OFF_DV, OFF_MQ, OFF_GT = 6144, 7168, 8192, 9216, 10240
LAM_INIT = 0.8 - 0.6 * math.exp(0.0)
GAMMA = [1.0 - 2.0 ** (-5.0 - h) for h in range(4)]
TWO_PI = 2.0 * math.pi
CW1 = 6.28125
CW2 = TWO_PI - CW1
PI_SAFE = 3.1415925

PV_GFFN1, PV_GMIX, PV_GFFN2, PV_GMEM = 0, 8, 16, 24
PV_GDQ, PV_GDK, PV_GDO, PV_GMQ, PV_GMK, PV_INV, PV_KDEC = 32, 33, 34, 35, 37, 39, 40
CF_ID, CF_N = 0, 128
CR_DEC, CR_QDEC, CR_N = 0, 512, 1024
CB_ID, CB_O1024, CB_BD64, CB_O128, CB_O256, CB_ONE, CB_TRI, CB_N = 0, 128, 256, 384, 512, 640, 768, 896


def build_program(stage=STAGE, same_engine_sync=True, debug=False):
    nc = bass.Bass("TRN2", target_bir_lowering=False)
    es = ExitStack()
    with es:
        def din(name, shape, dt=F32):
            return nc.dram_tensor(name, list(shape), dt, kind="ExternalInput").ap()

        def dscratch(name, shape, dt):
            kind = "ExternalOutput" if debug else "Internal"
            return nc.dram_tensor(name, list(shape), dt, kind=kind).ap()

        x_d = din("x", [S, D])
        mem_d = din("mem", [256, D])
        pos_d = din("positions", [1, S], I32)
        w1a_d = din("w_ffn1_in", [D, 2 * DFF])
        w1b_d = din("w_ffn1_out", [DFF, D])
        w2a_d = din("w_ffn2_in", [D, 2 * DFF])
        w2b_d = din("w_ffn2_out", [DFF, D])
        win_d = din("w_in", [D, INW])
        wkv_d = din("w_mem_kv", [D, 2048])
        wbr_d = din("w_br_ret", [2048, D])
        wbd_d = din("w_br_diff", [D, D])
        wbm_d = din("w_br_mem", [D, D])
        wo_d = din("w_o", [D, D])
        pvec_d = din("pvec", [128, 64])
        lamv_d = din("lamv", [1, 256])
        cf32_d = din("cf32", [128, CF_N])
        cret_d = din("cret", [128, CR_N])
        cbf_d = din("cbf", [128, CB_N], BF16)
        out_d = nc.dram_tensor("out", [S, D], F32, kind="ExternalOutput").ap()
        yp_d = dscratch("yp", [3, KC, 128, S], BF16)
        ypdb = [[Buf(f"ypd{b}_{t}") for t in range(4)] for b in range(3)]

        win3 = win_d.rearrange("(k p) n -> p k n", p=128)

        P = Prog(nc, es, same_engine_sync=same_engine_sync)

        def sb(name, shape, dt, st=es):
            return st.enter_context(nc.sbuf_tensor(name, list(shape), dt))

        xT = sb("xT", [128, KC * S], F32)
        hT = sb("hT", [128, KC * S], BF16)
        pvec = sb("pvec_sb", [128, 64], F32)
        cf32 = sb("cf32_sb", [128, CF_N], F32)
        cbf = sb("cbf_sb", [128, CB_N], BF16)
        epsc = sb("epsc", [128, 1], F32)
        onec = sb("onec", [128, 1], F32)
        NWB = 4
        wbufs = [sb(f"wbuf{i}", [128, 4096], BF16) for i in range(NWB)]
        wbobj = [Buf(f"wbuf{i}") for i in range(NWB)]
        W = WStream(P, wbufs, wbobj)

        xT3 = xT[:, :].rearrange("p (k s) -> p k s", k=KC)
        hT3 = hT[:, :].rearrange("p (k s) -> p k s", k=KC)
        xTb = [[Buf(f"xT{k}_{t}") for t in range(NT)] for k in range(KC)]
        hTb = [[Buf(f"hT{k}_{t}") for t in range(NT)] for k in range(KC)]
        b_const = Buf("const")

        identf = cf32[:, CF_ID:CF_ID + 128]
        identb = cbf[:, CB_ID:CB_ID + 128]
        ones_d = cbf[:, CB_O1024:CB_O1024 + 128]
        bd64 = cbf[:, CB_BD64:CB_BD64 + 128]
        ones128 = cbf[:, CB_O128:CB_O128 + 128]
        ones256 = cbf[:, CB_O256:CB_O256 + 128]
        ones1 = cbf[:, CB_ONE:CB_ONE + 128]
        tri = cbf[:, CB_TRI:CB_TRI + 128]

        ps, psb, pb, pbb, pw, pwb = [], [], [], [], [], []
        psum_ctr = [0]

        def set_psum(ph, n_single=6, n_bf=2, n_wide=0):
            k = psum_ctr[0]
            psum_ctr[0] += 1
            ps[:] = [ph.enter_context(nc.psum_tensor(f"ps{k}_{i}", [128, 512], F32)) for i in range(n_single)]
            psb[:] = [Buf(f"ps{i}") for i in range(n_single)]
            pb[:] = [ph.enter_context(nc.psum_tensor(f"pb{k}_{i}", [128, 1024], BF16)) for i in range(n_bf)]
            pbb[:] = [Buf(f"pb{i}") for i in range(n_bf)]
            pw[:] = [ph.enter_context(nc.psum_tensor(f"pw{k}_{i}", [128, 1024], F32)) for i in range(n_wide)]
            pwb[:] = [Buf(f"pwh{i}") for i in range(2 * n_wide)]

        def mm(psi, lhsT, rhs, start, stop, reads, sig=None, cols=None):
            out = ps[psi][:, :] if cols is None else ps[psi][:, cols]
            P.op("pe", lambda e: e.matmul(out, lhsT=lhsT, rhs=rhs, start=start, stop=stop),
                 reads=reads, writes=[psb[psi]], signal=(stop if sig is None else sig))

        def mm2(out, outbuf, lhsT, rhs, start, stop, reads, sig=None):
            P.op("pe", lambda e: e.matmul(out, lhsT=lhsT, rhs=rhs, start=start, stop=stop),
                 reads=reads, writes=[outbuf], signal=(stop if sig is None else sig))

        def act(out, in_, func, reads, writes, **kw):
            P.op("act", lambda e: e.activation(out=out, in_=in_, func=func, **kw), reads=reads, writes=writes,
                 fuse=("accum_out" not in kw))

        def rstd_from_ms(psi, n, rstd_ap, rstd_buf, prange=slice(0, 128)):
            act(rstd_ap, ps[psi][prange, 0:n], AF.Ln, [psb[psi], b_const], [rstd_buf], bias=epsc[prange, 0:1])
            act(rstd_ap, rstd_ap, AF.Exp, [rstd_buf], [rstd_buf], scale=-0.5)

        def plan_ffn(tag, wa, wb):
            wa3 = wa.rearrange("(k p) n -> p k n", p=128)
            wb3 = wb.rearrange("(j p) n -> p j n", p=128)
            for b in range(NFF // 2):
                W.add((tag, "a", b), [
                    (lambda t: v3(t, KC, 512)[:, :, 0:256], wa3[:, :, b * 256:(b + 1) * 256]),
                    (lambda t: v3(t, KC, 512)[:, :, 256:512],
                     wa3[:, :, DFF + b * 256:DFF + (b + 1) * 256]),
                ])
                W.add((tag, "b", b), [
                    (lambda t: v3(t, 2, 1024), wb3[:, 2 * b:2 * b + 2, :]),
                ])

        def blk_in(c0):
            return [(lambda t: v3(t, KC, 512), win3[:, :, c0:c0 + 512])]

        def plan_merge(tag, wsrc, nk, gate_off):
            w3 = wsrc.rearrange("(k p) n -> p k n", p=128)
            for cb in range(4):
                W.add((tag, "w", cb), [(lambda t, nk=nk: v3(t, nk, 256), w3[:, :, cb * 256:(cb + 1) * 256])])
                if cb % 2 == 0:
                    W.add((tag, "g", cb // 2), blk_in(gate_off + (cb // 2) * 512))

        plan_ffn("ffn1", w1a_d, w1b_d)
        if stage >= 2:
            for g4 in range(2):
                W.add(("dq", g4), blk_in(OFF_DQ + g4 * 512))
                W.add(("dk", g4), blk_in(OFF_DK + g4 * 512))
                W.add(("dv", g4), blk_in(OFF_DV + g4 * 512))
            for t in range(NT):
                plan_merge(("mdiff", t), wbd_d, 8, OFF_GT + 1024)
        if stage >= 3:
            wkv3 = wkv_d.rearrange("(k p) n -> p k n", p=128)
            for i in range(4):
                W.add(("mkv", i), [(lambda t: v3(t, KC, 512), wkv3[:, :, i * 512:(i + 1) * 512])])
            for t in range(NT):
                for i in range(2):
                    W.add(("mq", t, i), blk_in(OFF_MQ + i * 512))
                plan_merge(("mmem", t), wbm_d, 8, OFF_GT + 2048)
        if stage >= 4:
            for t in range(NT):
                for h in range(4):
                    W.add(("rqk", t, h), [
                        (lambda tt: v3(tt, KC, 512)[:, :, 0:256], win3[:, :, OFF_RQ + h * 256:OFF_RQ + (h + 1) * 256]),
                        (lambda tt: v3(tt, KC, 512)[:, :, 256:512], win3[:, :, OFF_RK + h * 256:OFF_RK + (h + 1) * 256]),
                    ])
                    W.add(("rv", t, h), blk_in(OFF_RV + h * 512))
                    W.add(("rg", t, h), blk_in(OFF_RG + h * 512))
                plan_merge(("mret", t), wbr_d, 16, OFF_GT)
        if stage >= 6:
            plan_ffn("ffn2", w2a_d, w2b_d)

        P.dma("sp", pvec[:, :], pvec_d[:, :], writes=[b_const])
        P.dma("sp", cf32[:, :], cf32_d[:, :], writes=[b_const])
        P.dma("sp", cbf[:, :], cbf_d[:, :], writes=[b_const])
        P.op("dve", lambda e: e.memset(epsc[:, :], EPS), writes=[b_const])
        P.op("dve", lambda e: e.memset(onec[:, :], 1.0), writes=[b_const])

        def load_x(ph):
            xin = [sb(f"xin{i}", [128, D], F32, ph) for i in range(2)]
            xinb = [Buf(f"xin{i}") for i in range(2)]
            for r in range(S // 128):
                t = r // 4
                xi, xib = xin[r % 2], xinb[r % 2]
                P.dma("sp", xi[:, :], x_d[r * 128:(r + 1) * 128, :], writes=[xib])
                for half in range(2):
                    pk = 6 + half
                    for q in range(4):
                        kc = half * 4 + q
                        P.op("pe", lambda e, kc=kc, q=q, pk=pk, xi=xi: e.transpose(
                            ps[pk][:, q * 128:(q + 1) * 128], xi[:, kc * 128:(kc + 1) * 128], identf),
                            reads=[xib, b_const], writes=[psb[pk]], signal=(q == 3))
                    dst = xT3[:, half * 4:half * 4 + 4, r * 128:(r + 1) * 128]
                    src = ps[pk][:, :].rearrange("p (a b) -> p a b", a=4)
                    wr = [xTb[half * 4 + q][t] for q in range(4)]
                    if half == 0:
                        P.op("dve", lambda e, dst=dst, src=src: e.tensor_copy(out=dst, in_=src),
                             reads=[psb[pk]], writes=wr)
                    else:
                        act(dst, src, AF.Copy, [psb[pk]], wr)

        out_toks = []

        def store_tile(t, xo, xob):
            for r in range(t * 4, t * 4 + 4):
                xo_, xob_ = xo[r % 2], xob[r % 2]
                for half in range(2):
                    pk = 6 + half
                    for q in range(4):
                        kc = half * 4 + q
                        P.op("pe", lambda e, kc=kc, q=q, pk=pk, r=r: e.transpose(
                            ps[pk][:, q * 128:(q + 1) * 128], xT3[:, kc, r * 128:(r + 1) * 128], identf),
                            reads=[xTb[kc][t], b_const], writes=[psb[pk]], signal=(q == 3))
                    dst = xo_[:, half * 512:(half + 1) * 512]
                    if half == 0:
                        P.op("dve", lambda e, dst=dst, pk=pk: e.tensor_copy(out=dst, in_=ps[pk][:, :]),
                             reads=[psb[pk]], writes=[xob_])
                    else:
                        act(dst, ps[pk][:, :], AF.Copy, [psb[pk]], [xob_])
                out_toks.append(P.dma("sp", out_d[r * 128:(r + 1) * 128, :], xo_[:, :], reads=[xob_]))

        ph_names = []
        def rmsnorm_to_hT(gcol0, ph, tiles=None, dst=None, nbufs=None):
            key = f"{gcol0}_{len(ph_names)}"
            ph_names.append(key)
            if nbufs is None:
                sq = [sb(f"nsq{i}_{key}", [128, TT], BF16, ph) for i in range(2)]
                sqb = [Buf(f"nsq{i}") for i in range(2)]
                rstd = sb(f"nrstd_{key}", [128, TT], F32, ph)
                rstdb = Buf("nrstd")
            else:
                sq, sqb, rstd, rstdb = nbufs
            for t in (range(NT) if tiles is None else tiles):
                ts = slice(t * TT, (t + 1) * TT)
                for kc in range(KC):
                    s_, sb_ = sq[kc % 2], sqb[kc % 2]
                    act(s_[:, :], xT3[:, kc, ts], AF.Square, [xTb[kc][t]], [sb_])
                    mm(5, ones_d, s_[:, :], kc == 0, kc == KC - 1, [sb_, b_const], sig=True)
                rstd_from_ms(5, TT, rstd[:, :], rstdb)
                for kc in range(KC):
                    o_ap = hT3[:, kc, ts] if dst is None else dst[0][:, kc, :]
                    o_b = hTb[kc][t] if dst is None else dst[1][kc]
                    P.op("dve", lambda e, kc=kc, o_ap=o_ap: e.scalar_tensor_tensor(
                        out=o_ap, in0=xT3[:, kc, ts], scalar=pvec[:, gcol0 + kc:gcol0 + kc + 1],
                        in1=rstd[:, :], op0=ALU.mult, op1=ALU.mult),
                        reads=[xTb[kc][t], rstdb, b_const], writes=[o_b])

        def ffn(tag, gcol0, do_load=False, do_store=False):
            with ExitStack() as ph:
                set_psum(ph, 8, 0, 0)
                if do_load:
                    load_x(ph)
                if do_store:
                    xo = [sb(f"xo{i}", [128, D], F32, ph) for i in range(2)]
                    xob = [Buf(f"xo{i}") for i in range(2)]
                rmsnorm_to_hT(gcol0, ph)
                sg = [sb(f"sg{i}_{tag}", [128, TT], F32, ph) for i in range(2)]
                sgb = [Buf(f"sg{i}") for i in range(2)]
                uu = [sb(f"uu{i}_{tag}", [128, TT], BF16, ph) for i in range(4)]
                uub = [Buf(f"uu{i}") for i in range(4)]
                it = 0
                W.HOLD = 3
                pend = []

                def out_proj(wb3, wbb, ub, t):
                    ts = slice(t * TT, (t + 1) * TT)
                    for c in range(KC):
                        po = 4 + (c % 4)
                        for j in range(2):
                            mm(po, wb3[:, j, c * 128:(c + 1) * 128], ub[j][0][:, :], j == 0, j == 1,
                               [wbb, ub[j][1]])
                        P.op("dve", lambda e, c=c, po=po: e.scalar_tensor_tensor(
                            out=xT3[:, c, ts], in0=ps[po][:, :], scalar=0.5, in1=xT3[:, c, ts],
                            op0=ALU.mult, op1=ALU.add),
                            reads=[psb[po], xTb[c][t]], writes=[xTb[c][t]])

                for b in range(NFF // 2):
                    wa, wab = W.take((tag, "a", b))
                    wbt, wbb = W.take((tag, "b", b))
                    wa3 = v3(wa, KC, 512)
                    wb3 = v3(wbt, 2, 1024)
                    for t in range(NT):
                        ts = slice(t * TT, (t + 1) * TT)
                        ub = []
                        for j in range(2):
                            pg, pu = (it % 2) * 2, (it % 2) * 2 + 1
                            for kc in range(KC):
                                mm(pg, wa3[:, kc, j * 128:(j + 1) * 128], hT3[:, kc, ts], kc == 0, kc == KC - 1,
                                   [wab, hTb[kc][t]])
                            for kc in range(KC):
                                mm(pu, wa3[:, kc, 256 + j * 128:256 + (j + 1) * 128], hT3[:, kc, ts],
                                   kc == 0, kc == KC - 1, [wab, hTb[kc][t]])
                            s_, sb_ = sg[it % 2], sgb[it % 2]
                            u_, ub_ = uu[it % 4], uub[it % 4]
                            act(s_[:, :], ps[pg][:, :], AF.Silu, [psb[pg]], [sb_])
                            P.op("dve", lambda e, s_=s_, u_=u_, pu=pu: e.tensor_tensor(
                                out=u_[:, :], in0=ps[pu][:, :], in1=s_[:, :], op=ALU.mult),
                                reads=[psb[pu], sb_], writes=[ub_])
                            ub.append((u_, ub_))
                            it += 1
                        if pend:
                            pt = pend.pop(0)
                            out_proj(*pt)
                            if do_store and b == NFF // 2 - 1:
                                store_tile(pt[3], xo, xob)
                        pend.append((wb3, wbb, ub, t))
                pt = pend.pop(0)
                out_proj(*pt)
                if do_store:
                    store_tile(pt[3], xo, xob)
                W.HOLD = 2
                P.barrier()

        def branch_merge(tag, bi, t, src3, src_bufs, nk, ph_bufs, h3t=None, hbt=None):
            gsb, gsbb, ypt, yptb = ph_bufs
            ts = slice(t * TT, (t + 1) * TT)
            if h3t is None:
                h3t = hT3[:, :, ts]
                hbt = [hTb[kc][t] for kc in range(KC)]
            wg3 = None
            for cb in range(4):
                wt, wtb = W.take((tag, "w", cb))
                w3 = v3(wt, nk, 256)
                if cb % 2 == 0:
                    wg, wgb = W.take((tag, "g", cb // 2))
                    wg3 = v3(wg, KC, 512)
                for cc in range(2):
                    c = cb * 2 + cc
                    pz, pg = (c % 2) * 2, (c % 2) * 2 + 1
                    for k in range(nk):
                        mm(pz, w3[:, k, cc * 128:(cc + 1) * 128], src3[:, k, :], k == 0, k == nk - 1,
                           [wtb, src_bufs[k]])
                    gc = (c % 4) * 128
                    for kc in range(KC):
                        mm(pg, wg3[:, kc, gc:gc + 128], h3t[:, kc, :], kc == 0, kc == KC - 1,
                           [wgb, hbt[kc]])
                    g_, gb_ = gsb[c % 2], gsbb[c % 2]
                    act(g_[:, :], ps[pg][:, :], AF.Sigmoid, [psb[pg]], [gb_])
                    P.op("dve", lambda e, c=c, pz=pz, g_=g_: e.tensor_tensor(
                        out=ypt[:, c * TT:(c + 1) * TT], in0=ps[pz][:, :], in1=g_[:, :], op=ALU.mult),
                        reads=[psb[pz], gb_], writes=[yptb])
            P.dma("sp", yp_d[bi, :, :, ts].rearrange("k p s -> p k s"),
                  ypt[:, 0:KC * TT].rearrange("p (k s) -> p k s", k=KC), reads=[yptb], writes=[ypdb[bi][t]])

        def merge_bufs(ph, nm):
            gsb = [sb(f"gsb{i}_{nm}", [128, TT], F32, ph) for i in range(2)]
            gsbb = [Buf(f"gsb{i}") for i in range(2)]
            ypt = sb(f"ypt_{nm}", [128, KC * TT], BF16, ph)
            return gsb, gsbb, ypt, Buf("ypt")

        if stage >= 1:
            ffn("ffn1", PV_GFFN1, do_load=True)

        if stage >= 2:
            with ExitStack() as ph:
                set_psum(ph)
                rmsnorm_to_hT(PV_GMIX, ph)
                P.barrier()

            with ExitStack() as ph:
                DIF = sb("DIF", [128, 8 * S], BF16, ph)
                DIF3 = DIF[:, :].rearrange("p (h s) -> p h s", h=8)
                difb = [[Buf(f"dif{h}_{t}") for t in range(NT)] for h in range(8)]
                ph2 = ExitStack()
                dq0 = sb("dq0", [128, S], BF16, ph2)
                dq1 = sb("dq1", [128, S], BF16, ph2)
                dkn = sb("dkn", [128, S], BF16, ph2)
                dvT = sb("dvT", [128, 16 * 128], BF16, ph2)
                dqb = [Buf(f"dq_{t}") for t in range(NT)]
                dkb = [Buf(f"dk_{t}") for t in range(NT)]
                dvb = [Buf(f"dv_{t}") for t in range(NT)]
                set_psum(ph2, 4, 0, 2)
                sqd = [sb(f"sqd{i}", [128, TT], BF16, ph2) for i in range(2)]
                sqdb = [Buf(f"sqd{i}") for i in range(2)]
                rsd = [sb(f"rsd{i}", [128, TT], F32, ph2) for i in range(2)]
                rsdb = [Buf(f"rsd{i}") for i in range(2)]
                ET = [sb(f"ET{i}", [128, 2 * TT], BF16, ph2) for i in range(2)]
                ETb = [Buf(f"ET{i}") for i in range(2)]
                Esum = sb("Esum", [128, TT], F32, ph2)
                Esumb = Buf("Esum")
                rl1 = sb("rl1", [128, TT], F32, ph2)
                rl1b = Buf("rl1")
                rl0 = sb("rl0", [128, TT], F32, ph2)
                rl0b = Buf("rl0")
                rse = sb("rse", [128, TT], F32, ph2)
                rseb = Buf("rse")
                sqe = sb("sqe", [128, TT], BF16, ph2)
                sqeb = Buf("sqe")
                Ehi = sb("Ehi", [128, TT], BF16, ph2)
                Elo = sb("Elo", [128, TT], BF16, ph2)
                Ehib, Elob = Buf("Ehi"), Buf("Elo")
                Ocp = sb("Ocp", [128, 2 * TT], F32, ph2)
                Ocpb = Buf("Ocp")
                lamv = sb("lamv_sb", [128, 256], F32, ph2)
                lamt = sb("lamt", [128, 64], F32, ph2)
                lcol = sb("lcol", [128, 4], F32, ph2)
                lamb = Buf("lam")

                P.dma("sp", lamv[:, :], lamv_d[0:1, :].broadcast_to([128, 256]), writes=[lamb])
                for i in range(2):
                    P.op("dve", lambda e, i=i: e.tensor_tensor(
                        out=lamt[:, :], in0=lamv[:, i * 128:i * 128 + 64], in1=lamv[:, i * 128 + 64:i * 128 + 128],
                        op=ALU.mult), reads=[lamb], writes=[lamb])
                    P.op("dve", lambda e, i=i: e.reduce_sum(out=lcol[:, i:i + 1], in_=lamt[:, :],
                                                            axis=mybir.AxisListType.X),
                         reads=[lamb], writes=[lamb])
                act(lcol[:, 0:2], lcol[:, 0:2], AF.Exp, [lamb], [lamb])
                P.op("dve", lambda e: e.tensor_tensor(out=lcol[:, 2:3], in0=lcol[:, 1:2], in1=lcol[:, 0:1],
                                                      op=ALU.subtract), reads=[lamb], writes=[lamb])
                P.op("dve", lambda e: e.tensor_scalar(out=lcol[:, 2:3], in0=lcol[:, 2:3], scalar1=-LAM_INIT,
                                                      scalar2=None, op0=ALU.add), reads=[lamb], writes=[lamb])
                P.op("dve", lambda e: e.tensor_scalar(out=lcol[:, 3:4], in0=pvec[:, PV_GDO:PV_GDO + 1],
                                                      scalar1=1.0 - LAM_INIT, scalar2=None, op0=ALU.mult),
                     reads=[lamb, b_const], writes=[lamb])
                nlam = lcol[:, 2:3]
                gdo = lcol[:, 3:4]
                P.op("dve", lambda e: e.memset(dq0[64:128, :], 0.0), writes=dqb)
                P.op("dve", lambda e: e.memset(dq1[0:64, :], 0.0), writes=dqb)

                W.HOLD = 4
                wq = wk = wv = None
                pending = []

                def flush_pending():
                    while pending:
                        pending.pop(0)()

                def pop_pending(n):
                    for _ in range(n):
                        if pending:
                            pending.pop(0)()

                def pwh(i):
                    return pw[i // 2][:, (i % 2) * 512:(i % 2 + 1) * 512]

                for h in range(8):
                    hl = h % 4
                    if hl == 0:
                        wq, wqb = W.take(("dq", h // 4))
                        wk, wkb = W.take(("dk", h // 4))
                        wv, wvb = W.take(("dv", h // 4))
                    wq3, wk3, wv3 = v3(wq, KC, 512), v3(wk, KC, 512), v3(wv, KC, 512)
                    groups = [(t, which) for t in range(NT) for which in (0, 1)]

                    def stageA(g):
                        t, which = groups[g]
                        ts = slice(t * TT, (t + 1) * TT)
                        w3, wb_ = (wq3, wqb) if which == 0 else (wk3, wkb)
                        i = g % 4
                        for kc in range(KC):
                            mm2(pwh(i), pwb[i], w3[:, kc, hl * 128:(hl + 1) * 128], hT3[:, kc, ts], kc == 0, kc == KC - 1,
                                [wb_, hTb[kc][t]])

                    def stageB(g):
                        t, which = groups[g]
                        ts = slice(t * TT, (t + 1) * TT)
                        i, j = g % 4, g % 2
                        act(sqd[j][:, :], pwh(i), AF.Square, [pwb[i]], [sqdb[j]])
                        mm(3, bd64, sqd[j][:, :], True, True, [sqdb[j], b_const])
                        rstd_from_ms(3, TT, rsd[j][:, :], rsdb[j])
                        if which == 0:
                            for c, dq in enumerate((dq0, dq1)):
                                pr = slice(c * 64, (c + 1) * 64)
                                P.op("dve", lambda e, dq=dq, pr=pr: e.scalar_tensor_tensor(
                                    out=dq[pr, ts], in0=pwh(i)[pr, :], scalar=pvec[pr, PV_GDQ:PV_GDQ + 1],
                                    in1=rsd[j][pr, :], op0=ALU.mult, op1=ALU.mult),
                                    reads=[pwb[i], rsdb[j], b_const], writes=[dqb[t]])
                        else:
                            P.op("dve", lambda e: e.scalar_tensor_tensor(
                                out=dkn[:, ts], in0=pwh(i), scalar=pvec[:, PV_GDK:PV_GDK + 1],
                                in1=rsd[j][:, :], op0=ALU.mult, op1=ALU.mult),
                                reads=[pwb[i], rsdb[j], b_const], writes=[dkb[t]])

                    def stageV(t):
                        for c4 in range(4):
                            tok = slice(t * TT + c4 * 128, t * TT + (c4 + 1) * 128)
                            for kc in range(KC):
                                mm(t % 2, hT3[:, kc, tok], wv3[:, kc, hl * 128:(hl + 1) * 128], kc == 0, kc == KC - 1,
                                   [wvb, hTb[kc][t]], cols=slice(c4 * 128, (c4 + 1) * 128), sig=(kc == KC - 1))
                        act(dvT[:, t * 512:(t + 1) * 512], ps[t % 2][:, :], AF.Copy, [psb[t % 2]], [dvb[t]])

                    for g in range(8 + 2):
                        if g < 8:
                            stageA(g)
                        if g >= 1:
                            pop_pending(2)
                        if g >= 2:
                            stageB(g - 2)
                        if g % 2 == 1 and g < 8:
                            stageV(g // 2)

                    for qt in range(NT):
                        nkt = 4 * qt + 4
                        ts = slice(qt * TT, (qt + 1) * TT)

                        def emitS(kt):
                            w = kt % 2
                            j0 = max(0, kt - 4 * qt) * 128
                            qs = slice(qt * TT + j0, (qt + 1) * TT)
                            for c, dq in enumerate((dq0, dq1)):
                                mm2(pw[w][:, c * 512 + j0:(c + 1) * 512], pwb[2 * w + c], dkn[:, kt * 128:(kt + 1) * 128],
                                    dq[:, qs], True, True, [dkb[kt // 4], dqb[qt]], sig=(c == 1))

                        emitS(0)
                        for kt in range(nkt):
                            if kt + 1 < nkt:
                                emitS(kt + 1)
                            if kt >= 1:
                                pop_pending(2 if len(pending) > 6 else 1)
                            w = kt % 2
                            j0 = max(0, kt - 4 * qt) * 128
                            diag = kt >= 4 * qt
                            E, Eb = ET[w], ETb[w]
                            s3 = pw[w][:, :].rearrange("p (c s) -> p c s", c=2)[:, :, j0:TT]
                            e3 = E[:, :].rearrange("p (c s) -> p c s", c=2)[:, :, j0:TT]
                            act(e3, s3, AF.Exp, [pwb[2 * w], pwb[2 * w + 1]], [Eb], scale=0.125)
                            if diag:
                                for c in range(2):
                                    blk = E[:, c * 512 + j0:c * 512 + j0 + 128]
                                    P.op("dve", lambda e, blk=blk: e.tensor_tensor(out=blk, in0=blk, in1=tri, op=ALU.mult),
                                         reads=[Eb, b_const], writes=[Eb])
                            e0, es0 = E[:, j0:TT], Esum[:, j0:TT]
                            if kt == 0:
                                P.op("dve", lambda e, e0=e0, es0=es0: e.tensor_copy(out=es0, in_=e0), reads=[Eb], writes=[Esumb])
                            else:
                                P.op("dve", lambda e, e0=e0, es0=es0: e.tensor_tensor(out=es0, in0=e0, in1=es0, op=ALU.add),
                                     reads=[Eb, Esumb], writes=[Esumb])
                            for c in range(2):
                                mm(c, dvT[:, kt * 128:(kt + 1) * 128], E[:, c * 512 + j0:(c + 1) * 512], kt == 0, kt == nkt - 1,
                                   [dvb[kt // 4], Eb], cols=slice(j0, TT), sig=True)
                            mm(3, ones1, E[:, 512 + j0:1024], kt == 0, kt == nkt - 1, [Eb, b_const], cols=slice(j0, TT), sig=True)
                        Es, Esb = Esum, Esumb
                        for c in range(2):
                            P.op("dve", lambda e, c=c: e.tensor_copy(out=Ocp[:, c * 512:(c + 1) * 512], in_=ps[c][:, :]),
                                 reads=[psb[c]], writes=[Ocpb])
                        act(rl1[:, :], ps[3][:, :], AF.Ln, [psb[3]], [rl1b])
                        act(rl1[:, :], rl1[:, :], AF.Exp, [rl1b], [rl1b], scale=-1.0)
                        P.op("dve", lambda e: e.tensor_tensor(out=Ocp[:, 512:1024], in0=Ocp[:, 512:1024], in1=rl1[:, :],
                                                              op=ALU.mult), reads=[Ocpb, rl1b], writes=[Ocpb])
                        P.op("dve", lambda e: e.tensor_copy(out=Ehi[:, :], in_=Esum[:, :]), reads=[Esumb], writes=[Ehib])
                        P.op("dve", lambda e: e.tensor_tensor(out=Elo[:, :], in0=Esum[:, :], in1=Ehi[:, :], op=ALU.subtract),
                             reads=[Esumb, Ehib], writes=[Elob])

                        def f1():
                            mm(2, ones1, Ehi[:, :], True, False, [Ehib, b_const], sig=False)
                            mm(2, ones1, Elo[:, :], False, True, [Elob, b_const], sig=True)

                        def f2():
                            act(rl0[:, :], ps[2][:, :], AF.Ln, [psb[2]], [rl0b])
                            act(rl0[:, :], rl0[:, :], AF.Exp, [rl0b], [rl0b], scale=-1.0)

                        def f2b():
                            P.op("dve", lambda e: e.tensor_tensor(
                                out=Ocp[:, 0:512], in0=Ocp[:, 0:512], in1=rl0[:, :], op=ALU.mult),
                                reads=[Ocpb, rl0b], writes=[Ocpb])

                        def f3():
                            P.op("dve", lambda e: e.scalar_tensor_tensor(
                                out=Ocp[:, 0:512], in0=Ocp[:, 512:1024], scalar=nlam, in1=Ocp[:, 0:512],
                                op0=ALU.mult, op1=ALU.add), reads=[Ocpb, lamb], writes=[Ocpb])
                            act(sqe[:, :], Ocp[:, 0:512], AF.Square, [Ocpb], [sqeb])

                        def f4():
                            mm(2, ones128, sqe[:, :], True, True, [sqeb, b_const])

                        def f5():
                            rstd_from_ms(2, TT, rse[:, :], rseb)

                        def f6(h=h, ts=ts):
                            P.op("dve", lambda e: e.scalar_tensor_tensor(
                                out=DIF3[:, h, ts], in0=Ocp[:, 0:512], scalar=gdo, in1=rse[:, :], op0=ALU.mult, op1=ALU.mult),
                                reads=[Ocpb, rseb, lamb], writes=[difb[h][ts.start // TT]])

                        pending.extend([f1, f2, f2b, f3, f4, f5, f6])
                flush_pending()
                W.HOLD = 2
                P.barrier()
                ph2.close()
                set_psum(ph)
                mb = merge_bufs(ph, "d")
                for t in range(NT):
                    ts = slice(t * TT, (t + 1) * TT)
                    branch_merge(("mdiff", t), 1, t, DIF3[:, :, ts], [difb[h][t] for h in range(8)], 8, mb)
                P.barrier()


        if stage >= 3:
            with ExitStack() as ph:
                set_psum(ph, 8, 0, 0)
                memin = sb("memin", [128, 2 * D], F32, ph)
                memT = sb("memT", [128, KC * 256], F32, ph)
                memh = sb("memh", [128, KC * 256], BF16, ph)
                mkn = sb("mkn", [128, 4 * 2 * 256], BF16, ph)
                mvT = sb("mvT", [128, 2 * 1024], BF16, ph)
                sqm = [sb(f"sqm{i}", [128, TT], BF16, ph) for i in range(2)]
                sqmb = [Buf(f"sqm{i}") for i in range(2)]
                rsm = sb("rsm", [128, TT], F32, ph)
                rsmb = Buf("rsm")
                mqn = sb("mqn", [128, 2 * TT], BF16, ph)
                mqnb = Buf("mqn")
                EM = [sb(f"EM{i}", [128, TT], BF16, ph) for i in range(2)]
                EMb = [Buf(f"EM{i}") for i in range(2)]
                rl = sb("rl", [128, TT], F32, ph)
                rlb = Buf("rl")
                MO = sb("MO", [128, KC * TT], BF16, ph)
                mob = Buf("MO")
                meminb, memTb, memhb, mknb, mvTb = Buf("memin"), Buf("memT"), Buf("memh"), Buf("mkn"), Buf("mvT")
                memin3 = memin[:, :].rearrange("p (a n) -> p a n", a=2)
                memT3 = memT[:, :].rearrange("p (k m) -> p k m", k=KC)
                memh3 = memh[:, :].rearrange("p (k m) -> p k m", k=KC)
                mkn4 = mkn[:, :].rearrange("p (h c m) -> p h c m", h=4, c=2)
                mvT3 = mvT[:, :].rearrange("p (a n) -> p a n", a=2)
                mqn3 = mqn[:, :].rearrange("p (c s) -> p c s", c=2)
                MO3 = MO[:, :].rearrange("p (k s) -> p k s", k=KC)

                P.dma("sp", memin3, mem_d.rearrange("(a p) n -> p a n", p=128), writes=[meminb])
                for mt in range(2):
                    for half in range(2):
                        pk = 4 + half
                        for q in range(4):
                            kc = half * 4 + q
                            P.op("pe", lambda e, kc=kc, q=q, pk=pk, mt=mt: e.transpose(
                                ps[pk][:, q * 128:(q + 1) * 128], memin3[:, mt, kc * 128:(kc + 1) * 128], identf),
                                reads=[meminb, b_const], writes=[psb[pk]], signal=(q == 3))
                        dst = memT3[:, half * 4:half * 4 + 4, mt * 128:(mt + 1) * 128]
                        src = ps[pk][:, :].rearrange("p (a b) -> p a b", a=4)
                        P.op("dve", lambda e, dst=dst, src=src: e.tensor_copy(out=dst, in_=src),
                             reads=[psb[pk]], writes=[memTb])
                for kc in range(KC):
                    s_, sb_ = sqm[kc % 2], sqmb[kc % 2]
                    act(s_[:, 0:256], memT3[:, kc, :], AF.Square, [memTb], [sb_])
                    mm(5, ones_d, s_[:, 0:256], kc == 0, kc == KC - 1, [sb_, b_const], sig=True, cols=slice(0, 256))
                rstd_from_ms(5, 256, rsm[:, 0:256], rsmb)
                for kc in range(KC):
                    P.op("dve", lambda e, kc=kc: e.scalar_tensor_tensor(
                        out=memh3[:, kc, :], in0=memT3[:, kc, :], scalar=pvec[:, PV_GMEM + kc:PV_GMEM + kc + 1],
                        in1=rsm[:, 0:256], op0=ALU.mult, op1=ALU.mult),
                        reads=[memTb, rsmb, b_const], writes=[memhb])
                c256 = slice(0, 256)
                for blk in range(2):
                    wk, wkb = W.take(("mkv", blk))
                    wk3 = v3(wk, KC, 512)
                    for hh in range(2):
                        hm = blk * 2 + hh
                        for dc in range(2):
                            for kc in range(KC):
                                mm(dc, wk3[:, kc, hh * 256 + dc * 128:hh * 256 + (dc + 1) * 128], memh3[:, kc, :],
                                   kc == 0, kc == KC - 1, [wkb, memhb], cols=c256)
                            act(sqm[dc][:, 0:256], ps[dc][:, 0:256], AF.Square, [psb[dc]], [sqmb[dc]])
                            mm(2, ones256, sqm[dc][:, 0:256], dc == 0, dc == 1, [sqmb[dc], b_const], sig=True, cols=c256)
                        rstd_from_ms(2, 256, rsm[:, 0:256], rsmb)
                        for dc in range(2):
                            P.op("dve", lambda e, dc=dc, hm=hm: e.scalar_tensor_tensor(
                                out=mkn4[:, hm, dc, :], in0=ps[dc][:, 0:256], scalar=pvec[:, PV_GMK + dc:PV_GMK + dc + 1],
                                in1=rsm[:, 0:256], op0=ALU.mult, op1=ALU.mult),
                                reads=[psb[dc], rsmb, b_const], writes=[mknb])
                for vb in range(2):
                    wv, wvb = W.take(("mkv", 2 + vb))
                    wv3 = v3(wv, KC, 512)
                    for mt in range(2):
                        for kc in range(KC):
                            mm(3, memh3[:, kc, mt * 128:(mt + 1) * 128], wv3[:, kc, :], kc == 0, kc == KC - 1,
                               [wvb, memhb])
                        act(mvT3[:, mt, vb * 512:(vb + 1) * 512], ps[3][:, :], AF.Copy, [psb[3]], [mvTb])

                mb = merge_bufs(ph, "m")
                mqn2 = [mqn, sb("mqn_b", [128, 2 * TT], BF16, ph)]
                mqnb2 = [mqnb, Buf("mqn_b")]
                wq3_cur = [None, None]

                def stageP(t, hm):
                    ts = slice(t * TT, (t + 1) * TT)
                    if hm % 2 == 0:
                        wq, wqb = W.take(("mq", t, hm // 2))
                        wq3_cur[0], wq3_cur[1] = v3(wq, KC, 512), wqb
                    wq3, wqb = wq3_cur
                    hh = hm % 2
                    ba = (hm % 2) * 2
                    m3 = mqn2[hm % 2][:, :].rearrange("p (c s) -> p c s", c=2)
                    for dc in range(2):
                        for kc in range(KC):
                            mm(ba + dc, wq3[:, kc, hh * 256 + dc * 128:hh * 256 + (dc + 1) * 128], hT3[:, kc, ts],
                               kc == 0, kc == KC - 1, [wqb, hTb[kc][t]])
                        act(sqm[dc][:, :], ps[ba + dc][:, :], AF.Square, [psb[ba + dc]], [sqmb[dc]])
                        mm(4, ones256, sqm[dc][:, :], dc == 0, dc == 1, [sqmb[dc], b_const], sig=True)
                    rstd_from_ms(4, TT, rsm[:, :], rsmb)
                    for dc in range(2):
                        P.op("dve", lambda e, dc=dc: e.scalar_tensor_tensor(
                            out=m3[:, dc, :], in0=ps[ba + dc][:, :], scalar=pvec[:, PV_GMQ + dc:PV_GMQ + dc + 1],
                            in1=rsm[:, :], op0=ALU.mult, op1=ALU.mult),
                            reads=[psb[ba + dc], rsmb, b_const], writes=[mqnb2[hm % 2]])

                def stageA(t, hm):
                    m3 = mqn2[hm % 2][:, :].rearrange("p (c s) -> p c s", c=2)
                    for mt in range(2):
                        for dc in range(2):
                            mm(5 + mt, mkn4[:, hm, dc, mt * 128:(mt + 1) * 128], m3[:, dc, :], dc == 0, dc == 1,
                               [mknb, mqnb2[hm % 2]])
                        act(EM[mt][:, :], ps[5 + mt][:, :], AF.Exp, [psb[5 + mt]], [EMb[mt]], scale=1.0 / 16.0)
                    for mt in range(2):
                        mm(7, ones1, EM[mt][:, :], mt == 0, mt == 1, [EMb[mt], b_const])
                    for ec in range(2):
                        for mt in range(2):
                            mm(5 + ec, mvT3[:, mt, hm * 256 + ec * 128:hm * 256 + (ec + 1) * 128], EM[mt][:, :],
                               mt == 0, mt == 1, [mvTb, EMb[mt]])
                    act(rl[:, :], ps[7][:, :], AF.Ln, [psb[7]], [rlb])
                    act(rl[:, :], rl[:, :], AF.Exp, [rlb], [rlb], scale=-1.0)
                    for ec in range(2):
                        P.op("dve", lambda e, ec=ec: e.tensor_tensor(
                            out=MO3[:, hm * 2 + ec, :], in0=ps[5 + ec][:, :], in1=rl[:, :], op=ALU.mult),
                            reads=[psb[5 + ec], rlb], writes=[mob])

                for t in range(NT):
                    stageP(t, 0)
                    for hm in range(4):
                        if hm + 1 < 4:
                            stageP(t, hm + 1)
                        stageA(t, hm)
                    branch_merge(("mmem", t), 2, t, MO3, [mob] * 8, 8, mb)
                P.barrier()

        if stage >= 4:
            with ExitStack() as ph:
                set_psum(ph, 6, 2, 0)
                pbKb, pbRb = Buf("pbK"), Buf("pbR")
                htile = hT[:, 0:4096]
                RT = hT[:, 4096:12288]
                ypt_r = hT[:, 12288:16384]
                htile3 = htile.rearrange("p (k s) -> p k s", k=KC)
                RT3 = RT.rearrange("p (k s) -> p k s", k=16)
                htb = [Buf(f"ht{k}") for k in range(KC)]
                rtb = [Buf(f"rt{k}") for k in range(16)]
                cret = sb("cret_sb", [128, CR_N], F32, ph)
                cretb = Buf("cret")
                posi = sb("posi", [128, TT], I32, ph)
                posb = Buf("posi")
                Tm = [sb(f"Tm{i}", [128, TT], F32, ph) for i in range(4)]
                Tmb = [Buf(f"Tm{i}") for i in range(4)]
                cosT = sb("cosT", [128, TT], F32, ph)
                sinT = sb("sinT", [128, TT], F32, ph)
                csb = Buf("cossin")
                qr = [sb(f"qr{i}", [128, 2 * TT], BF16, ph) for i in range(2)]
                qd = [sb(f"qd{i}", [128, 2 * TT], BF16, ph) for i in range(2)]
                kr = [sb(f"kr{i}", [128, 2 * TT], BF16, ph) for i in range(2)]
                qrb = [Buf(f"qr{i}") for i in range(2)]
                qdb = [Buf(f"qd{i}") for i in range(2)]
                krb = [Buf(f"kr{i}") for i in range(2)]
                kT = [sb(f"kT{i}", [128, 1024], BF16, ph) for i in range(2)]
                kTb = [Buf(f"kT{i}") for i in range(2)]
                v_sb = [sb(f"v_sb{i}", [128, 4 * TT], BF16, ph) for i in range(2)]
                sg_sb = [sb(f"sg_sb{i}", [128, 4 * TT], BF16, ph) for i in range(2)]
                vbb = [[Buf(f"v{i}_{c}") for c in range(4)] for i in range(2)]
                sgbb = [[Buf(f"sg{i}_{c}") for c in range(4)] for i in range(2)]
                state = sb("state", [128, 8 * TT], F32, ph)
                state3 = state[:, :].rearrange("p (a e) -> p a e", a=8)
                stb = [Buf(f"st{a}") for a in range(8)]
                stbf = sb("stbf", [128, 2 * TT], BF16, ph)
                stbf3 = stbf[:, :].rearrange("p (a e) -> p a e", a=2)
                stbb = [Buf(f"stbf{a}") for a in range(2)]
                scD = [sb(f"scD{i}", [128, 128], BF16, ph) for i in range(2)]
                scDb = [Buf(f"scD{i}") for i in range(2)]
                ssq = sb("ssq", [128, 8], F32, ph)
                ssqb = Buf("ssq")
                junk = sb("junk", [128, TT], BF16, ph)
                junkb = Buf("junk")
                ret_sb = [sb(f"ret_sb{i}", [128, TT], BF16, ph) for i in range(2)]
                retb = [Buf(f"ret_sb{i}") for i in range(2)]
                nsq = [sb(f"rnsq{i}", [128, TT], BF16, ph) for i in range(2)]
                nsqb = [Buf(f"rnsq{i}") for i in range(2)]
                nbufs = (nsq, nsqb, Tm[3], Tmb[3])
                mb_r = ([Tm[0], Tm[1]], [Tmb[0], Tmb[1]], ypt_r, Buf("ypt_r"))

                P.dma("sp", cret[:, :], cret_d[:, :], writes=[cretb])

                def rotary_prep(t):
                    ts = slice(t * TT, (t + 1) * TT)
                    P.dma("sp", posi[:, :], pos_d[0:1, ts].broadcast_to([128, TT]), writes=[posb])
                    ang, ta, tb = Tm[0], Tm[1], Tm[2]
                    ki = Tm[3][:, :].bitcast(I32)
                    P.op("dve", lambda e: e.tensor_scalar(out=ang[:, :], in0=posi[:, :], scalar1=pvec[:, PV_INV:PV_INV + 1],
                                                          scalar2=None, op0=ALU.mult),
                         reads=[posb, b_const], writes=[Tmb[0]])
                    for which, dstT in ((0, sinT), (1, cosT)):
                        if which == 1:
                            P.op("dve", lambda e: e.tensor_scalar(out=ang[:, :], in0=ang[:, :], scalar1=math.pi / 2.0,
                                                                  scalar2=None, op0=ALU.add),
                                 reads=[Tmb[0]], writes=[Tmb[0]])
                        P.op("dve", lambda e: e.tensor_scalar(out=ki, in0=ang[:, :], scalar1=1.0 / TWO_PI,
                                                              scalar2=None, op0=ALU.mult),
                             reads=[Tmb[0]], writes=[Tmb[3]])
                        P.op("dve", lambda e: e.scalar_tensor_tensor(out=ta[:, :], in0=ki, scalar=-CW1, in1=ang[:, :],
                                                                     op0=ALU.mult, op1=ALU.add),
                             reads=[Tmb[3], Tmb[0]], writes=[Tmb[1]])
                        P.op("dve", lambda e: e.scalar_tensor_tensor(out=ta[:, :], in0=ki, scalar=-CW2, in1=ta[:, :],
                                                                     op0=ALU.mult, op1=ALU.add),
                             reads=[Tmb[3], Tmb[1]], writes=[Tmb[1]])
                        P.op("dve", lambda e: e.tensor_scalar(out=tb[:, :], in0=ta[:, :], scalar1=math.pi, scalar2=None,
                                                              op0=ALU.is_gt), reads=[Tmb[1]], writes=[Tmb[2]])
                        P.op("dve", lambda e: e.scalar_tensor_tensor(out=ta[:, :], in0=tb[:, :], scalar=-TWO_PI, in1=ta[:, :],
                                                                     op0=ALU.mult, op1=ALU.add),
                             reads=[Tmb[2], Tmb[1]], writes=[Tmb[1]])
                        P.op("dve", lambda e: e.tensor_scalar(out=tb[:, :], in0=ta[:, :], scalar1=-math.pi, scalar2=None,
                                                              op0=ALU.is_lt), reads=[Tmb[1]], writes=[Tmb[2]])
                        P.op("dve", lambda e: e.scalar_tensor_tensor(out=ta[:, :], in0=tb[:, :], scalar=TWO_PI, in1=ta[:, :],
                                                                     op0=ALU.mult, op1=ALU.add),
                             reads=[Tmb[2], Tmb[1]], writes=[Tmb[1]])
                        P.op("dve", lambda e: e.tensor_scalar(out=ta[:, :], in0=ta[:, :], scalar1=-PI_SAFE, scalar2=PI_SAFE,
                                                              op0=ALU.max, op1=ALU.min), reads=[Tmb[1]], writes=[Tmb[1]])
                        act(dstT[:, :], ta[:, :], AF.Sin, [Tmb[1]], [csb])

                def phase_i(t, h):
                    pp = h % 2
                    qr3 = qr[pp][:, :].rearrange("p (c s) -> p c s", c=2)
                    qd3 = qd[pp][:, :].rearrange("p (c s) -> p c s", c=2)
                    kr3 = kr[pp][:, :].rearrange("p (c s) -> p c s", c=2)
                    v3_ = v_sb[pp][:, :].rearrange("p (c e) -> p c e", c=4)
                    sg3_ = sg_sb[pp][:, :].rearrange("p (c e) -> p c e", c=4)
                    wqk, wqkb = W.take(("rqk", t, h))
                    wqk3 = v3(wqk, KC, 512)
                    qdec_b = cret[:, CR_QDEC + h * 128:CR_QDEC + (h + 1) * 128].rearrange(
                        "p (o d) -> p o d", o=1).broadcast_to([128, 4, 128])
                    for which in range(2):
                        for dc in range(2):
                            for kc in range(KC):
                                mm(dc, wqk3[:, kc, which * 256 + dc * 128:which * 256 + (dc + 1) * 128],
                                   htile3[:, kc, :], kc == 0, kc == KC - 1, [wqkb, htb[kc]])
                            yield
                        for half in range(2):
                            c1, c2 = (cosT, sinT) if half == 0 else (sinT, cosT)
                            P.op("dve", lambda e, c1=c1: e.tensor_tensor(out=Tm[0][:, :], in0=ps[0][:, :], in1=c1[:, :],
                                                                         op=ALU.mult),
                                 reads=[psb[0], csb], writes=[Tmb[0]])
                            P.op("dve", lambda e, c2=c2: e.tensor_tensor(out=Tm[1][:, :], in0=ps[1][:, :], in1=c2[:, :],
                                                                         op=ALU.mult),
                                 reads=[psb[1], csb], writes=[Tmb[1]])
                            op_ = ALU.subtract if half == 0 else ALU.add
                            if which == 0:
                                P.op("dve", lambda e, op_=op_: e.tensor_tensor(out=Tm[2][:, :], in0=Tm[0][:, :],
                                                                               in1=Tm[1][:, :], op=op_),
                                     reads=[Tmb[0], Tmb[1]], writes=[Tmb[2]])
                                act(qr3[:, half, :], Tm[2][:, :], AF.Copy, [Tmb[2]], [qrb[pp]])
                                P.op("dve", lambda e, half=half: e.tensor_tensor(
                                    out=qd3[:, half, :].rearrange("p (c d) -> p c d", c=4),
                                    in0=Tm[2][:, :].rearrange("p (c d) -> p c d", c=4), in1=qdec_b, op=ALU.mult),
                                    reads=[Tmb[2], cretb], writes=[qdb[pp]])
                            else:
                                P.op("dve", lambda e, op_=op_, half=half: e.tensor_tensor(
                                    out=kr3[:, half, :], in0=Tm[0][:, :], in1=Tm[1][:, :], op=op_),
                                    reads=[Tmb[0], Tmb[1]], writes=[krb[pp]])
                            yield
                    for hf in range(2):
                        for i4 in range(4):
                            idx = hf * 4 + i4
                            c, dc = idx // 2, idx % 2
                            P.op("pe", lambda e, c=c, dc=dc, i4=i4: e.transpose(
                                pb[0][:, i4 * 128:(i4 + 1) * 128], kr3[:, dc, c * 128:(c + 1) * 128], identb),
                                reads=[krb[pp], b_const], writes=[pbKb], signal=(i4 == 3))
                        P.op("dve", lambda e, hf=hf: e.tensor_scalar(
                            out=kT[pp][:, hf * 512:(hf + 1) * 512], in0=pb[0][:, 0:512],
                            scalar1=pvec[:, PV_KDEC + h:PV_KDEC + h + 1], scalar2=None, op0=ALU.mult),
                            reads=[pbKb, b_const], writes=[kTb[pp]])
                        yield
                    wv, wvb = W.take(("rv", t, h))
                    wv3 = v3(wv, KC, 512)
                    for c in range(4):
                        bk = c % 2
                        for kc in range(KC):
                            mm(bk, htile3[:, kc, c * 128:(c + 1) * 128], wv3[:, kc, :], kc == 0, kc == KC - 1,
                               [wvb, htb[kc]])
                        act(v3_[:, c, :], ps[bk][:, :], AF.Copy, [psb[bk]], [vbb[pp][c]])
                        yield
                    wg, wgb = W.take(("rg", t, h))
                    wg3 = v3(wg, KC, 512)
                    for c in range(4):
                        bk = (c + 1) % 2
                        for kc in range(KC):
                            mm(bk, htile3[:, kc, c * 128:(c + 1) * 128], wg3[:, kc, :], kc == 0, kc == KC - 1,
                               [wgb, htb[kc]])
                        act(Tm[3][:, :], ps[bk][:, :], AF.Exp, [psb[bk]], [Tmb[3]], scale=-1.0)
                        act(Tm[3][:, :], Tm[3][:, :], AF.Ln, [Tmb[3]], [Tmb[3]], bias=onec[:, 0:1])
                        act(Tm[3][:, :], Tm[3][:, :], AF.Exp, [Tmb[3]], [Tmb[3]], scale=-1.0)
                        P.op("dve", lambda e, c=c, bk=bk: e.tensor_tensor(out=sg3_[:, c, :], in0=ps[bk][:, :], in1=Tm[3][:, :],
                                                                           op=ALU.mult),
                             reads=[psb[bk], Tmb[3]], writes=[sgbb[pp][c]])
                        yield

                it_cnt = [0]

                def phase_ii(t, h):
                    pp = h % 2
                    qr3 = qr[pp][:, :].rearrange("p (c s) -> p c s", c=2)
                    qd3 = qd[pp][:, :].rearrange("p (c s) -> p c s", c=2)
                    kr3 = kr[pp][:, :].rearrange("p (c s) -> p c s", c=2)
                    v3_ = v_sb[pp][:, :].rearrange("p (c e) -> p c e", c=4)
                    sg3_ = sg_sb[pp][:, :].rearrange("p (c e) -> p c e", c=4)
                    if t > 0:
                        for dc in range(2):
                            act(stbf3[:, dc, :], state3[:, h * 2 + dc, :], AF.Copy, [stb[h * 2 + dc]], [stbb[dc]])

                    def tail_norm(c, col):
                        act(junk[:, :], ps[3][:, :], AF.Square, [psb[3]], [junkb, ssqb],
                            accum_out=ssq[:, col:col + 1])
                        act(ssq[:, col:col + 1], ssq[:, col:col + 1], AF.Ln, [ssqb, b_const], [ssqb],
                            scale=1.0 / 512.0, bias=epsc[:, 0:1])
                        act(ssq[:, col:col + 1], ssq[:, col:col + 1], AF.Exp, [ssqb], [ssqb], scale=-0.5)
                        r_, rb_ = ret_sb[c % 2], retb[c % 2]
                        P.op("dve", lambda e: e.scalar_tensor_tensor(
                            out=r_[:, :], in0=ps[3][:, :], scalar=ssq[:, col:col + 1], in1=sg3_[:, c, :],
                            op0=ALU.mult, op1=ALU.mult),
                            reads=[psb[3], ssqb, sgbb[pp][c]], writes=[rb_])

                    def tail_xpose(c):
                        cs = slice(c * 128, (c + 1) * 128)
                        r_, rb_ = ret_sb[c % 2], retb[c % 2]
                        for ec in range(4):
                            P.op("pe", lambda e, ec=ec: e.transpose(
                                pb[1][:, ec * 128:(ec + 1) * 128], r_[:, ec * 128:(ec + 1) * 128], identb),
                                reads=[rb_, b_const], writes=[pbRb], signal=(ec == 3))
                        act(RT3[:, h * 4:(h + 1) * 4, cs], pb[1][:, 0:512].rearrange("p (a b) -> p a b", a=4), AF.Copy,
                            [pbRb], [rtb[h * 4 + k] for k in range(4)])

                    for c in range(4):
                        gchunk = t * 4 + c
                        cs = slice(c * 128, (c + 1) * 128)
                        first = gchunk == 0
                        for dc in range(2):
                            mm(2, kr3[:, dc, cs], qr3[:, dc, cs], dc == 0, dc == 1, [krb[pp], qrb[pp]], cols=slice(0, 128))
                        if gchunk < 15:
                            for dc in range(2):
                                mm(4 + dc, kT[pp][:, c * 256 + dc * 128:c * 256 + (dc + 1) * 128], v3_[:, c, :], True, True,
                                   [kTb[pp], vbb[pp][c]])
                        sd, sdb = scD[c % 2], scDb[c % 2]
                        P.op("dve", lambda e, sd=sd: e.tensor_tensor(
                            out=sd[:, :], in0=ps[2][:, 0:128], in1=cret[:, CR_DEC + h * 128:CR_DEC + (h + 1) * 128],
                            op=ALU.mult), reads=[psb[2], cretb], writes=[sdb])
                        yield
                        mm(3, sd[:, :], v3_[:, c, :], True, first, [sdb, vbb[pp][c]], sig=True)
                        if not first:
                            for dc in range(2):
                                mm(3, qd3[:, dc, cs], stbf3[:, dc, :], False, dc == 1, [qdb[pp], stbb[dc]], sig=True)
                        if c > 0:
                            tail_xpose(c - 1)
                        yield
                        if gchunk < 15:
                            for dc in range(2):
                                a = h * 2 + dc
                                if first:
                                    P.op("dve", lambda e, a=a, dc=dc: e.tensor_copy(out=state3[:, a, :], in_=ps[4 + dc][:, :]),
                                         reads=[psb[4 + dc]], writes=[stb[a]])
                                else:
                                    P.op("dve", lambda e, a=a, dc=dc: e.scalar_tensor_tensor(
                                        out=state3[:, a, :], in0=state3[:, a, :], scalar=float(GAMMA[h] ** 128),
                                        in1=ps[4 + dc][:, :], op0=ALU.mult, op1=ALU.add),
                                        reads=[psb[4 + dc], stb[a]], writes=[stb[a]])
                                if c < 3:
                                    act(stbf3[:, dc, :], state3[:, a, :], AF.Copy, [stb[a]], [stbb[dc]])
                        tail_norm(c, it_cnt[0] % 8)
                        it_cnt[0] += 1
                        yield
                    tail_xpose(3)
                    yield

                def run_both(g_main, g_fill):
                    a_done = b_done = False
                    k = 0
                    while not (a_done and b_done):
                        if not a_done:
                            try:
                                next(g_main)
                            except StopIteration:
                                a_done = True
                        for _ in range(2 if k % 2 == 0 else 1):
                            if not b_done:
                                try:
                                    next(g_fill)
                                except StopIteration:
                                    b_done = True
                        k += 1

                def run_one(g):
                    for _ in g:
                        pass

                rotary_prep(0)
                for t in range(NT):
                    rmsnorm_to_hT(PV_GMIX, ph, tiles=[t], dst=(htile3, htb), nbufs=nbufs)
                    run_one(phase_i(t, 0))
                    for h in range(4):
                        if h + 1 < 4:
                            run_both(phase_ii(t, h), phase_i(t, h + 1))
                        else:
                            run_one(phase_ii(t, h))
                    if t + 1 < NT:
                        rotary_prep(t + 1)
                    branch_merge(("mret", t), 0, t, RT3, rtb, 16, mb_r, h3t=htile3, hbt=htb)
                P.barrier()

        if stage >= 5:
            with ExitStack() as ph:
                set_psum(ph)
                wo_sb = sb("wo_sb", [128, KC * D], BF16, ph)
                wob = Buf("wo")
                wo_sb3 = wo_sb[:, :].rearrange("p (k n) -> p k n", k=KC)
                P.dma("pool", wo_sb3, wo_d.rearrange("(k p) n -> p k n", p=128), writes=[wob])
                ypl = [sb(f"ypl{b}", [128, KC * TT], BF16, ph) for b in range(3)]
                yplb = [Buf(f"ypl{b}") for b in range(3)]
                Y = sb("Ysum", [128, KC * TT], BF16, ph)
                Y3 = Y[:, :].rearrange("p (k s) -> p k s", k=KC)
                Yb = [Buf(f"Y{k}") for k in range(KC)]
                tsum = [sb(f"tsum{i}", [128, TT], F32, ph) for i in range(2)]
                tsumb = [Buf(f"tsum{i}") for i in range(2)]
                for t in range(NT):
                    ts = slice(t * TT, (t + 1) * TT)
                    for b in range(3):
                        P.dma("sp", ypl[b][:, :].rearrange("p (k s) -> p k s", k=KC),
                              yp_d[b, :, :, ts].rearrange("k p s -> p k s"), reads=[ypdb[b][t]], writes=[yplb[b]])
                    for k in range(KC):
                        ks = slice(k * TT, (k + 1) * TT)
                        tt_, ttb_ = tsum[k % 2], tsumb[k % 2]
                        P.op("dve", lambda e, ks=ks, tt_=tt_: e.tensor_tensor(out=tt_[:, :], in0=ypl[0][:, ks], in1=ypl[1][:, ks],
                                                                              op=ALU.add),
                             reads=[yplb[0], yplb[1]], writes=[ttb_])
                        P.op("dve", lambda e, ks=ks, tt_=tt_, k=k: e.tensor_tensor(out=Y3[:, k, :], in0=tt_[:, :], in1=ypl[2][:, ks],
                                                                                   op=ALU.add),
                             reads=[ttb_, yplb[2]], writes=[Yb[k]])
                    for c in range(KC):
                        po = 4 + (c % 2)
                        for k in range(KC):
                            mm(po, wo_sb3[:, k, c * 128:(c + 1) * 128], Y3[:, k, :], k == 0, k == KC - 1, [wob, Yb[k]])
                        P.op("dve", lambda e, c=c, po=po: e.tensor_tensor(out=xT3[:, c, ts], in0=ps[po][:, :], in1=xT3[:, c, ts],
                                                                          op=ALU.add),
                             reads=[psb[po], xTb[c][t]], writes=[xTb[c][t]])
                P.barrier()

        if stage >= 6:
            ffn("ffn2", PV_GFFN2, do_store=True)

        if stage < 6:
            with ExitStack() as ph:
                set_psum(ph, 8, 0, 0)
                if stage < 1:
                    load_x(ph)
                xo = [sb(f"xo{i}", [128, D], F32, ph) for i in range(2)]
                xob = [Buf(f"xo{i}") for i in range(2)]
                for t in range(NT):
                    store_tile(t, xo, xob)
        P.finish(out_toks)
        build_program.stats = (P.n_inst, P.n_wait)
    return nc


def _consts():
    cf32 = np.zeros((128, CF_N), dtype=np.float32)
    cf32[:, CF_ID:CF_ID + 128] = np.eye(128)
    cret = np.zeros((128, CR_N), dtype=np.float32)
    idx = np.arange(128, dtype=np.float64)
    for h in range(4):
        g = GAMMA[h]
        dist = idx[None, :] - idx[:, None]
        dec = np.where(dist >= 0, g ** np.maximum(dist, 0.0), 0.0) / 16.0
        cret[:, CR_DEC + h * 128:CR_DEC + (h + 1) * 128] = dec
        qd = g ** (idx + 1.0)
        cret[:, CR_QDEC + h * 128:CR_QDEC + (h + 1) * 128] = qd[None, :]
    cbf = np.zeros((128, CB_N), dtype=np.float32)
    cbf[:, CB_ID:CB_ID + 128] = np.eye(128)
    cbf[:, CB_O1024:CB_O1024 + 128] = 1.0 / 1024.0
    cbf[0:64, CB_BD64:CB_BD64 + 64] = 1.0 / 64.0
    cbf[64:128, CB_BD64 + 64:CB_BD64 + 128] = 1.0 / 64.0
    cbf[:, CB_O128:CB_O128 + 128] = 1.0 / 128.0
    cbf[:, CB_O256:CB_O256 + 128] = 1.0 / 256.0
    cbf[:, CB_ONE:CB_ONE + 128] = 1.0
    cbf[:, CB_TRI:CB_TRI + 128] = (idx[None, :] >= idx[:, None]).astype(np.float32)
    return cf32, cret, cbf.astype(ml_dtypes.bfloat16)


def _pvec(inp):
    pv = np.zeros((128, 64), dtype=np.float32)

    def fm(v):
        return np.ascontiguousarray(np.asarray(v, dtype=np.float32).reshape(-1, 128).T)

    pv[:, PV_GFFN1:PV_GFFN1 + 8] = fm(inp["g_ffn1"][0])
    pv[:, PV_GMIX:PV_GMIX + 8] = fm(inp["g_mix"][0])
    pv[:, PV_GFFN2:PV_GFFN2 + 8] = fm(inp["g_ffn2"][0])
    pv[:, PV_GMEM:PV_GMEM + 8] = fm(inp["g_mem"][0])
    pv[:, PV_GDQ] = np.tile(np.asarray(inp["g_diff_q"][0], dtype=np.float32), 2)
    pv[:, PV_GDK] = np.tile(np.asarray(inp["g_diff_k"][0], dtype=np.float32), 2)
    pv[:, PV_GDO] = np.asarray(inp["g_diff_out"][0], dtype=np.float32)
    pv[:, PV_GMQ:PV_GMQ + 2] = fm(inp["g_mem_q"][0])
    pv[:, PV_GMK:PV_GMK + 2] = fm(inp["g_mem_k"][0])
    idx = np.arange(128, dtype=np.float64)
    pv[:, PV_INV] = (10000.0 ** (-idx / 128.0)).astype(np.float32)
    for h in range(4):
        pv[:, PV_KDEC + h] = (GAMMA[h] ** (127.0 - idx) / 16.0).astype(np.float32)
    return pv


_SHARED_KEYS = ("w_ffn1_in", "w_ffn1_out", "w_ffn2_in", "w_ffn2_out", "w_in", "w_mem_kv",
                "w_br_ret", "w_br_diff", "w_br_mem", "w_o")


def make_in_maps(inp, ncores=NCORES):
    cf32, cret, cbf = _consts()
    shared = {k: np.ascontiguousarray(inp[k][0]) for k in _SHARED_KEYS}
    shared["cret"] = cret
    shared["pvec"] = _pvec(inp)
    shared["cf32"] = cf32
    shared["cbf"] = cbf
    shared["lamv"] = np.ascontiguousarray(np.concatenate(
        [inp["lam_q1"][0], inp["lam_k1"][0], inp["lam_q2"][0], inp["lam_k2"][0]]).astype(np.float32)[None, :])
    maps = []
    for c in range(ncores):
        m = dict(shared)
        m["x"] = np.ascontiguousarray(inp["x"][c])
        m["mem"] = np.ascontiguousarray(inp["mem"][c])
        m["positions"] = np.ascontiguousarray(inp["positions"][c][None, :].astype(np.int32))
        maps.append(m)
    return maps


_NC_CACHE = {}


def kernel(**inputs):
    inp = {k: np.asarray(v) for k, v in inputs.items()}
    if "nc" not in _NC_CACHE:
        _NC_CACHE["nc"] = build_program()
    nc = _NC_CACHE["nc"]
    in_maps = make_in_maps(inp)
    res = run_bass_kernel_spmd(nc, in_maps, core_ids=list(range(NCORES)))
    out = np.stack([np.asarray(r["out"], dtype=np.float32) for r in res.results], axis=0)
    return out
```
